# Optimizing a Trainium2 kernel written in Bass

```python
import jax, jax.numpy as jnp
from jax import lax
import numpy as np

D_MODEL = 1024
BATCH = 32
SEQ = 2048
DEPTH = 1

N_HEADS = 16
N_KV_GROUPS = 2
HEADS_PER_GROUP = N_HEADS // N_KV_GROUPS
HEAD_DIM = 64
NSA_WIDTH = N_HEADS * HEAD_DIM
KV_WIDTH = N_KV_GROUPS * HEAD_DIM
CMP_BLOCK = 32
CMP_STRIDE = 16
CMP_HIDDEN = 256
SEL_BLOCK = 64
N_SELECT = 8
WINDOW = 512
Q_BLOCK = 128
CONV_WIDTH = D_MODEL
CONV_KERNEL = 31
REL_BUCKETS = 32
REL_MAX_DIST = 128
EPS = 1e-6
MASK_VALUE = -1e30
FORCE_SCORE = 1e6
IN_WIDTH = NSA_WIDTH + 6 * KV_WIDTH + 3 * N_HEADS + NSA_WIDTH + 3 * CONV_WIDTH + 2 * D_MODEL

kernel_name = "hybrid_conformer_conv_nsa_gated_block"


def rmsnorm(x, g):
    xf = x.astype(jnp.float32)
    y = xf * lax.rsqrt(jnp.mean(xf * xf, axis=-1, keepdims=True) + EPS)
    return (y * g.astype(jnp.float32)).astype(x.dtype)


def layernorm(x, g, b):
    xf = x.astype(jnp.float32)
    mu = jnp.mean(xf, axis=-1, keepdims=True)
    var = jnp.mean(jnp.square(xf - mu), axis=-1, keepdims=True)
    y = (xf - mu) * lax.rsqrt(var + EPS)
    return (y * g.astype(jnp.float32) + b.astype(jnp.float32)).astype(x.dtype)


def masked_softmax(logits, mask):
    lf = jnp.where(mask, logits.astype(jnp.float32), MASK_VALUE)
    m = jnp.max(lf, axis=-1, keepdims=True)
    e = jnp.where(mask, jnp.exp(lf - m), 0.0)
    return e / jnp.maximum(jnp.sum(e, axis=-1, keepdims=True), 1e-30)


def t5_bucket(rel):
    rel = jnp.maximum(rel, 0)
    max_exact = REL_BUCKETS // 2
    relf = jnp.maximum(rel, 1).astype(jnp.float32)
    large = max_exact + (jnp.log(relf / max_exact) / np.float32(np.log(REL_MAX_DIST / max_exact))
                         * (REL_BUCKETS - max_exact)).astype(jnp.int32)
    large = jnp.minimum(large, REL_BUCKETS - 1)
    return jnp.where(rel < max_exact, rel, large)


def compress(k, pos, w1, w2):
    B, S, G, dk = k.shape
    n_cmp = (S - CMP_BLOCK) // CMP_STRIDE + 1
    idx = jnp.arange(n_cmp)[:, None] * CMP_STRIDE + jnp.arange(CMP_BLOCK)[None, :]
    blocks = k[:, idx] + pos[None, None, :, None, :]
    blocks = blocks.transpose(0, 1, 3, 2, 4).reshape(B, n_cmp, G, CMP_BLOCK * dk)
    return jax.nn.silu(blocks @ w1) @ w2


def nsa_attention(q, k_cmp, v_cmp, k_slc, v_slc, k_win, v_win, branch_gates, rel_bias):
    B, S = q.shape[:2]
    G, hg, dk = N_KV_GROUPS, HEADS_PER_GROUP, HEAD_DIM
    q = q.reshape(B, S, G, hg, dk) * (dk ** -0.5)
    n_cmp = k_cmp.shape[1]
    n_sel = S // SEL_BLOCK
    n_top = min(N_SELECT, n_sel)
    kw_len = Q_BLOCK + WINDOW

    cmp_start = jnp.arange(n_cmp) * CMP_STRIDE
    cmp_end = cmp_start + CMP_BLOCK - 1
    sel_start = jnp.arange(n_sel) * SEL_BLOCK
    overlap = ((cmp_start[:, None] <= sel_start[None, :] + SEL_BLOCK - 1)
               & (cmp_end[:, None] >= sel_start[None, :])).astype(jnp.float32)

    ks_blocks = k_slc.reshape(B, n_sel, SEL_BLOCK, G, dk).transpose(0, 3, 1, 2, 4)
    vs_blocks = v_slc.reshape(B, n_sel, SEL_BLOCK, G, dk).transpose(0, 3, 1, 2, 4)
    kw_pad = jnp.pad(k_win, ((0, 0), (WINDOW, 0), (0, 0), (0, 0)))
    vw_pad = jnp.pad(v_win, ((0, 0), (WINDOW, 0), (0, 0), (0, 0)))
    table_g = rel_bias.reshape(REL_BUCKETS, G, hg).transpose(1, 0, 2)
    gates = jax.nn.sigmoid(branch_gates.astype(jnp.float32)).reshape(B, S, 3, G, hg)
    b_idx = jnp.arange(B)[:, None, None, None]
    g_idx = jnp.arange(G)[None, None, :, None]
    sel_j = jnp.arange(n_sel)

    def block(qi):
        q0 = qi * Q_BLOCK
        t = q0 + jnp.arange(Q_BLOCK)
        qb = lax.dynamic_slice_in_dim(q, q0, Q_BLOCK, axis=1)
        gb = lax.dynamic_slice_in_dim(gates, q0, Q_BLOCK, axis=1)

        rel_c = t[:, None] - cmp_end[None, :]
        bias_c = rel_bias[t5_bucket(rel_c)].reshape(Q_BLOCK, n_cmp, G, hg).transpose(0, 2, 3, 1)
        logit_c = jnp.einsum('bqghd,bcgd->bqghc', qb, k_cmp) + bias_c
        p_c = masked_softmax(logit_c, (rel_c >= 0)[:, None, None, :])
        o_c = jnp.einsum('bqghc,bcgd->bqghd', p_c, v_cmp)

        imp = jnp.einsum('bqghc,cj->bqgj', p_c, overlap)
        blk_t = t // SEL_BLOCK
        valid = sel_j[None, :] <= blk_t[:, None]
        forced = ((sel_j[None, :] == 0) | (sel_j[None, :] == blk_t[:, None])
                  | (sel_j[None, :] == blk_t[:, None] - 1))
        prio = jnp.where(forced[None, :, None, :], FORCE_SCORE, imp)
        prio = jnp.where(valid[None, :, None, :], prio, -FORCE_SCORE)
        top_val, top_idx = lax.top_k(prio, n_top)
        blk_ok = jnp.repeat(top_val > -0.5 * FORCE_SCORE, SEL_BLOCK, axis=-1)
        k_sel = ks_blocks[b_idx, g_idx, top_idx].reshape(B, Q_BLOCK, G, n_top * SEL_BLOCK, dk)
        v_sel = vs_blocks[b_idx, g_idx, top_idx].reshape(B, Q_BLOCK, G, n_top * SEL_BLOCK, dk)
        key_pos = (top_idx[..., None] * SEL_BLOCK + jnp.arange(SEL_BLOCK)).reshape(
            B, Q_BLOCK, G, n_top * SEL_BLOCK)
        rel_s = t[None, :, None, None] - key_pos
        bias_s = table_g[g_idx, t5_bucket(rel_s)].transpose(0, 1, 2, 4, 3)
        logit_s = jnp.einsum('bqghd,bqgkd->bqghk', qb, k_sel) + bias_s
        mask_s = (blk_ok & (rel_s >= 0))[:, :, :, None, :]
        p_s = masked_softmax(logit_s, mask_s)
        o_s = jnp.einsum('bqghk,bqgkd->bqghd', p_s, v_sel)

        kw = lax.dynamic_slice_in_dim(kw_pad, q0, kw_len, axis=1)
        vw = lax.dynamic_slice_in_dim(vw_pad, q0, kw_len, axis=1)
        key_pos_w = q0 - WINDOW + jnp.arange(kw_len)
        rel_w = t[:, None] - key_pos_w[None, :]
        mask_w = (rel_w >= 0) & (rel_w < WINDOW) & (key_pos_w[None, :] >= 0)
        bias_w = rel_bias[t5_bucket(rel_w)].reshape(Q_BLOCK, kw_len, G, hg).transpose(0, 2, 3, 1)
        logit_w = jnp.einsum('bqghd,bkgd->bqghk', qb, kw) + bias_w
        p_w = masked_softmax(logit_w, mask_w[:, None, None, :])
        o_w = jnp.einsum('bqghk,bkgd->bqghd', p_w, vw)

        return (gb[:, :, 0, :, :, None] * o_c + gb[:, :, 1, :, :, None] * o_s
                + gb[:, :, 2, :, :, None] * o_w)

    out = lax.map(block, jnp.arange(S // Q_BLOCK))
    return out.transpose(1, 0, 2, 3, 4, 5).reshape(B, S, NSA_WIDTH)


def causal_depthwise_conv(u, w, b):
    C = u.shape[-1]
    y = lax.conv_general_dilated(
        u, w[:, None, :].astype(u.dtype), window_strides=(1,), padding=[(CONV_KERNEL - 1, 0)],
        dimension_numbers=('NWC', 'WIO', 'NWC'), feature_group_count=C)
    return y + b


def setup_inputs(seed: int = 0) -> dict:
    key = jax.random.key(seed)
    ks = jax.random.split(key, 20)
    f32 = jnp.float32
    nrm = lambda k, shape, scale: (jax.random.normal(k, shape, f32) * scale).astype(f32)
    L = DEPTH
    return {
        "x": nrm(ks[0], (BATCH, SEQ, D_MODEL), 1.0),
        "norm_in_g": 1.0 + nrm(ks[1], (L, D_MODEL), 0.01),
        "w_in": nrm(ks[2], (L, D_MODEL, IN_WIDTH), D_MODEL ** -0.5),
        "pos_ck": nrm(ks[3], (L, CMP_BLOCK, HEAD_DIM), 0.1),
        "w_ck1": nrm(ks[4], (L, CMP_BLOCK * HEAD_DIM, CMP_HIDDEN), (CMP_BLOCK * HEAD_DIM) ** -0.5),
        "w_ck2": nrm(ks[5], (L, CMP_HIDDEN, HEAD_DIM), CMP_HIDDEN ** -0.5),
        "pos_cv": nrm(ks[6], (L, CMP_BLOCK, HEAD_DIM), 0.1),
        "w_cv1": nrm(ks[7], (L, CMP_BLOCK * HEAD_DIM, CMP_HIDDEN), (CMP_BLOCK * HEAD_DIM) ** -0.5),
        "w_cv2": nrm(ks[8], (L, CMP_HIDDEN, HEAD_DIM), CMP_HIDDEN ** -0.5),
        "rel_bias": nrm(ks[9], (REL_BUCKETS, N_HEADS), 0.5),
        "conv_w": nrm(ks[10], (L, CONV_KERNEL, CONV_WIDTH), CONV_KERNEL ** -0.5),
        "conv_b": nrm(ks[11], (L, CONV_WIDTH), 0.01),
        "conv_ln_g": 1.0 + nrm(ks[12], (L, CONV_WIDTH), 0.01),
        "conv_ln_b": nrm(ks[13], (L, CONV_WIDTH), 0.01),
        "w_conv_proj": nrm(ks[14], (L, CONV_WIDTH, D_MODEL), CONV_WIDTH ** -0.5),
        "w_nsa_proj": nrm(ks[15], (L, NSA_WIDTH, D_MODEL), NSA_WIDTH ** -0.5),
        "w_out": nrm(ks[16], (L, D_MODEL, D_MODEL), D_MODEL ** -0.5),
        "norm_f_g": 1.0 + nrm(ks[17], (D_MODEL,), 0.01),
    }


def reference(x, norm_in_g, w_in, pos_ck, w_ck1, w_ck2, pos_cv, w_cv1, w_cv2, rel_bias,
              conv_w, conv_b, conv_ln_g, conv_ln_b, w_conv_proj, w_nsa_proj, w_out, norm_f_g):
    B, S, _ = x.shape
    sizes = [NSA_WIDTH] + [KV_WIDTH] * 6 + [3 * N_HEADS, NSA_WIDTH, 2 * CONV_WIDTH,
                                            CONV_WIDTH, 2 * D_MODEL]
    offsets = np.cumsum(sizes)[:-1].tolist()
    kv_shape = (B, S, N_KV_GROUPS, HEAD_DIM)
    for l in range(DEPTH):
        h = rmsnorm(x, norm_in_g[l])
        proj = h @ w_in[l]
        (q, kc_raw, vc_raw, k_slc, v_slc, k_win, v_win, nsa_gates, z_nsa,
         glu_in, z_conv, merge_g) = jnp.split(proj, offsets, axis=-1)

        k_cmp = compress(kc_raw.reshape(kv_shape), pos_ck[l], w_ck1[l], w_ck2[l])
        v_cmp = compress(vc_raw.reshape(kv_shape), pos_cv[l], w_cv1[l], w_cv2[l])
        o_nsa = nsa_attention(q, k_cmp, v_cmp, k_slc.reshape(kv_shape), v_slc.reshape(kv_shape),
                              k_win.reshape(kv_shape), v_win.reshape(kv_shape), nsa_gates, rel_bias)
        y_nsa = (o_nsa.astype(x.dtype) * jax.nn.silu(z_nsa)) @ w_nsa_proj[l]

        a, b = jnp.split(glu_in, 2, axis=-1)
        u = a * jax.nn.sigmoid(b)
        c = causal_depthwise_conv(u, conv_w[l], conv_b[l])
        c = jax.nn.silu(layernorm(c, conv_ln_g[l], conv_ln_b[l]))
        y_conv = (c * jax.nn.silu(z_conv)) @ w_conv_proj[l]

        g_conv, g_nsa = jnp.split(jax.nn.sigmoid(merge_g), 2, axis=-1)
        x = x + (g_conv * y_conv + g_nsa * y_nsa) @ w_out[l]
    return rmsnorm(x, norm_f_g)
```

```python
import numpy as np
from contextlib import ExitStack
import ml_dtypes
import concourse.bass as bass
import concourse.mybir as mybir
from concourse.bass_utils import run_bass_kernel_spmd

F32 = mybir.dt.float32
BF16 = mybir.dt.bfloat16
F32R = mybir.dt.float32r
AF = mybir.ActivationFunctionType
ALU = mybir.AluOpType

NCORES = 8
NSEQ = 4
S = 2048
D = 1024
CH = 512
NCH = S // CH
EPS = 1e-6
NEG = -30000.0
NPE = 23


class Prog:
    ENGS = ('pe', 'act', 'dve', 'pool', 'sp')
    XLAT = 450.0
    SLAT = 120.0

    def __init__(self, nc, es):
        self.nc = nc
        self.es = es
        self.nodes = []
        self.regions = [0]
        self.sems = {}
        self.res = {}
        self.last_stream = {}
        self.alias = {}
        for e in self.ENGS:
            self._sem('E_' + e)

    def _sem(self, name):
        if name not in self.sems:
            self.sems[name] = self.es.enter_context(self.nc.semaphore(name))
        return self.sems[name]

    def sb(self, name, shape, dt, es=None):
        return (es or self.es).enter_context(self.nc.sbuf_tensor(name, list(shape), dt))

    def ps(self, name, shape, dt):
        return self.es.enter_context(self.nc.psum_tensor(name, list(shape), dt))

    def op(self, eng, fn, reads=(), writes=(), xr=(), xw=(), dma=None, cost=150.0, nbytes=0):
        nid = len(self.nodes)
        al = self.alias
        reads = [x for k in reads for x in al.get(k, (k,))]
        writes = [x for k in writes for x in al.get(k, (k,))]
        me = eng if dma is None else 'dma:' + dma
        if dma is not None:
            self._sem(dma)
        wait, order = set(), set()

        def R(k):
            return self.res.setdefault(k, {'w': {}, 'r': []})

        for k in reads:
            for a, p in R(k)['w'].items():
                wait.add(p)
        for k in xr:
            r = R(k)
            for a, p in r['w'].items():
                wait.add(p)
            for a, p in r['r']:
                if a != me:
                    wait.add(p)
        for k in list(writes) + list(xw):
            r = R(k)
            for a, p in list(r['w'].items()) + r['r']:
                (order if a == me else wait).add(p)
        if dma is not None and dma in self.last_stream:
            order.add(self.last_stream[dma])
        if dma is not None:
            self.last_stream[dma] = nid
        wait.discard(nid)
        order.discard(nid)
        order -= wait
        for k in list(reads) + list(xr):
            self.res[k]['r'].append((me, nid))
        for k in list(writes) + list(xw):
            self.res[k]['w'] = {me: nid}
            self.res[k]['r'] = []
        self.nodes.append(dict(eng=eng, fn=fn, dma=dma, wait=wait, order=order, cost=float(cost), nbytes=nbytes))
        return nid

    def fence(self):
        self.regions.append(len(self.nodes))

    def _schedule_region(self, lo, hi):
        import heapq
        nodes = self.nodes
        succ = {}
        indeg = {}
        for i in range(lo, hi):
            n = nodes[i]
            cnt = 0
            for p in n['wait'] | n['order']:
                if p >= lo:
                    succ.setdefault(p, []).append(i)
                    cnt += 1
            indeg[i] = cnt
        ready = {e: [] for e in self.ENGS}
        est = {}
        for i in range(lo, hi):
            if indeg[i] == 0:
                est[i] = 0.0
                heapq.heappush(ready[nodes[i]['eng']], i)
        free = {e: 0.0 for e in self.ENGS}
        start, finish = {}, {}
        dma_free = [0.0]
        order = {e: [] for e in self.ENGS}
        K = 24
        remaining = hi - lo
        while remaining:
            best = None
            for e in self.ENGS:
                h = ready[e]
                if not h:
                    continue
                cands = heapq.nsmallest(K, h)
                for i in cands:
                    st = max(free[e], est[i])
                    key = (st, i)
                    if best is None or key < best[0]:
                        best = (key, e, i)
            (st, i), e, _ = best
            ready[e].remove(i)
            heapq.heapify(ready[e])
            n = nodes[i]
            start[i] = st
            if n['dma'] is None:
                finish[i] = st + n['cost']
                free[e] = finish[i]
            else:
                free[e] = st + 60.0
                d0 = max(st + 1800.0, dma_free[0])
                finish[i] = d0 + n['nbytes'] / 150.0
                dma_free[0] = finish[i]
            order[e].append(i)
            remaining -= 1
            for sidx in succ.get(i, ()):
                indeg[sidx] -= 1
                if indeg[sidx] == 0:
                    sn = nodes[sidx]
                    t = 0.0
                    for p in sn['wait']:
                        if p >= lo:
                            t = max(t, finish[p] + (self.XLAT if nodes[p]['eng'] != sn['eng'] or nodes[p]['dma'] else self.SLAT))
                    for p in sn['order']:
                        if p >= lo:
                            t = max(t, start[p])
                    est[sidx] = t
                    heapq.heappush(ready[sn['eng']], sidx)
        self.sim_end = max(finish.values()) if finish else 0.0
        return order

    def emit(self, final_streams=()):
        nodes = self.nodes
        bounds = self.regions + [len(nodes)]
        eng_order = {e: [] for e in self.ENGS}
        pos = {}
        cnt = {s: 0 for s in self.sems}
        for r in range(len(bounds) - 1):
            lo, hi = bounds[r], bounds[r + 1]
            order = self._schedule_region(lo, hi)
            for e in self.ENGS:
                for i in order[e]:
                    n = nodes[i]
                    if n['dma'] is None:
                        cnt['E_' + e] += 1
                        pos[i] = ('E_' + e, cnt['E_' + e])
                    else:
                        cnt[n['dma']] += 16
                        pos[i] = (n['dma'], cnt[n['dma']])
                    eng_order[e].append(('op', i))
            if r < len(bounds) - 2:
                snap = dict(cnt)
                for e in self.ENGS:
                    eng_order[e].append(('fence', snap))
        block = self.es.enter_context(self.nc.Block())
        P = self
        self.n_waits = 0

        def run(name, eng):
            seen = {}
            for kind, x in eng_order[name]:
                if kind == 'fence':
                    for s, v in x.items():
                        if v > 0 and s != 'E_' + name and seen.get(s, 0) < v:
                            seen[s] = v
                            eng.wait_ge(P.sems[s], v)
                            P.n_waits += 1
                    continue
                n = nodes[x]
                need = {}
                for p in n['wait']:
                    s, v = pos[p]
                    if need.get(s, 0) < v:
                        need[s] = v
                for s, v in need.items():
                    if seen.get(s, 0) < v:
                        seen[s] = v
                        eng.wait_ge(P.sems[s], v)
                        P.n_waits += 1
                s, v = pos[x]
                n['fn'](eng).then_inc(P.sems[s], 16 if n['dma'] is not None else 1)
            if name == 'sp':
                for s in final_streams:
                    eng.wait_ge(P.sems[s], cnt[s])

        @block.tensor
        def _(e):
            run('pe', e)

        @block.scalar
        def _(e):
            run('act', e)

        @block.vector
        def _(e):
            run('dve', e)

        @block.gpsimd
        def _(e):
            run('pool', e)

        @block.sync
        def _(e):
            run('sp', e)

    @staticmethod
    def _n(ap):
        n = 1
        for d in ap.shape[1:]:
            n *= d
        return n

    def mm(self, out, lhsT, rhs, start, stop, reads, xw, skip=False):
        n = self._n(out)
        c = max(78.0, n / 1.95 + 12.0)
        if rhs.dtype == F32:
            c *= 4
        elif rhs.dtype == F32R:
            c *= 2
        self.op('pe', lambda e: e.matmul(out, lhsT, rhs, start=start, stop=stop,
                                         skip_group_check=skip), reads=reads, xw=xw, cost=c)

    def tr(self, out, in_, ident, reads, xw):
        self.op('pe', lambda e: e.transpose(out, in_, ident), reads=reads, xw=xw, cost=110.0)

    def act(self, out, in_, func, bias=None, scale=None, accum=None, **kw):
        def f(e):
            a = dict(out=out, in_=in_, func=func)
            if bias is not None:
                a['bias'] = bias
            if scale is not None:
                a['scale'] = scale
            if accum is not None:
                a['accum_out'] = accum
            return e.activation(**a)
        c = 120.0 + self._n(out) / 1.2
        self.op('act', f, cost=c, **kw)

    def _vc(self, eng, out, rate=0.96):
        if eng == 'pool':
            return 120.0 + self._n(out) / 0.45
        return 75.0 + self._n(out) / rate

    def tt(self, eng, out, in0, in1, op, rate=0.96, **kw):
        self.op(eng, lambda e: e.tensor_tensor(out, in0, in1, op), cost=self._vc(eng, out, rate), **kw)

    def ts(self, eng, out, in0, s1, s2, op0, op1=None, rate=0.96, **kw):
        c = self._vc(eng, out, rate)
        if op1 is None:
            self.op(eng, lambda e: e.tensor_scalar(out, in0, s1, s2, op0=op0), cost=c, **kw)
        else:
            self.op(eng, lambda e: e.tensor_scalar(out, in0, s1, s2, op0=op0, op1=op1), cost=c, **kw)

    def stt(self, eng, out, in0, scalar, in1, op0, op1, rate=0.96, **kw):
        self.op(eng, lambda e: e.scalar_tensor_tensor(out, in0, scalar, in1, op0=op0, op1=op1),
                cost=self._vc(eng, out, rate), **kw)

    def cp(self, eng, out, in_, rate=0.96, **kw):
        if eng == 'act':
            self.act(out, in_, AF.Copy, **kw)
        else:
            self.op(eng, lambda e: e.tensor_copy(out, in_), cost=self._vc(eng, out, rate), **kw)

    def dma(self, out, in_, sem, eng='sp', **kw):
        nb = 1
        for d in out.shape:
            nb *= d
        nb *= 2 if out.dtype == BF16 else 4
        self.op(eng, lambda e: e.dma_start(out=out, in_=in_), dma=sem, nbytes=nb, **kw)


def _t5_bucket(rel):
    rel = np.maximum(rel, 0)
    relf = np.maximum(rel, 1).astype(np.float32)
    large = 16 + (np.log(relf / np.float32(16)) / np.float32(np.log(128 / 16)) * np.float32(16)).astype(np.int32)
    large = np.minimum(large, 31)
    return np.where(rel < 16, rel, large)


def _host_consts():
    c = {}
    bf = ml_dtypes.bfloat16
    c['c_ident'] = np.eye(128, dtype=np.float32).astype(bf)
    c['c_identf'] = np.eye(128, dtype=np.float32)
    c['c_onesf'] = np.ones((128, 128), np.float32)
    relv = np.arange(1136) - 527
    bkv = _t5_bucket(relv)
    ohv = np.zeros((32, 1136), np.float32)
    for b in range(32):
        ohv[b, :] = ((bkv == b) & (relv >= 0))
    c['c_ohv'] = ohv
    jr = np.zeros((128, 168), np.float32)
    for kk in range(128):
        jr[kk, 127 - kk] = 1.0
    for kk in range(40):
        jr[kk, 128 + 39 - kk] = 1.0
    c['c_jrev'] = jr.astype(bf)
    c['c_m4'] = (np.arange(128)[None, :] < np.arange(128)[:, None]).astype(np.float32).astype(bf)
    selA = np.zeros((128, 16, 32), np.float32)
    selB = np.zeros((128, 16, 32), np.float32)
    j = np.arange(32)[None, :]
    for qt in range(16):
        t = qt * 128 + np.arange(128)[:, None]
        blk = t // 64
        valid = j <= blk
        forced = (j == 0) | (j == blk) | (j == blk - 1)
        selA[:, qt, :] = (valid & ~forced)
        selB[:, qt, :] = np.where(valid, np.where(forced, 1e6, 0.0), -1e6)
    c['c_selA'] = selA.astype(bf)
    c['c_selB'] = selB.astype(bf)
    cs = np.arange(127) * 16
    ce = cs + 31
    ss = np.arange(32) * 64
    ov = ((cs[:, None] <= ss[None, :] + 63) & (ce[:, None] >= ss[None, :])).astype(np.float32)
    ovf = np.zeros((128, 33), np.float32)
    ovf[:127, :32] = ov
    ovf[:127, 32] = 1.0
    c['c_ovf'] = ovf.astype(bf)
    ovn = np.zeros((40, 4, 33), np.float32)
    for tc in range(4):
        for rr in range(40):
            cidx = 32 * tc - 8 + rr
            if 0 <= cidx < 127:
                ovn[rr, tc, :32] = ov[cidx]
                ovn[rr, tc, 32] = 1.0
    c['c_ovn'] = ovn.astype(bf)
    bi = np.zeros((32, S), np.float32)
    for jj in range(32):
        bi[jj, jj * 64:(jj + 1) * 64] = 1.0
    c['c_blkind'] = bi.astype(bf)
    return c


def _units():
    U = []
    KC, VC, KS, VS, KW, VW, GT, ZN, GA, GB, ZC, MC, MN = (1024, 1152, 1280, 1408, 1536, 1664, 1792, 1840,
                                                          2864, 3888, 4912, 5936, 6960)
    U += [('w_in', [(KC, 64), (KC, 64)]), ('w_in', [(KC + 64, 64), (KC + 64, 64)]),
          ('w_in', [(VC, 64), (VC, 64)]), ('w_in', [(VC + 64, 64), (VC + 64, 64)])]
    U += [('w_in', [(KS, 128)]), ('w_in', [(KW, 128)]), ('w_in', [(VS, 128)]), ('w_in', [(VW, 128)])]
    U += [('w_in', [(GT, 48)])]
    U += [('w_in', [(i * 128, 128)]) for i in range(8)]
    U += [None, None, None]
    U += [('w_in', [(ZN + i * 128, 128)]) for i in range(8)]
    U += [('w_nsa', [(i * 128, 128)]) for i in range(8)]
    for i in range(8):
        U += [('w_in', [(GB + i * 128, 128)]), ('w_in', [(GA + i * 128, 128)])]
    U += [('w_in', [(ZC + i * 128, 128)]) for i in range(8)]
    U += [('w_conv', [(i * 128, 128)]) for i in range(8)]
    U += [('w_in', [(MC + i * 128, 128)]) for i in range(8)]
    U += [('w_in', [(MN + i * 128, 128)]) for i in range(8)]
    U += [('w_out', [(i * 128, 128)]) for i in range(8)]
    assert len(U) == 92
    return U


NBLK = 23 + 4


def build_nc():
    nc = bass.Bass("TRN2", target_bir_lowering=False)
    dt = nc.dram_tensor
    x = dt("x", [NSEQ, S, D], F32, kind="ExternalInput").ap()
    y = dt("y", [NSEQ, S, D], F32, kind="ExternalOutput").ap()
    w_in = dt("w_in", [D, 7984], F32, kind="ExternalInput").ap()
    w_nsa = dt("w_nsa", [D, D], F32, kind="ExternalInput").ap()
    w_conv = dt("w_conv", [D, D], F32, kind="ExternalInput").ap()
    w_out = dt("w_out", [D, D], F32, kind="ExternalInput").ap()
    wsrc = {'w_in': w_in, 'w_nsa': w_nsa, 'w_conv': w_conv, 'w_out': w_out}
    w_ck1 = dt("w_ck1", [2048, 256], F32, kind="ExternalInput").ap()
    w_cv1 = dt("w_cv1", [2048, 256], F32, kind="ExternalInput").ap()
    w_ck2 = dt("w_ck2", [256, 64], F32, kind="ExternalInput").ap()
    w_cv2 = dt("w_cv2", [256, 64], F32, kind="ExternalInput").ap()
    pos_ck = dt("pos_ck", [16, 128], F32, kind="ExternalInput").ap()
    pos_cv = dt("pos_cv", [16, 128], F32, kind="ExternalInput").ap()
    rel_bias = dt("rel_bias", [1, 512], F32, kind="ExternalInput").ap()
    prm_in = dt("prm", [35, D], F32, kind="ExternalInput").ap()
    norm_f = dt("norm_f", [1, D], F32, kind="ExternalInput").ap()
    c_ident = dt("c_ident", [128, 128], BF16, kind="ExternalInput").ap()
    c_identf = dt("c_identf", [128, 128], F32, kind="ExternalInput").ap()
    c_onesf = dt("c_onesf", [128, 128], F32, kind="ExternalInput").ap()
    c_ohv = dt("c_ohv", [32, 1136], F32, kind="ExternalInput").ap()
    c_jrev = dt("c_jrev", [128, 168], BF16, kind="ExternalInput").ap()
    gvd = dt("gvd", [16, 1136], BF16, kind="Internal")
    rel_bias2 = dt("rel_bias2", [32, 16], F32, kind="ExternalInput").ap()
    c_m4 = dt("c_m4", [128, 128], BF16, kind="ExternalInput").ap()
    c_selA = dt("c_selA", [128, 16, 32], BF16, kind="ExternalInput").ap()
    c_selB = dt("c_selB", [128, 16, 32], BF16, kind="ExternalInput").ap()
    c_ovf = dt("c_ovf", [128, 33], BF16, kind="ExternalInput").ap()
    c_ovn = dt("c_ovn", [40, 4, 33], BF16, kind="ExternalInput").ap()
    c_blkind = dt("c_blkind", [32, S], BF16, kind="ExternalInput").ap()
    wsc = dt("wsc", [NBLK, 128, 4096], BF16, kind="Internal").ap()

    units = _units()

    with ExitStack() as es:
        P = Prog(nc, es)
        A = [P.ps(f"psA{i}", [128, 512], F32) for i in range(4)]
        O = [P.ps(f"psO{i}", [128, 512], F32) for i in range(2)]
        X = P.ps("psX", [128, 512], F32)
        T = P.ps("psT", [128, 1024], BF16)
        arot = [0]

        def nextA():
            i = arot[0] % 4
            arot[0] += 1
            return A[i], f"A{i}"

        srot = [0]

        def nextS():
            i = srot[0] % 3
            srot[0] += 1
            return A[i], f"A{i}"

        brot = [0]

        def nextBG():
            i = brot[0] % 2
            brot[0] += 1
            return (X, "X")

        ident = P.sb("ident", [128, 128], BF16)
        identf = P.sb("identf", [128, 128], F32)
        onesf = P.sb("onesf", [128, 128], F32)
        Ed = P.sb("Ed", [128, 16, 256], BF16)
        Er = P.sb("Er", [40, 16, 512], BF16)
        m4 = P.sb("m4", [128, 128], BF16)
        selA = P.sb("selA", [128, 16, 32], BF16)
        selB = P.sb("selB", [128, 16, 32], BF16)
        prm = P.sb("prm_sb", [128, 8, 35], F32)
        normf = P.sb("normf", [128, D], F32)
        b31 = P.sb("b31", [128, 16], F32)
        w2k = P.sb("w2k", [128, 2, 64], BF16)
        w2v = P.sb("w2v", [128, 2, 64], BF16)
        posb = P.sb("posb", [128, 4], F32)
        posbh = P.sb("posbh", [128, 4], F32)
        prmh = P.sb("prmh", [128, 8, 2], F32)
        kslc = P.sb("kslc", [128, 2, S], BF16)
        kwin = P.sb("kwin", [128, 2, S], BF16)
        vslc = P.sb("vslc", [128, 16, 2, 65], BF16)
        vwin = P.sb("vwin", [128, 16, 2, 65], BF16)
        kr2 = [P.sb(f"kr2_{i}", [128, 528], BF16) for i in range(4)]
        hidv = [P.sb(f"hidv{g}", [128, 2, 136], BF16) for g in range(2)]
        hidk = P.sb("hidk", [128, 2, 32], BF16)
        kcmpT = P.sb("kcmpT", [128, 2, 136], BF16)
        vcf = P.sb("vcf", [128, 2, 97], BF16)
        vcn = [P.sb(f"vcn{t}", [40, 2, 97], BF16) for t in range(4)]
        small = P.sb("small", [128, 96], F32)

        with ExitStack() as ses:
            for (t, src, k) in [(ident, c_ident, 'ident'), (identf, c_identf, 'identf'), (onesf, c_onesf, 'onesf'),
                                (m4, c_m4, 'm4'), (selA, c_selA, 'selA'), (selB, c_selB, 'selB')]:
                P.dma(t[:], src, 'c_' + k, writes=[k])
            P.dma(normf[:], norm_f.partition_broadcast(128), 'cst', writes=['normf'])
            P.op('pool', lambda e: e.memset(kslc[:], 0.0), writes=['kslc'], cost=4000)
            P.op('pool', lambda e: e.memset(kwin[:], 0.0), writes=['kwin'], cost=4000)
            for g in range(2):
                P.dma(kslc[64:96, g, :], c_blkind, 'cst', writes=['kslc'])
                P.dma(vcf[:, g, 64:97], c_ovf, 'cst', writes=['vcf_o'])
                for t in range(4):
                    P.dma(vcn[t][:, g, 64:97], c_ovn[:, t, :], 'cst', writes=[f'vcn{t}_o'])
            praw = P.sb("praw", [35, D], F32, ses)
            P.dma(praw[:], prm_in, 'c1', writes=['praw'])
            for c in range(8):
                P.tr(X[:, c * 35:(c + 1) * 35], praw[:, c * 128:(c + 1) * 128], identf[0:35, 0:35],
                     reads=['praw', 'identf'], xw=['X'])
            P.cp('dve', prm[:].rearrange("p a b -> p (a b)"), X[:, 0:280], xr=['X'], writes=['prm'])
            posraw = P.sb("posraw", [32, 128], F32, ses)
            P.dma(posraw[0:16, :], pos_ck, 'c2', writes=['posraw'])
            P.dma(posraw[16:32, :], pos_cv, 'c2', writes=['posraw'])
            P.tr(X[:, 0:32], posraw[:], identf[0:32, 0:32], reads=['posraw', 'identf'], xw=['X'])
            post = P.sb("post", [128, 32], BF16, ses)
            P.cp('dve', post[:], X[:, 0:32], xr=['X'], writes=['post'])
            w2raw = P.sb("w2raw", [128, 2, 2, 64], F32, ses)
            P.dma(w2raw[:, 0, :, :], w_ck2.rearrange("(c p) n -> p c n", p=128), 'c3', writes=['w2raw'])
            P.dma(w2raw[:, 1, :, :], w_cv2.rearrange("(c p) n -> p c n", p=128), 'c3', writes=['w2raw'])
            P.cp('dve', w2k[:], w2raw[:, 0, :, :], reads=['w2raw'], writes=['w2k'])
            P.cp('dve', w2v[:], w2raw[:, 1, :, :], reads=['w2raw'], writes=['w2v'])
            rb = P.sb("rb", [128, 32, 16], F32, ses)
            P.dma(rb[:].rearrange("p a b -> p (a b)"), rel_bias.partition_broadcast(128), 'c4', writes=['rb'])
            P.cp('dve', b31[:], rb[:, 31, :], reads=['rb'], writes=['b31'])
            rbT = P.sb("rbT", [32, 16], F32, ses)
            ohv = P.sb("ohv", [32, 1136], F32, ses)
            jrev = P.sb("jrev", [128, 168], BF16, ses)
            gv = P.sb("gv", [16, 1136], BF16, ses)
            Hd = P.sb("Hd", [128, 16, 256], BF16, ses)
            Hc = P.sb("Hc", [40, 16, 512], BF16, ses)
            P.dma(rbT[:], rel_bias2, 'c6', writes=['rbT'])
            P.dma(ohv[:], c_ohv, 'c7', writes=['ohv'])
            P.dma(jrev[:], c_jrev, 'c8', writes=['jrev'])
            P.act(rbT[:], rbT[:], AF.Exp, reads=['rbT'], writes=['rbT'])
            for ci, (c0_, c1_) in enumerate([(0, 512), (512, 1024), (1024, 1136)]):
                P.mm(A[ci][0:16, 0:c1_ - c0_], rbT[:, :], ohv[:, c0_:c1_], True, True, reads=['rbT', 'ohv'], xw=[f'A{ci}'])
            P.op('dve', lambda e: e.reciprocal(small[0:16, 0:1], A[1][0:16, 215:216]), xr=['A1'], writes=['small'])
            for ci, (c0_, c1_) in enumerate([(0, 512), (512, 1024), (1024, 1136)]):
                P.ts('dve', gv[:, c0_:c1_], A[ci][0:16, 0:c1_ - c0_], small[0:16, 0:1], None, ALU.mult,
                     xr=[f'A{ci}'], reads=['small'], writes=['gv'])
            P.dma(gvd.ap(), gv[:], 'c9', reads=['gv'], writes=['gvd'])
            P.dma(Hd[:], bass.AP(tensor=gvd, offset=400, ap=[[1, 128], [1136, 16], [1, 256]]), 'c10',
                  reads=['gvd'], writes=['Hd'])
            P.dma(Hc[:], bass.AP(tensor=gvd, offset=0, ap=[[16, 40], [1136, 16], [1, 512]]), 'c11',
                  reads=['gvd'], writes=['Hc'])
            for h2 in range(8):
                ps, pk = nextA()
                P.mm(ps[:, :], jrev[:, 0:128], Hd[:, 2 * h2:2 * h2 + 2, :].rearrange("p a b -> p (a b)"), True, True,
                     reads=['jrev', 'Hd'], xw=[pk])
                P.cp('dve' if h2 % 2 else 'act', Ed[:, 2 * h2:2 * h2 + 2, :].rearrange("p a b -> p (a b)"), ps[:, :],
                     xr=[pk], writes=[f'Ed{2 * h2}', f'Ed{2 * h2 + 1}'])
            for h in range(16):
                ps, pk = nextA()
                P.mm(ps[0:40, :], jrev[0:40, 128:168], Hc[:, h, :], True, True, reads=['jrev', 'Hc'], xw=[pk])
                P.cp('dve' if h % 2 else 'act', Er[:, h, :], ps[0:40, :], xr=[pk], writes=[f'Er{h}'])
            stg = [P.sb(f"stg{i}", [128, 8, 512], F32, ses) for i in range(2)]
            stb = [P.sb(f"stb{i}", [128, 8, 512], BF16, ses) for i in range(2)]
            gin_b = prm[:, :, 0:1].broadcast_to([128, 8, 512])
            for i_ in range(2):
                P.op('pool', lambda e, i_=i_: e.memset(stb[i_][:], 0.0), writes=[f'stb{i_}'], cost=4000)
            for blk in range(NBLK):
                s = blk % 2
                sk, bk_ = f'stg{s}', f'stb{s}'
                if blk < 23:
                    scale = False
                    for ui in range(4):
                        un = units[blk * 4 + ui]
                        if un is None:
                            continue
                        src, cols = un
                        scale = scale or (src == 'w_in')
                        off = ui * 128
                        for (c0, n) in cols:
                            P.dma(stg[s][:, :, off:off + n],
                                  wsrc[src][:, c0:c0 + n].rearrange("(k p) n -> p k n", p=128), f'wld{s}', writes=[sk])
                            off += n
                    if blk == 4:
                        P.op('pool', lambda e, t=stg[s]: e.memset(t[:, :, 128:512], 0.0), writes=[sk])
                    if blk == 2:
                        P.op('pool', lambda e, t=stg[s]: e.memset(t[:, :, 48:128], 0.0), writes=[sk])
                    if scale:
                        P.tt('dve' if blk % 2 == 0 else 'pool', stb[s][:], stg[s][:], gin_b, ALU.mult,
                             reads=[sk, 'prm'], writes=[bk_])
                    else:
                        P.cp('act', stb[s][:], stg[s][:], reads=[sk], writes=[bk_])
                else:
                    wsel = w_ck1 if blk < 25 else w_cv1
                    hf = (blk - 23) % 2
                    P.dma(stg[s][:].rearrange("p a b -> p (a b)")[:, 0:2048].rearrange("p (j n) -> p j n", j=8),
                          wsel[hf * 1024:(hf + 1) * 1024, :].rearrange("(j p) n -> p j n", p=128), f'wld{s}', writes=[sk])
                    P.cp('act', stb[s][:].rearrange("p a b -> p (a b)")[:, 0:2048],
                         stg[s][:].rearrange("p a b -> p (a b)")[:, 0:2048], reads=[sk], writes=[bk_])
                P.dma(wsc[blk], stb[s][:].rearrange("p a b -> p (a b)"), f'wst{s}', reads=[bk_], writes=['wsc'])
            for which in range(2):
                for hf in range(2):
                    s = hf
                    P.dma(stb[s][:].rearrange("p a b -> p (a b)"), wsc[23 + which * 2 + hf], f'w2l{s}',
                          reads=['wsc'], writes=[f'stb{s}'])
                    w1v_ = stb[s][:].rearrange("p a b -> p (a b)")[:, 0:2048].rearrange("p (j n) -> p j n", j=8)
                    for hc in range(2):
                        for jj in range(8):
                            j = hf * 8 + jj
                            P.mm(X[:, (which * 2 + hc) * 2 + hf:(which * 2 + hc) * 2 + hf + 1],
                                 w1v_[:, jj, hc * 128:(hc + 1) * 128],
                                 post[:, which * 16 + j:which * 16 + j + 1], jj == 0, jj == 7,
                                 reads=[f'stb{s}', 'post'], xw=['X'])
            P.cp('dve', small[:, 0:8], X[:, 0:8], xr=['X'], writes=['small'])
            P.tt('dve', posb[:], small[:, 0:8:2], small[:, 1:8:2], ALU.add, reads=['small'], writes=['posb'])
            P.ts('dve', posbh[:], posb[:], 0.5, None, ALU.mult, reads=['posb'], writes=['posbh'])
            P.ts('dve', prmh[:], prm[:, :, 2:4], 0.5, None, ALU.mult, reads=['prm'], writes=['prmh'])
            P.op('pool', lambda e: e.memset(kcmpT[:], 0.0), writes=['kcmpT'])
            for i_ in range(4):
                P.op('pool', lambda e, i_=i_: e.memset(kr2[i_][:], 0.0), writes=[f'kr2_{i_}'])
            for g in range(2):
                P.op('pool', lambda e, g=g: e.memset(hidv[g][:], 0.0), writes=[f'hidv{g}'])
            P.op('pool', lambda e: e.memset(vslc[:, :, :, 64:65], 1.0), writes=['vslc'])
            P.op('pool', lambda e: e.memset(vwin[:, :, :, 64:65], 1.0), writes=['vwin'])
            P.op('pool', lambda e: e.memset(vcf[:, :, 0:64], 0.0), writes=['vcf_v'])
            for t in range(4):
                P.op('pool', lambda e, t=t: e.memset(vcn[t][:, :, 0:64], 0.0), writes=[f'vcn{t}_v'])
            P.fence()

        xt2 = [P.sb(f"xt{i}", [128, D], F32) for i in range(1)]
        xs2 = [P.sb(f"xs{i}", [128, D], BF16) for i in range(2)]
        hT2 = [P.sb(f"hT{i}", [128, 8, CH], BF16) for i in range(2)]
        chalf = P.sb("chalf", [128, 1], F32)
        P.op('pool', lambda e: e.memset(chalf[:], -0.5), writes=['chalf'])
        P.op('pool', lambda e: e.memset(qaug[:], 0.0), writes=[f'q{h}' for h in range(16)], cost=8000)
        P.alias['T16'] = [f'T16c{i}' for i in range(8)]
        P.alias['u'] = [f'u{i}' for i in range(8)]
        P.alias['vcf'] = ['vcf_v', 'vcf_o']
        for t_ in range(4):
            P.alias[f'vcn{t_}'] = [f'vcn{t_}_v', f'vcn{t_}_o']
        gchunk = [0]
        wb = [P.sb(f"wb{i}", [128, 8, 512], BF16) for i in range(2)]
        qaug = P.sb("qaug", [128, 16, CH], BF16)
        B0 = P.sb("B0", [128, 8, CH], BF16)
        B1 = P.sb("B1", [128, 8, CH], BF16)
        B2 = P.sb("B2", [128, 8, CH], BF16)
        u = P.sb("u", [128, 8, 30 + CH], BF16)
        T16 = P.sb("T16", [128, 4, D], F32)
        dgA = P.sb("dgA", [128, 16, 128], BF16)
        dgB = P.sb("dgB", [128, 15, 128], BF16)
        Pb = [P.sb(f"Pb{i}", [128, CH], BF16) for i in range(3)]
        Pf2 = [P.sb(f"Pf{i}", [96, CH], BF16) for i in range(2)]
        Pn2 = [P.sb(f"Pn{i}", [40, CH], BF16) for i in range(2)]
        ftmp = [P.sb(f"ftmp{i}", [128, 4, 64], F32) for i in range(2)]
        itmp = [P.sb(f"itmp{i}", [128, 4, 32], F32) for i in range(2)]
        ctmp = P.sb("ctmp", [128, 2, 64], F32)
        gates = P.sb("gates", [128, 4, 48], F32)
        ofb = P.sb("ofb", [128, D], BF16)
        tmpf = P.sb("tmpf", [128, 4, CH], F32)
        impacc = P.sb("impacc", [128, 4, 32], F32)
        prio = P.sb("prio", [128, 4, 32], F32)
        top8 = P.sb("top8", [128, 4, 8], F32)
        selM = P.sb("selM", [128, 4, 32], BF16)

        th = [tmpf[:, 0, :], tmpf[:, 1, :]]
        wcount = [0]

        def load_block(blk):
            s = wcount[0] % 2
            wcount[0] += 1
            P.dma(wb[s][:].rearrange("p a b -> p (a b)"), wsc[blk], f'wb{s}', reads=['wsc'], writes=[f'wb{s}'])
            return wb[s], f'wb{s}'

        def fm_job(wt, wk, ui, rhs, rkeys, evac, bank=None):
            ps, pk = (bank or nextA)()
            for kc in range(8):
                P.mm(ps[:, :], wt[:, kc, ui * 128:(ui + 1) * 128], rhs[:, kc, :], kc == 0, kc == 7,
                     reads=[wk] + rkeys, xw=[pk])
            evac(ps, pk)

        HT = {}

        def emit_pro_A1(seq, tc):
            if True:
                T0 = tc * CH
                hp = gchunk[0] % 2
                gchunk[0] += 1
                hT, hk = hT2[hp], f'hT{hp}'
                for qt in range(4):
                    xi = qt % 2
                    xt, xs, xtk, xsk = xt2[0], xs2[xi], 'xt0', f'xs{xi}'
                    sc = 16 + 4 * xi
                    P.dma(xt[:], x[seq, T0 + qt * 128:T0 + (qt + 1) * 128, :], 'xld0', writes=[xtk])
                    P.act(xs[:], xt[:], AF.Square, accum=small[:, sc:sc + 1], reads=[xtk], writes=[xsk, f'ss{xi}'])
                    P.ts('pool', small[:, sc + 1:sc + 2], small[:, sc:sc + 1], 1.0 / D, EPS, ALU.mult, ALU.add,
                         reads=[f'ss{xi}'], writes=[f'ms{xi}'])
                    P.tt('pool', small[:, sc + 2:sc + 3], small[:, sc + 1:sc + 2], chalf[:], ALU.pow,
                         reads=[f'ms{xi}', 'chalf'], writes=[f'rstd{xi}'])
                    P.ts('dve', xs[:], xt[:], small[:, sc + 2:sc + 3], None, ALU.mult, reads=[xtk, f'rstd{xi}'], writes=[xsk])
                    for c in range(8):
                        P.tr(T[:, c * 128:(c + 1) * 128], xs[:, c * 128:(c + 1) * 128], ident[:],
                             reads=[xsk, 'ident'], xw=['T'])
                    P.cp('act', hT[:, :, qt * 128:(qt + 1) * 128], T[:, :].rearrange("p (c n) -> p c n", c=8),
                         xr=['T'], writes=[hk])
                if tc > 0:
                    for i in range(4):
                        P.cp('pool', kr2[i][:, 0:16], kr2[i][:, 512:528], reads=[f'kr2_{i}'], writes=[f'kr2_{i}'])
                wt, wk = load_block(0)
                for ui in range(4):
                    def ev(ps, pk, ui=ui):
                        P.cp('dve', kr2[ui][0:64, 16:528], ps[0:64, :], xr=[pk], writes=[f'kr2_{ui}'])
                        P.cp('act', kr2[ui][64:128, 15:527], ps[64:128, :], xr=[pk], writes=[f'kr2_{ui}'])
                    fm_job(wt, wk, ui, hT, [hk], ev)
                wt, wk = load_block(1)
                for ui, (dst, dk) in enumerate([(kslc, 'kslc'), (kwin, 'kwin')]):
                    def ev(ps, pk, dst=dst, dk=dk):
                        P.cp('dve', dst[0:64, 0, T0:T0 + CH], ps[0:64, :], xr=[pk], writes=[dk])
                        P.cp('act', dst[0:64, 1, T0:T0 + CH], ps[64:128, :], xr=[pk], writes=[dk])
                    fm_job(wt, wk, ui, hT, [hk], ev)
                for qt in range(4):
                    ps, pk = nextA()
                    for kc in range(8):
                        P.mm(ps[:, 0:256], hT[:, kc, qt * 128:(qt + 1) * 128], wt[:, kc, 256:512], kc == 0, kc == 7,
                             reads=[wk, hk], xw=[pk])
                    kb = 4 * tc + qt
                    P.cp('dve', vslc[:, kb, :, 0:64], ps[:, 0:128].rearrange("p (g d) -> p g d", g=2),
                         xr=[pk], writes=['vslc'])
                    P.cp('act', vwin[:, kb, :, 0:64], ps[:, 128:256].rearrange("p (g d) -> p g d", g=2),
                         xr=[pk], writes=['vwin'])
                wt, wk = load_block(2)
                for qt in range(4):
                    ps, pk = nextA()
                    for kc in range(8):
                        P.mm(ps[:, 0:48], hT[:, kc, qt * 128:(qt + 1) * 128], wt[:, kc, 0:48], kc == 0, kc == 7,
                             reads=[wk, hk], xw=[pk])
                    P.act(gates[:, qt, :], ps[:, 0:48], AF.Tanh, scale=0.5, xr=[pk], writes=['gates'])
                    P.ts('dve', gates[:, qt, :], gates[:, qt, :], 0.5, 0.5, ALU.mult, ALU.add, reads=['gates'], writes=['gates'])

                def ev_q(i):
                    def ev(ps, pk):
                        P.act(qaug[0:64, 2 * i, :], ps[0:64, :], AF.Copy, scale=0.125, xr=[pk], writes=[f'q{2 * i}'])
                        P.ts('dve', qaug[0:64, 2 * i + 1, :], ps[64:128, :], 0.125, None, ALU.mult,
                             xr=[pk], writes=[f'q{2 * i + 1}'])
                    return ev

                for blk in range(2, 5):
                    if blk > 2:
                        wt, wk = load_block(blk)
                    for ui in range(4):
                        gu = blk * 4 + ui
                        if gu < 9 or gu > 16:
                            continue
                        fm_job(wt, wk, ui, hT, [hk], ev_q(gu - 9))
                HT[(seq, tc)] = (hT, hk)

        chunks = [(sq_, tc_) for sq_ in range(NSEQ) for tc_ in range(NCH)]
        for ci, (seq, tc) in enumerate(chunks):
            if True:
                T0 = tc * CH
                if (seq, tc) not in HT:
                    emit_pro_A1(seq, tc)
                hT, hk = HT[(seq, tc)]
                for blk in (5, 6):
                    wt, wk = load_block(blk)
                    for ui in range(4):
                        i = (blk - 5) * 4 + ui

                        def ev(ps, pk, i=i):
                            P.act(th[i % 2], ps[:, :], AF.Tanh, scale=0.5, xr=[pk], writes=[f'sq{i % 2}'])
                            P.stt('dve', B0[:, i, :], th[i % 2], 1.0, ps[:, :], ALU.add, ALU.mult,
                                  xr=[pk], reads=[f'sq{i % 2}'], writes=['B0'])
                        fm_job(wt, wk, ui, hT, [hk], ev)
                c0 = max(0, 32 * tc - 1)
                nn = 32 * tc + 31 - c0
                nfar = max(0, 32 * tc - 8)
                for which in range(2):
                    w1t = []
                    for hf in range(2):
                        wt, wk = load_block(23 + which * 2 + hf)
                        w1t.append((wt[:].rearrange("p a b -> p (a b)")[:, 0:2048].rearrange("p (j n) -> p j n", j=8), wk))
                    for g in range(2):
                        ki = which * 2 + g
                        ps, pk = nextA()
                        for hc in range(2):
                            for j in range(16):
                                st = 16 * (c0 - 32 * tc) + 16 + 2 * j
                                w1v_, wk = w1t[j // 8]
                                P.mm(ps[:, hc * 32:hc * 32 + nn], w1v_[:, j % 8, hc * 128:(hc + 1) * 128],
                                     kr2[ki][:, st:st + 16 * (nn - 1) + 1:16], j == 0, j == 15,
                                     reads=[wk, f'kr2_{ki}'], xw=[pk])
                        for hc in range(2):
                            pc = which * 2 + hc
                            P.act(ctmp[:, hc, 0:nn], ps[:, hc * 32:hc * 32 + nn], AF.Tanh, scale=0.5,
                                  bias=posbh[:, pc:pc + 1], xr=[pk], writes=[f'cth{hc}'])
                            P.ts('dve', ctmp[:, hc, 32:32 + nn], ps[:, hc * 32:hc * 32 + nn], posb[:, pc:pc + 1], None, ALU.add,
                                 xr=[pk], reads=['posb'], writes=[f'ctt{hc}'])
                            hdst, hkey = (hidk[:, hc, 0:nn], 'hidk') if which == 0 else \
                                (hidv[g][:, hc, 8 + c0:8 + c0 + nn], f'hidv{g}')
                            P.stt('dve', hdst, ctmp[:, hc, 0:nn], 1.0, ctmp[:, hc, 32:32 + nn], ALU.add, ALU.mult,
                                  reads=[f'cth{hc}', f'ctt{hc}'], writes=[hkey])
                        if which == 0:
                            for hc in range(2):
                                P.mm(X[0:64, 0:nn], w2k[:, hc, :], hidk[:, hc, 0:nn], hc == 0, hc == 1,
                                     reads=['w2k', 'hidk'], xw=['X'])
                            P.ts('dve', kcmpT[0:64, g, 8 + c0:8 + c0 + nn], X[0:64, 0:nn], 0.5, None, ALU.mult, xr=['X'], writes=['kcmpT'])
                        else:
                            if nfar > 0:
                                for hc in range(2):
                                    P.mm(X[0:nfar, 0:64], hidv[g][:, hc, 8:8 + nfar], w2v[:, hc, :], hc == 0, hc == 1,
                                         reads=['w2v', f'hidv{g}'], xw=['X'])
                                P.ts('dve', vcf[0:nfar, g, 0:64], X[0:nfar, 0:64], 0.5, None, ALU.mult, xr=['X'], writes=['vcf_v'])
                            for hc in range(2):
                                P.mm(X[0:40, 64:128], hidv[g][:, hc, 32 * tc:32 * tc + 40], w2v[:, hc, :], hc == 0, hc == 1,
                                     reads=['w2v', f'hidv{g}'], xw=['X'])
                            P.ts('dve', vcn[tc][:, g, 0:64], X[0:40, 64:128], 0.5, None, ALU.mult, xr=['X'], writes=[f'vcn{tc}_v'])
                orot = [0]

                def nextO():
                    i = orot[0] % 2
                    orot[0] += 1
                    return O[i], f"O{i}"

                fbrot = [0]

                def finish_branch(h, br, Ot, Ok, first, zcol):
                    k = fbrot[0] % 2
                    fbrot[0] += 1
                    base = 64 + 12 * k
                    rzm, rz, fac = small[:, base:base + 4], small[:, base + 4:base + 8], small[:, base + 8:base + 12]
                    kz, kr, kf = f'rz{k}', f'rzr{k}', f'fac{k}'
                    Ov = Ot[:, :].rearrange("p (q n) -> p q n", q=4)
                    P.ts('dve', rzm, Ov[:, :, zcol], 1e-30, None, ALU.max, xr=[Ok], writes=[kz])
                    P.op('dve', lambda e: e.reciprocal(rz, rzm), reads=[kz], writes=[kr], cost=80)
                    P.tt('dve', fac, rz, gates[:, :, br * 16 + h], ALU.mult, reads=[kr, 'gates'], writes=[kf])
                    hi = 1 if h >= 8 else 0
                    tks = [f'T16c{2 * qt + hi}' for qt in range(4)]
                    dst = T16[:, :, h * 64:(h + 1) * 64]
                    facb = fac.unsqueeze(2).broadcast_to([128, 4, 64])
                    if first:
                        P.tt('dve', dst, Ov[:, :, 0:64], facb, ALU.mult, xr=[Ok], reads=[kf], writes=tks)
                    else:
                        P.tt('dve', ftmp[k][:], Ov[:, :, 0:64], facb, ALU.mult, xr=[Ok], reads=[kf], writes=[f'ftmp{k}'])
                        P.tt('pool', dst, dst, ftmp[k][:], ALU.add, reads=[f'ftmp{k}'] + tks, writes=tks)
                    return k, rz, kr

                for g in range(2):
                    for hh in range(8):
                        h = g * 8 + hh
                        Pf, Pn, pfk, pnk = Pf2[h % 2], Pn2[h % 2], f'Pf{h % 2}', f'Pn{h % 2}'
                        if nfar > 0:
                            ps, pk = nextS()
                            P.mm(ps[0:nfar, :], kcmpT[:, g, 8:8 + nfar], qaug[:, h, :], True, True,
                                 reads=['kcmpT', f'q{h}'], xw=[pk])
                            P.act(Pf[0:nfar, :], ps[0:nfar, :], AF.Exp, bias=b31[0:nfar, h:h + 1], xr=[pk], writes=[pfk])
                        ps, pk = nextS()
                        P.mm(ps[0:40, :], kcmpT[:, g, 32 * tc:32 * tc + 40], qaug[:, h, :], True, True,
                             reads=['kcmpT', f'q{h}'], xw=[pk])
                        P.act(Pn[:, :], ps[0:40, :], AF.Exp, bias=b31[0:40, h:h + 1], xr=[pk], writes=[pnk])
                        P.tt('pool', Pn[:, :], Pn[:, :], Er[:, h, :], ALU.mult, reads=[pnk, f'Er{h}'], writes=[pnk])
                        Ot, Ok = A[3], 'A3'
                        Ov = Ot[:, :].rearrange("p (q n) -> p q n", q=4)
                        for qt in range(4):
                            if nfar > 0:
                                P.mm(Ov[:, qt, 0:97], Pf[0:nfar, qt * 128:(qt + 1) * 128], vcf[0:nfar, g, :], True, False,
                                     reads=[pfk, 'vcf'], xw=[Ok])
                            P.mm(Ov[:, qt, 0:97], Pn[:, qt * 128:(qt + 1) * 128], vcn[tc][:, g, :], nfar == 0, True,
                                 reads=[pnk, f'vcn{tc}'], xw=[Ok])
                        k, rz, kr = finish_branch(h, 0, Ot, Ok, True, 96)
                        rzb = rz.unsqueeze(2).broadcast_to([128, 4, 32])
                        if hh == 0:
                            P.tt('dve', impacc[:], Ov[:, :, 64:96], rzb, ALU.mult, xr=[Ok], reads=[kr], writes=['impacc'])
                        else:
                            P.tt('dve', itmp[k][:], Ov[:, :, 64:96], rzb, ALU.mult, xr=[Ok], reads=[kr], writes=[f'itmp{k}'])
                            P.tt('pool', impacc[:], impacc[:], itmp[k][:], ALU.add, reads=[f'itmp{k}', 'impacc'],
                                 writes=['impacc'])
                    P.tt('dve', prio[:], impacc[:], selA[:, 4 * tc:4 * tc + 4, :], ALU.mult,
                         reads=['impacc', 'selA'], writes=['prio'])
                    P.tt('dve', prio[:], prio[:], selB[:, 4 * tc:4 * tc + 4, :], ALU.add,
                         reads=['prio', 'selB'], writes=['prio'])
                    for qt in range(4):
                        P.op('dve', lambda e, qt=qt: e.max(top8[:, qt, :], prio[:, qt, :]), reads=['prio'], writes=['top8'])
                    P.ts('dve', small[:, 40:44], top8[:, :, 7], -5e5, None, ALU.max, reads=['top8'], writes=['thr'])
                    for qt in range(4):
                        P.ts('dve', selM[:, qt, :], prio[:, qt, :], small[:, 40 + qt:41 + qt], NEG, ALU.is_lt, ALU.mult,
                             reads=['prio', 'thr'], writes=['selM'])
                        P.tr(T[0:32, qt * 128:(qt + 1) * 128], selM[:, qt, :], ident[:], reads=['selM', 'ident'], xw=['T'])
                    for hh in range(8):
                        h = g * 8 + hh
                        P.cp('act' if hh % 2 == 0 else 'dve', qaug[64:96, h, :], T[0:32, 0:512], xr=['T'], writes=[f'qm{h}'])
                tiles = []
                for br, h in [(2, hh_) for hh_ in range(16)] + [(1, hh_) for hh_ in range(16)]:
                    g = h // 8
                    if True:
                        kb_lo = 0 if br == 1 else max(0, 4 * tc - 4)
                        kbs = list(range(kb_lo, 4 * tc + 4))
                        for i, kb in enumerate(kbs):
                            qlo = max(0, kb - 4 * tc)
                            qhi = 3 if br == 1 else min(3, kb + 4 - 4 * tc)
                            tiles.append(dict(h=h, g=g, br=br, kb=kb, qlo=qlo, qhi=qhi, first=(i == 0),
                                              last=(i == len(kbs) - 1)))
                prot = [0]
                cur = {}

                def stage_qk(t):
                    ps, pk = nextS()
                    pi = prot[0] % 3
                    prot[0] += 1
                    t['ps'], t['pk'], t['pb'], t['pbk'] = ps, pk, Pb[pi], f'Pb{pi}'
                    h, g, kb = t['h'], t['g'], t['kb']
                    c0_, c1_ = t['qlo'] * 128, (t['qhi'] + 1) * 128
                    if t['br'] == 1:
                        P.mm(ps[:, c0_:c1_], kslc[:, g, kb * 128:(kb + 1) * 128], qaug[:, h, c0_:c1_], True, True,
                             reads=['kslc', f'q{h}', f'qm{h}'], xw=[pk])
                    else:
                        P.mm(ps[:, c0_:c1_], kwin[:, g, kb * 128:(kb + 1) * 128], qaug[:, h, c0_:c1_], True, True,
                             reads=['kwin', f'q{h}'], xw=[pk])
                    P.act(t['pb'][:, c0_:c1_], ps[:, c0_:c1_], AF.Exp, bias=b31[:, h:h + 1], xr=[pk], writes=[t['pbk']])
                    d0 = kb - 4 * tc
                    dl = [d for d in (0, 1) if 0 <= d0 + d <= 3 and t['qlo'] <= d0 + d <= t['qhi']]
                    if dl:
                        a = (d0 + dl[0]) * 128
                        b = (d0 + dl[-1] + 1) * 128
                        P.tt('dve', t['pb'][:, a:b], t['pb'][:, a:b], Ed[:, h, dl[0] * 128:(dl[-1] + 1) * 128], ALU.mult,
                             rate=1.92, reads=[t['pbk'], f'Ed{h}'], writes=[t['pbk']])
                    if t['br'] == 2 and 0 <= d0 + 4 <= 3:
                        a = (d0 + 4) * 128
                        P.tt('pool', t['pb'][:, a:a + 128], t['pb'][:, a:a + 128], m4[:], ALU.mult,
                             reads=[t['pbk'], 'm4'], writes=[t['pbk']])

                def stage_pv(t):
                    key = (t['h'], t['br'])
                    if t['first']:
                        cur[key] = nextO()
                    Ot, Ok = cur[key]
                    Ov = Ot[:, :].rearrange("p (q n) -> p q n", q=4)
                    vt, vk = (vslc, 'vslc') if t['br'] == 1 else (vwin, 'vwin')
                    for qt in range(t['qlo'], t['qhi'] + 1):
                        st = t['first'] and qt == t['qlo']
                        P.mm(Ov[:, qt, 0:65], t['pb'][:, qt * 128:(qt + 1) * 128], vt[:, t['kb'], t['g'], :], st, False,
                             reads=[t['pbk'], vk], xw=[Ok], skip=True)
                    if t['last']:
                        finish_branch(t['h'], t['br'], Ot, Ok, False, 64)

                for i in range(len(tiles) + 1):
                    if i < len(tiles):
                        stage_qk(tiles[i])
                    if i >= 1:
                        stage_pv(tiles[i - 1])
                if tc == 0:
                    P.op('pool', lambda e: e.memset(u[:, :, 0:30], 0.0), writes=['u'])
                else:
                    P.cp('pool', u[:, :, 0:30], u[:, :, CH:CH + 30], reads=['u'], writes=['u'])
                for blk in range(9, 13):
                    wt, wk = load_block(blk)
                    for ui in range(4):
                        gu = blk * 4 + ui - 36
                        i = gu // 2
                        if gu % 2 == 0:
                            def ev(ps, pk, i=i):
                                P.act(th[i % 2], ps[:, :], AF.Tanh, scale=0.5, xr=[pk], writes=[f'sq{i % 2}'])
                        else:
                            def ev(ps, pk, i=i):
                                P.stt('dve', u[:, i, 30:30 + CH], th[i % 2], 1.0, ps[:, :], ALU.add, ALU.mult,
                                      xr=[pk], reads=[f'sq{i % 2}'], writes=[f'u{i}'])
                        fm_job(wt, wk, ui, hT, [hk], ev, bank=nextBG)
                for blk in (13, 14):
                    wt, wk = load_block(blk)
                    for ui in range(4):
                        i = (blk - 13) * 4 + ui

                        def ev(ps, pk, i=i):
                            P.act(th[i % 2], ps[:, :], AF.Tanh, scale=0.5, xr=[pk], writes=[f'sq{i % 2}'])
                            P.stt('dve', B2[:, i, :], th[i % 2], 1.0, ps[:, :], ALU.add, ALU.mult,
                                  xr=[pk], reads=[f'sq{i % 2}'], writes=['B2'])
                        fm_job(wt, wk, ui, hT, [hk], ev, bank=nextBG)
                for qt in range(4):
                    P.cp('act', ofb[:], T16[:, qt, :], reads=[f'T16c{2 * qt}', f'T16c{2 * qt + 1}'], writes=['ofb'])
                    for c in range(8):
                        P.tr(T[:, c * 128:(c + 1) * 128], ofb[:, c * 128:(c + 1) * 128], ident[:],
                             reads=['ofb', 'ident'], xw=['T'])
                    P.stt('dve', B1[:, :, qt * 128:(qt + 1) * 128], T[:, :].rearrange("p (c n) -> p c n", c=8), 0.5,
                          B0[:, :, qt * 128:(qt + 1) * 128], ALU.mult, ALU.mult, xr=['T'], reads=['B0'], writes=['B1'])
                cfp = T16[:, :, :].rearrange("p a b -> p (a b)").rearrange("p (c n) -> p c n", c=8)
                for i in range(8):
                    P.tt('pool', dgA[:], ident[:].unsqueeze(1).broadcast_to([128, 16, 128]),
                         prm[:, i, 4:20].unsqueeze(2).broadcast_to([128, 16, 128]), ALU.mult,
                         reads=['ident', 'prm'], writes=['dgA'])
                    P.tt('pool', dgB[:, 0:NPE - 16, :], ident[:].unsqueeze(1).broadcast_to([128, NPE - 16, 128]),
                         prm[:, i, 20:4 + NPE].unsqueeze(2).broadcast_to([128, NPE - 16, 128]), ALU.mult,
                         reads=['ident', 'prm'], writes=['dgB'])
                    ps, pk = nextA()
                    for j in range(NPE):
                        dgt, dk_ = (dgA[:, j, :], 'dgA') if j < 16 else (dgB[:, j - 16, :], 'dgB')
                        P.mm(ps[:, :], dgt, u[:, i, j:j + CH], j == 0, j == NPE - 1, reads=[dk_, f'u{i}'], xw=[pk])
                    acc, ak = tmpf[:, 2 + i % 2, :], ('mean' if i % 2 == 0 else 'rstdc')
                    for j in range(NPE, 31):
                        wj = prm[:, i, 4 + j:5 + j]
                        if j == NPE:
                            P.ts('dve', acc, u[:, i, j:j + CH], wj, None, ALU.mult, reads=[f'u{i}', 'prm'], writes=[ak])
                        else:
                            P.stt('dve', acc, u[:, i, j:j + CH], wj, acc, ALU.mult, ALU.add,
                                  reads=[f'u{i}', 'prm', ak], writes=[ak])
                    P.act(cfp[:, i, :], ps[:, :], AF.Identity, bias=prm[:, i, 1:2], scale=0.5, xr=[pk], writes=[f'T16c{i}'])
                    P.stt('dve', cfp[:, i, :], acc, 0.5, cfp[:, i, :], ALU.mult, ALU.add,
                          reads=[ak, f'T16c{i}'], writes=[f'T16c{i}'])
                    P.act(tmpf[:, i % 2, :], cfp[:, i, :], AF.Square, reads=[f'T16c{i}'], writes=[f'sq{i % 2}'])
                    P.mm(O[0][:, :], onesf[:], cfp[:, i, :], i == 0, i == 7, reads=['onesf', f'T16c{i}'], xw=['O0'])
                    P.mm(O[1][:, :], onesf[:], tmpf[:, i % 2, :], i == 0, i == 7, reads=['onesf', f'sq{i % 2}'], xw=['O1'])
                for blk in (7, 8):
                    wt, wk = load_block(blk)
                    for ui in range(4):
                        f = (blk - 7) * 4 + ui

                        def ev(ps, pk, f=f):
                            P.act(B0[:, f, :], ps[:, :], AF.Copy, scale=4.0, xr=[pk], writes=['B0'])
                        fm_job(wt, wk, ui, B1, ['B1'], ev)
                mean, msq, rstd = tmpf[:, 2, :], tmpf[:, 0, :], tmpf[:, 3, :]
                P.ts('dve', mean, O[0][:, :], 1.0 / D, None, ALU.mult, xr=['O0'], writes=['mean'])
                P.tt('dve', msq, mean, mean, ALU.mult, reads=['mean'], writes=['sq0'])
                P.stt('dve', rstd, O[1][:, :], 1.0 / D, msq, ALU.mult, ALU.subtract, xr=['O1'], reads=['sq0'], writes=['rstdc'])
                P.ts('dve', rstd, rstd, EPS, None, ALU.add, reads=['rstdc'], writes=['rstdc'])
                P.act(rstd, rstd, AF.Sqrt, reads=['rstdc'], writes=['rstdc'])
                P.op('dve', lambda e: e.reciprocal(rstd, rstd), reads=['rstdc'], writes=['rstdc'], cost=75 + CH / 0.96)
                for i in range(8):
                    P.tt('dve', cfp[:, i, :], cfp[:, i, :], mean, ALU.subtract, reads=[f'T16c{i}', 'mean'], writes=[f'T16c{i}'])
                    P.tt('pool', cfp[:, i, :], cfp[:, i, :], rstd, ALU.mult, reads=[f'T16c{i}', 'rstdc'], writes=[f'T16c{i}'])
                    P.act(th[i % 2], cfp[:, i, :], AF.Tanh, bias=prmh[:, i, 1:2], scale=prmh[:, i, 0:1],
                          reads=[f'T16c{i}', 'prmh'], writes=[f'sq{i % 2}'])
                    P.ts('dve', cfp[:, i, :], cfp[:, i, :], prm[:, i, 2:3], prm[:, i, 3:4], ALU.mult, ALU.add,
                         reads=[f'T16c{i}', 'prm'], writes=[f'T16c{i}'])
                    P.stt('dve', th[i % 2], th[i % 2], 1.0, cfp[:, i, :], ALU.add, ALU.mult,
                          reads=[f'sq{i % 2}', f'T16c{i}'], writes=[f'sq{i % 2}'])
                    P.tt('dve', B1[:, i, :], th[i % 2], B2[:, i, :], ALU.mult,
                         reads=[f'sq{i % 2}', 'B2'], writes=['B1'])
                for qt in range(4):
                    P.dma(T16[:, qt, :], x[seq, T0 + qt * 128:T0 + (qt + 1) * 128, :], f'xres{qt}', writes=[f'T16c{2 * qt}', f'T16c{2 * qt + 1}'])
                if ci + 1 < len(chunks):
                    emit_pro_A1(*chunks[ci + 1])
                for blk in (15, 16):
                    wt, wk = load_block(blk)
                    for ui in range(4):
                        f = (blk - 15) * 4 + ui

                        def ev(ps, pk, f=f):
                            P.cp('act', B2[:, f, :], ps[:, :], xr=[pk], writes=['B2'])
                        fm_job(wt, wk, ui, B1, ['B1'], ev)
                for blk in range(17, 21):
                    wt, wk = load_block(blk)
                    for ui in range(4):
                        gu = blk * 4 + ui - 68
                        i = gu % 8
                        if gu < 8:
                            def ev(ps, pk, i=i):
                                P.act(th[i % 2], ps[:, :], AF.Tanh, scale=0.5, xr=[pk], writes=[f'sq{i % 2}'])
                                P.stt('dve', B1[:, i, :], th[i % 2], 1.0, B2[:, i, :], ALU.add, ALU.mult,
                                      reads=[f'sq{i % 2}', 'B2'], writes=['B1'])
                        else:
                            def ev(ps, pk, i=i):
                                P.act(th[i % 2], ps[:, :], AF.Tanh, scale=0.5, xr=[pk], writes=[f'sq{i % 2}'])
                                P.stt('dve', th[i % 2], th[i % 2], 1.0, B0[:, i, :], ALU.add, ALU.mult,
                                      reads=[f'sq{i % 2}', 'B0'], writes=[f'sq{i % 2}'])
                                P.tt('pool', B1[:, i, :], B1[:, i, :], th[i % 2], ALU.add,
                                     reads=[f'sq{i % 2}', 'B1'], writes=['B1'])
                        fm_job(wt, wk, ui, hT, [hk], ev)
                for blk in (21, 22):
                    wt, wk = load_block(blk)
                    half = blk - 21
                    for qt in range(4):
                        ps, pk = nextA()
                        for kc in range(8):
                            P.mm(ps[:, :], B1[:, kc, qt * 128:(qt + 1) * 128], wt[:, kc, :], kc == 0, kc == 7,
                                 reads=[wk, 'B1'], xw=[pk])
                        dst = T16[:, qt, half * 512:(half + 1) * 512]
                        P.stt('dve', dst, ps[:, :], 0.125, dst, ALU.mult, ALU.add, xr=[pk], reads=[f'T16c{2 * qt + half}'], writes=[f'T16c{2 * qt + half}'])
                for qt in range(4):
                    tk = [f'T16c{2 * qt}', f'T16c{2 * qt + 1}']
                    P.act(ofb[:], T16[:, qt, :], AF.Square, accum=small[:, 48 + qt:49 + qt], reads=tk,
                          writes=['ofb', f'fss{qt}'])
                    P.ts('pool', small[:, 52 + qt:53 + qt], small[:, 48 + qt:49 + qt], 1.0 / D, EPS, ALU.mult, ALU.add,
                         reads=[f'fss{qt}'], writes=[f'fms{qt}'])
                    P.tt('pool', small[:, 60 + qt:61 + qt], small[:, 52 + qt:53 + qt], chalf[:], ALU.pow,
                         reads=[f'fms{qt}', 'chalf'], writes=[f'frs{qt}'])
                    P.stt('dve', T16[:, qt, :], T16[:, qt, :], small[:, 60 + qt:61 + qt], normf[:], ALU.mult, ALU.mult,
                          reads=tk + [f'frs{qt}', 'normf'], writes=tk)
                    P.dma(y[seq, T0 + qt * 128:T0 + (qt + 1) * 128, :], T16[:, qt, :], f'yst{qt}', reads=tk)
        P.emit(final_streams=[f'yst{q_}' for q_ in range(4)])
    return nc


_CACHE = {}


def kernel(x, norm_in_g, w_in, pos_ck, w_ck1, w_ck2, pos_cv, w_cv1, w_cv2, rel_bias,
           conv_w, conv_b, conv_ln_g, conv_ln_b, w_conv_proj, w_nsa_proj, w_out, norm_f_g):
    f = lambda a: np.ascontiguousarray(np.asarray(a, dtype=np.float32))
    x = f(x)
    if 'nc' not in _CACHE:
        _CACHE['nc'] = build_nc()
        _CACHE['consts'] = _host_consts()
    nc = _CACHE['nc']
    prm = np.concatenate([f(norm_in_g).reshape(1, D), f(conv_b).reshape(1, D), f(conv_ln_g).reshape(1, D),
                          f(conv_ln_b).reshape(1, D), f(conv_w).reshape(31, D)], axis=0)
    shared = {
        "w_in": f(w_in).reshape(D, 7984), "w_nsa": f(w_nsa_proj).reshape(D, D), "w_conv": f(w_conv_proj).reshape(D, D),
        "w_out": f(w_out).reshape(D, D), "w_ck1": f(w_ck1).reshape(2048, 256), "w_cv1": f(w_cv1).reshape(2048, 256),
        "w_ck2": f(w_ck2).reshape(256, 64), "w_cv2": f(w_cv2).reshape(256, 64),
        "pos_ck": f(pos_ck).reshape(16, 128), "pos_cv": f(pos_cv).reshape(16, 128),
        "rel_bias": f(rel_bias).reshape(1, 512), "rel_bias2": f(rel_bias).reshape(32, 16), "prm": np.ascontiguousarray(prm),
        "norm_f": f(norm_f_g).reshape(1, D),
    }
    shared.update(_CACHE['consts'])
    in_maps = []
    for c in range(NCORES):
        m = dict(shared)
        m["x"] = np.ascontiguousarray(x[c * NSEQ:(c + 1) * NSEQ])
        in_maps.append(m)
    res = run_bass_kernel_spmd(nc, in_maps, core_ids=list(range(NCORES)))
    return np.concatenate([r["y"] for r in res.results], axis=0).astype(np.float32)
```

```python
import numpy as np
from contextlib import ExitStack
import ml_dtypes
import concourse.bass as bass
import concourse.mybir as mybir
from concourse.bass_utils import run_bass_kernel_spmd

F32 = mybir.dt.float32
BF16 = mybir.dt.bfloat16
F32R = mybir.dt.float32r
AF = mybir.ActivationFunctionType
ALU = mybir.AluOpType

NCORES = 8
NSEQ = 4
S = 2048
D = 1024
CH = 512
NCH = S // CH
EPS = 1e-6
NEG = -30000.0


class Prog:
    ENGS = ('pe', 'act', 'dve', 'pool', 'sp')
    XLAT = 450.0
    SLAT = 120.0

    def __init__(self, nc, es):
        self.nc = nc
        self.es = es
        self.nodes = []
        self.regions = [0]
        self.sems = {}
        self.res = {}
        self.last_stream = {}
        self.alias = {}
        for e in self.ENGS:
            self._sem('E_' + e)

    def _sem(self, name):
        if name not in self.sems:
            self.sems[name] = self.es.enter_context(self.nc.semaphore(name))
        return self.sems[name]

    def sb(self, name, shape, dt, es=None):
        return (es or self.es).enter_context(self.nc.sbuf_tensor(name, list(shape), dt))

    def ps(self, name, shape, dt):
        return self.es.enter_context(self.nc.psum_tensor(name, list(shape), dt))

    def op(self, eng, fn, reads=(), writes=(), xr=(), xw=(), dma=None, cost=150.0, nbytes=0):
        nid = len(self.nodes)
        al = self.alias
        reads = [x for k in reads for x in al.get(k, (k,))]
        writes = [x for k in writes for x in al.get(k, (k,))]
        me = eng if dma is None else 'dma:' + dma
        if dma is not None:
            self._sem(dma)
        wait, order = set(), set()

        def R(k):
            return self.res.setdefault(k, {'w': {}, 'r': []})

        for k in reads:
            for a, p in R(k)['w'].items():
                wait.add(p)
        for k in xr:
            r = R(k)
            for a, p in r['w'].items():
                wait.add(p)
            for a, p in r['r']:
                if a != me:
                    wait.add(p)
        for k in list(writes) + list(xw):
            r = R(k)
            for a, p in r['w'].items():
                (order if (a == me and (eng == 'pe' or dma is not None)) else wait).add(p)
            for a, p in r['r']:
                (order if a == me else wait).add(p)
        if dma is not None and dma in self.last_stream:
            order.add(self.last_stream[dma])
        if dma is not None:
            self.last_stream[dma] = nid
        wait.discard(nid)
        order.discard(nid)
        order -= wait
        for k in list(reads) + list(xr):
            self.res[k]['r'].append((me, nid))
        for k in list(writes) + list(xw):
            self.res[k]['w'] = {me: nid}
            self.res[k]['r'] = []
        self.nodes.append(dict(eng=eng, fn=fn, dma=dma, wait=wait, order=order, cost=float(cost), nbytes=nbytes))
        return nid

    def fence(self):
        self.regions.append(len(self.nodes))

    def _schedule_region(self, lo, hi):
        import heapq
        nodes = self.nodes
        succ = {}
        indeg = {}
        for i in range(lo, hi):
            n = nodes[i]
            cnt = 0
            for p in n['wait'] | n['order']:
                if p >= lo:
                    succ.setdefault(p, []).append(i)
                    cnt += 1
            indeg[i] = cnt
        ready = {e: [] for e in self.ENGS}
        est = {}
        for i in range(lo, hi):
            if indeg[i] == 0:
                est[i] = 0.0
                heapq.heappush(ready[nodes[i]['eng']], i)
        free = {e: 0.0 for e in self.ENGS}
        start, finish = {}, {}
        dma_free = [0.0]
        order = {e: [] for e in self.ENGS}
        K = 24
        remaining = hi - lo
        while remaining:
            best = None
            for e in self.ENGS:
                h = ready[e]
                if not h:
                    continue
                cands = heapq.nsmallest(K, h)
                for i in cands:
                    st = max(free[e], est[i])
                    key = (st, i)
                    if best is None or key < best[0]:
                        best = (key, e, i)
            (st, i), e, _ = best
            ready[e].remove(i)
            heapq.heapify(ready[e])
            n = nodes[i]
            start[i] = st
            if n['dma'] is None:
                finish[i] = st + n['cost']
                free[e] = finish[i]
            else:
                free[e] = st + 60.0
                d0 = max(st + 1800.0, dma_free[0])
                finish[i] = d0 + n['nbytes'] / 150.0
                dma_free[0] = finish[i]
            order[e].append(i)
            remaining -= 1
            for sidx in succ.get(i, ()):
                indeg[sidx] -= 1
                if indeg[sidx] == 0:
                    sn = nodes[sidx]
                    t = 0.0
                    for p in sn['wait']:
                        if p >= lo:
                            t = max(t, finish[p] + (self.XLAT if nodes[p]['eng'] != sn['eng'] or nodes[p]['dma'] else self.SLAT))
                    for p in sn['order']:
                        if p >= lo:
                            t = max(t, start[p])
                    est[sidx] = t
                    heapq.heappush(ready[sn['eng']], sidx)
        self.sim_end = max(finish.values()) if finish else 0.0
        return order

    def emit(self, final_streams=()):
        nodes = self.nodes
        bounds = self.regions + [len(nodes)]
        eng_order = {e: [] for e in self.ENGS}
        pos = {}
        cnt = {s: 0 for s in self.sems}
        for r in range(len(bounds) - 1):
            lo, hi = bounds[r], bounds[r + 1]
            order = self._schedule_region(lo, hi)
            for e in self.ENGS:
                for i in order[e]:
                    n = nodes[i]
                    if n['dma'] is None:
                        cnt['E_' + e] += 1
                        pos[i] = ('E_' + e, cnt['E_' + e])
                    else:
                        cnt[n['dma']] += 16
                        pos[i] = (n['dma'], cnt[n['dma']])
                    eng_order[e].append(('op', i))
            if r < len(bounds) - 2:
                snap = dict(cnt)
                for e in self.ENGS:
                    eng_order[e].append(('fence', snap))
        block = self.es.enter_context(self.nc.Block())
        P = self
        self.n_waits = 0

        def run(name, eng):
            seen = {}
            for kind, x in eng_order[name]:
                if kind == 'fence':
                    for s, v in x.items():
                        if v > 0 and s != 'E_' + name and seen.get(s, 0) < v:
                            seen[s] = v
                            eng.wait_ge(P.sems[s], v)
                            P.n_waits += 1
                    continue
                n = nodes[x]
                need = {}
                for p in n['wait']:
                    s, v = pos[p]
                    if need.get(s, 0) < v:
                        need[s] = v
                for s, v in need.items():
                    if seen.get(s, 0) < v:
                        seen[s] = v
                        eng.wait_ge(P.sems[s], v)
                        P.n_waits += 1
                s, v = pos[x]
                n['fn'](eng).then_inc(P.sems[s], 16 if n['dma'] is not None else 1)
            if name == 'sp':
                for s in final_streams:
                    eng.wait_ge(P.sems[s], cnt[s])

        @block.tensor
        def _(e):
            run('pe', e)

        @block.scalar
        def _(e):
            run('act', e)

        @block.vector
        def _(e):
            run('dve', e)

        @block.gpsimd
        def _(e):
            run('pool', e)

        @block.sync
        def _(e):
            run('sp', e)

    @staticmethod
    def _n(ap):
        n = 1
        for d in ap.shape[1:]:
            n *= d
        return n

    def mm(self, out, lhsT, rhs, start, stop, reads, xw, skip=False):
        n = self._n(out)
        c = max(78.0, n / 1.95 + 12.0)
        if rhs.dtype == F32:
            c *= 4
        elif rhs.dtype == F32R:
            c *= 2
        self.op('pe', lambda e: e.matmul(out, lhsT, rhs, start=start, stop=stop,
                                         skip_group_check=skip), reads=reads, xw=xw, cost=c)

    def tr(self, out, in_, ident, reads, xw):
        self.op('pe', lambda e: e.transpose(out, in_, ident), reads=reads, xw=xw, cost=110.0)

    def act(self, out, in_, func, bias=None, scale=None, accum=None, **kw):
        def f(e):
            a = dict(out=out, in_=in_, func=func)
            if bias is not None:
                a['bias'] = bias
            if scale is not None:
                a['scale'] = scale
            if accum is not None:
                a['accum_out'] = accum
            return e.activation(**a)
        c = 120.0 + self._n(out) / 1.2
        self.op('act', f, cost=c, **kw)

    def _vc(self, eng, out, rate=0.96):
        if eng == 'pool':
            return 120.0 + self._n(out) / 0.45
        return 75.0 + self._n(out) / rate

    def tt(self, eng, out, in0, in1, op, rate=0.96, **kw):
        self.op(eng, lambda e: e.tensor_tensor(out, in0, in1, op), cost=self._vc(eng, out, rate), **kw)

    def ts(self, eng, out, in0, s1, s2, op0, op1=None, rate=0.96, **kw):
        c = self._vc(eng, out, rate)
        if op1 is None:
            self.op(eng, lambda e: e.tensor_scalar(out, in0, s1, s2, op0=op0), cost=c, **kw)
        else:
            self.op(eng, lambda e: e.tensor_scalar(out, in0, s1, s2, op0=op0, op1=op1), cost=c, **kw)

    def stt(self, eng, out, in0, scalar, in1, op0, op1, rate=0.96, **kw):
        self.op(eng, lambda e: e.scalar_tensor_tensor(out, in0, scalar, in1, op0=op0, op1=op1),
                cost=self._vc(eng, out, rate), **kw)

    def cp(self, eng, out, in_, rate=0.96, **kw):
        if eng == 'act':
            self.act(out, in_, AF.Copy, **kw)
        else:
            self.op(eng, lambda e: e.tensor_copy(out, in_), cost=self._vc(eng, out, rate), **kw)

    def dma(self, out, in_, sem, eng='sp', **kw):
        nb = 1
        for d in out.shape:
            nb *= d
        nb *= 2 if out.dtype == BF16 else 4
        self.op(eng, lambda e: e.dma_start(out=out, in_=in_), dma=sem, nbytes=nb, **kw)


def _t5_bucket(rel):
    rel = np.maximum(rel, 0)
    relf = np.maximum(rel, 1).astype(np.float32)
    large = 16 + (np.log(relf / np.float32(16)) / np.float32(np.log(128 / 16)) * np.float32(16)).astype(np.int32)
    large = np.minimum(large, 31)
    return np.where(rel < 16, rel, large)


def _host_consts():
    c = {}
    bf = ml_dtypes.bfloat16
    c['c_ident'] = np.eye(128, dtype=np.float32).astype(bf)
    c['c_identf'] = np.eye(128, dtype=np.float32)
    c['c_onesf'] = np.ones((128, 128), np.float32)
    relv = np.arange(1136) - 527
    bkv = _t5_bucket(relv)
    ohv = np.zeros((32, 1136), np.float32)
    for b in range(32):
        ohv[b, :] = ((bkv == b) & (relv >= 0))
    c['c_ohv'] = ohv
    jr = np.zeros((128, 168), np.float32)
    for kk in range(128):
        jr[kk, 127 - kk] = 1.0
    for kk in range(40):
        jr[kk, 128 + 39 - kk] = 1.0
    c['c_jrev'] = jr.astype(bf)
    c['c_m4'] = (np.arange(128)[None, :] < np.arange(128)[:, None]).astype(np.float32).astype(bf)
    selA = np.zeros((128, 16, 32), np.float32)
    selB = np.zeros((128, 16, 32), np.float32)
    j = np.arange(32)[None, :]
    for qt in range(16):
        t = qt * 128 + np.arange(128)[:, None]
        blk = t // 64
        valid = j <= blk
        forced = (j == 0) | (j == blk) | (j == blk - 1)
        selA[:, qt, :] = (valid & ~forced)
        selB[:, qt, :] = np.where(valid, np.where(forced, 1e6, 0.0), -1e6)
    c['c_selA'] = selA.astype(bf)
    c['c_selB'] = selB.astype(bf)
    cs = np.arange(127) * 16
    ce = cs + 31
    ss = np.arange(32) * 64
    ov = ((cs[:, None] <= ss[None, :] + 63) & (ce[:, None] >= ss[None, :])).astype(np.float32)
    ovf = np.zeros((128, 33), np.float32)
    ovf[:127, :32] = ov
    ovf[:127, 32] = 1.0
    c['c_ovf'] = ovf.astype(bf)
    ovn = np.zeros((40, 4, 33), np.float32)
    for tc in range(4):
        for rr in range(40):
            cidx = 32 * tc - 8 + rr
            if 0 <= cidx < 127:
                ovn[rr, tc, :32] = ov[cidx]
                ovn[rr, tc, 32] = 1.0
    c['c_ovn'] = ovn.astype(bf)
    bi = np.zeros((32, S), np.float32)
    for jj in range(32):
        bi[jj, jj * 64:(jj + 1) * 64] = 1.0
    c['c_blkind'] = bi.astype(bf)
    return c


def _units():
    U = []
    KC, VC, KS, VS, KW, VW, GT, ZN, GA, GB, ZC, MC, MN = (1024, 1152, 1280, 1408, 1536, 1664, 1792, 1840,
                                                          2864, 3888, 4912, 5936, 6960)
    U += [('w_in', [(KC, 64), (KC, 64)]), ('w_in', [(KC + 64, 64), (KC + 64, 64)]),
          ('w_in', [(VC, 64), (VC, 64)]), ('w_in', [(VC + 64, 64), (VC + 64, 64)])]
    U += [('w_in', [(KS, 128)]), ('w_in', [(KW, 128)]), ('w_in', [(VS, 128)]), ('w_in', [(VW, 128)])]
    U += [('w_in', [(GT, 48)])]
    U += [('w_in', [(i * 128, 128)]) for i in range(8)]
    U += [None, None, None]
    U += [('w_in', [(ZN + i * 128, 128)]) for i in range(8)]
    U += [('w_nsa', [(i * 128, 128)]) for i in range(8)]
    for i in range(8):
        U += [('w_in', [(GB + i * 128, 128)]), ('w_in', [(GA + i * 128, 128)])]
    U += [('w_in', [(ZC + i * 128, 128)]) for i in range(8)]
    U += [('w_conv', [(i * 128, 128)]) for i in range(8)]
    U += [('w_in', [(MC + i * 128, 128)]) for i in range(8)]
    U += [('w_in', [(MN + i * 128, 128)]) for i in range(8)]
    U += [('w_out', [(i * 128, 128)]) for i in range(8)]
    assert len(U) == 92
    return U


NBLK = 23 + 4


def build_nc():
    nc = bass.Bass("TRN2", target_bir_lowering=False)
    dt = nc.dram_tensor
    x = dt("x", [NSEQ, S, D], F32, kind="ExternalInput").ap()
    y = dt("y", [NSEQ, S, D], F32, kind="ExternalOutput").ap()
    w_in = dt("w_in", [D, 7984], F32, kind="ExternalInput").ap()
    w_nsa = dt("w_nsa", [D, D], F32, kind="ExternalInput").ap()
    w_conv = dt("w_conv", [D, D], F32, kind="ExternalInput").ap()
    w_out = dt("w_out", [D, D], F32, kind="ExternalInput").ap()
    wsrc = {'w_in': w_in, 'w_nsa': w_nsa, 'w_conv': w_conv, 'w_out': w_out}
    w_ck1 = dt("w_ck1", [2048, 256], F32, kind="ExternalInput").ap()
    w_cv1 = dt("w_cv1", [2048, 256], F32, kind="ExternalInput").ap()
    w_ck2 = dt("w_ck2", [256, 64], F32, kind="ExternalInput").ap()
    w_cv2 = dt("w_cv2", [256, 64], F32, kind="ExternalInput").ap()
    pos_ck = dt("pos_ck", [16, 128], F32, kind="ExternalInput").ap()
    pos_cv = dt("pos_cv", [16, 128], F32, kind="ExternalInput").ap()
    rel_bias = dt("rel_bias", [1, 512], F32, kind="ExternalInput").ap()
    prm_in = dt("prm", [35, D], F32, kind="ExternalInput").ap()
    norm_f = dt("norm_f", [1, D], F32, kind="ExternalInput").ap()
    c_ident = dt("c_ident", [128, 128], BF16, kind="ExternalInput").ap()
    c_identf = dt("c_identf", [128, 128], F32, kind="ExternalInput").ap()
    c_onesf = dt("c_onesf", [128, 128], F32, kind="ExternalInput").ap()
    c_ohv = dt("c_ohv", [32, 1136], F32, kind="ExternalInput").ap()
    c_jrev = dt("c_jrev", [128, 168], BF16, kind="ExternalInput").ap()
    gvd = dt("gvd", [16, 1136], BF16, kind="Internal")
    rel_bias2 = dt("rel_bias2", [32, 16], F32, kind="ExternalInput").ap()
    c_m4 = dt("c_m4", [128, 128], BF16, kind="ExternalInput").ap()
    c_selA = dt("c_selA", [128, 16, 32], BF16, kind="ExternalInput").ap()
    c_selB = dt("c_selB", [128, 16, 32], BF16, kind="ExternalInput").ap()
    c_ovf = dt("c_ovf", [128, 33], BF16, kind="ExternalInput").ap()
    c_ovn = dt("c_ovn", [40, 4, 33], BF16, kind="ExternalInput").ap()
    c_blkind = dt("c_blkind", [32, S], BF16, kind="ExternalInput").ap()
    wsc = dt("wsc", [NBLK, 128, 4096], BF16, kind="Internal").ap()

    units = _units()

    with ExitStack() as es:
        P = Prog(nc, es)
        A = [P.ps(f"psA{i}", [128, 512], F32) for i in range(4)]
        O = [P.ps(f"psO{i}", [128, 512], F32) for i in range(2)]
        X = P.ps("psX", [128, 512], F32)
        T = P.ps("psT", [128, 1024], BF16)
        arot = [0]

        def nextA():
            i = arot[0] % 4
            arot[0] += 1
            return A[i], f"A{i}"

        srot = [0]

        def nextS():
            i = srot[0] % 3
            srot[0] += 1
            return A[i], f"A{i}"

        brot = [0]

        def nextBG():
            i = brot[0] % 2
            brot[0] += 1
            return (X, "X")

        ident = P.sb("ident", [128, 128], BF16)
        identf = P.sb("identf", [128, 128], F32)
        onesf = P.sb("onesf", [128, 128], F32)
        Ed = P.sb("Ed", [128, 16, 256], BF16)
        Er = P.sb("Er", [40, 16, 512], BF16)
        m4 = P.sb("m4", [128, 128], BF16)
        selA = P.sb("selA", [128, 16, 32], BF16)
        selB = P.sb("selB", [128, 16, 32], BF16)
        prm = P.sb("prm_sb", [128, 8, 35], F32)
        normf = P.sb("normf", [128, D], F32)
        b31 = P.sb("b31", [128, 16], F32)
        w2k = P.sb("w2k", [128, 2, 64], BF16)
        w2v = P.sb("w2v", [128, 2, 64], BF16)
        posb = P.sb("posb", [128, 4], F32)
        posbh = P.sb("posbh", [128, 4], F32)
        prmh = P.sb("prmh", [128, 8, 2], F32)
        kslc = P.sb("kslc", [128, 2, S], BF16)
        kwin = P.sb("kwin", [128, 2, S], BF16)
        vslc = P.sb("vslc", [128, 16, 2, 65], BF16)
        vwin = P.sb("vwin", [128, 16, 2, 65], BF16)
        kr2 = [P.sb(f"kr2_{i}", [128, 528], BF16) for i in range(4)]
        hidv = [P.sb(f"hidv{g}", [128, 2, 136], BF16) for g in range(2)]
        hidk = P.sb("hidk", [128, 2, 32], BF16)
        kcmpT = P.sb("kcmpT", [128, 2, 136], BF16)
        vcf = P.sb("vcf", [128, 2, 97], BF16)
        vcn = [P.sb(f"vcn{t}", [40, 2, 97], BF16) for t in range(4)]
        small = P.sb("small", [128, 96], F32)

        with ExitStack() as ses:
            for (t, src, k) in [(ident, c_ident, 'ident'), (identf, c_identf, 'identf'), (onesf, c_onesf, 'onesf'),
                                (m4, c_m4, 'm4'), (selA, c_selA, 'selA'), (selB, c_selB, 'selB')]:
                P.dma(t[:], src, 'c_' + k, writes=[k])
            P.dma(normf[:], norm_f.partition_broadcast(128), 'cst', writes=['normf'])
            P.op('pool', lambda e: e.memset(kslc[:], 0.0), writes=['kslc'], cost=4000)
            P.op('pool', lambda e: e.memset(kwin[:], 0.0), writes=['kwin'], cost=4000)
            for g in range(2):
                P.dma(kslc[64:96, g, :], c_blkind, 'cst', writes=['kslc'])
                P.dma(vcf[:, g, 64:97], c_ovf, 'cst', writes=['vcf_o'])
                for t in range(4):
                    P.dma(vcn[t][:, g, 64:97], c_ovn[:, t, :], 'cst', writes=[f'vcn{t}_o'])
            praw = P.sb("praw", [35, D], F32, ses)
            P.dma(praw[:], prm_in, 'c1', writes=['praw'])
            for c in range(8):
                P.tr(X[:, c * 35:(c + 1) * 35], praw[:, c * 128:(c + 1) * 128], identf[0:35, 0:35],
                     reads=['praw', 'identf'], xw=['X'])
            P.cp('dve', prm[:].rearrange("p a b -> p (a b)"), X[:, 0:280], xr=['X'], writes=['prm'])
            posraw = P.sb("posraw", [32, 128], F32, ses)
            P.dma(posraw[0:16, :], pos_ck, 'c2', writes=['posraw'])
            P.dma(posraw[16:32, :], pos_cv, 'c2', writes=['posraw'])
            P.tr(X[:, 0:32], posraw[:], identf[0:32, 0:32], reads=['posraw', 'identf'], xw=['X'])
            post = P.sb("post", [128, 32], BF16, ses)
            P.cp('dve', post[:], X[:, 0:32], xr=['X'], writes=['post'])
            w2raw = P.sb("w2raw", [128, 2, 2, 64], F32, ses)
            P.dma(w2raw[:, 0, :, :], w_ck2.rearrange("(c p) n -> p c n", p=128), 'c3', writes=['w2raw'])
            P.dma(w2raw[:, 1, :, :], w_cv2.rearrange("(c p) n -> p c n", p=128), 'c3', writes=['w2raw'])
            P.cp('dve', w2k[:], w2raw[:, 0, :, :], reads=['w2raw'], writes=['w2k'])
            P.cp('dve', w2v[:], w2raw[:, 1, :, :], reads=['w2raw'], writes=['w2v'])
            rb = P.sb("rb", [128, 32, 16], F32, ses)
            P.dma(rb[:].rearrange("p a b -> p (a b)"), rel_bias.partition_broadcast(128), 'c4', writes=['rb'])
            P.cp('dve', b31[:], rb[:, 31, :], reads=['rb'], writes=['b31'])
            rbT = P.sb("rbT", [32, 16], F32, ses)
            ohv = P.sb("ohv", [32, 1136], F32, ses)
            jrev = P.sb("jrev", [128, 168], BF16, ses)
            gv = P.sb("gv", [16, 1136], BF16, ses)
            Hd = P.sb("Hd", [128, 16, 256], BF16, ses)
            Hc = P.sb("Hc", [40, 16, 512], BF16, ses)
            P.dma(rbT[:], rel_bias2, 'c6', writes=['rbT'])
            P.dma(ohv[:], c_ohv, 'c7', writes=['ohv'])
            P.dma(jrev[:], c_jrev, 'c8', writes=['jrev'])
            P.act(rbT[:], rbT[:], AF.Exp, reads=['rbT'], writes=['rbT'])
            for ci, (c0_, c1_) in enumerate([(0, 512), (512, 1024), (1024, 1136)]):
                P.mm(A[ci][0:16, 0:c1_ - c0_], rbT[:, :], ohv[:, c0_:c1_], True, True, reads=['rbT', 'ohv'], xw=[f'A{ci}'])
            P.op('dve', lambda e: e.reciprocal(small[0:16, 0:1], A[1][0:16, 215:216]), xr=['A1'], writes=['small'])
            for ci, (c0_, c1_) in enumerate([(0, 512), (512, 1024), (1024, 1136)]):
                P.ts('dve', gv[:, c0_:c1_], A[ci][0:16, 0:c1_ - c0_], small[0:16, 0:1], None, ALU.mult,
                     xr=[f'A{ci}'], reads=['small'], writes=['gv'])
            P.dma(gvd.ap(), gv[:], 'c9', reads=['gv'], writes=['gvd'])
            P.dma(Hd[:], bass.AP(tensor=gvd, offset=400, ap=[[1, 128], [1136, 16], [1, 256]]), 'c10',
                  reads=['gvd'], writes=['Hd'])
            P.dma(Hc[:], bass.AP(tensor=gvd, offset=0, ap=[[16, 40], [1136, 16], [1, 512]]), 'c11',
                  reads=['gvd'], writes=['Hc'])
            for h2 in range(8):
                ps, pk = nextA()
                P.mm(ps[:, :], jrev[:, 0:128], Hd[:, 2 * h2:2 * h2 + 2, :].rearrange("p a b -> p (a b)"), True, True,
                     reads=['jrev', 'Hd'], xw=[pk])
                P.cp('dve' if h2 % 2 else 'act', Ed[:, 2 * h2:2 * h2 + 2, :].rearrange("p a b -> p (a b)"), ps[:, :],
                     xr=[pk], writes=[f'Ed{2 * h2}', f'Ed{2 * h2 + 1}'])
            for h in range(16):
                ps, pk = nextA()
                P.mm(ps[0:40, :], jrev[0:40, 128:168], Hc[:, h, :], True, True, reads=['jrev', 'Hc'], xw=[pk])
                P.cp('dve' if h % 2 else 'act', Er[:, h, :], ps[0:40, :], xr=[pk], writes=[f'Er{h}'])
            stg = [P.sb(f"stg{i}", [128, 8, 512], F32, ses) for i in range(2)]
            stb = [P.sb(f"stb{i}", [128, 8, 512], BF16, ses) for i in range(2)]
            gin_b = prm[:, :, 0:1].broadcast_to([128, 8, 512])
            for i_ in range(2):
                P.op('pool', lambda e, i_=i_: e.memset(stb[i_][:], 0.0), writes=[f'stb{i_}'], cost=4000)
            for blk in range(NBLK):
                s = blk % 2
                sk, bk_ = f'stg{s}', f'stb{s}'
                if blk < 23:
                    scale = False
                    for ui in range(4):
                        un = units[blk * 4 + ui]
                        if un is None:
                            continue
                        src, cols = un
                        scale = scale or (src == 'w_in')
                        off = ui * 128
                        for (c0, n) in cols:
                            P.dma(stg[s][:, :, off:off + n],
                                  wsrc[src][:, c0:c0 + n].rearrange("(k p) n -> p k n", p=128), f'wld{s}', writes=[sk])
                            off += n
                    if blk == 4:
                        P.op('pool', lambda e, t=stg[s]: e.memset(t[:, :, 128:512], 0.0), writes=[sk])
                    if blk == 2:
                        P.op('pool', lambda e, t=stg[s]: e.memset(t[:, :, 48:128], 0.0), writes=[sk])
                    if scale:
                        P.tt('dve' if blk % 2 == 0 else 'pool', stb[s][:], stg[s][:], gin_b, ALU.mult,
                             reads=[sk, 'prm'], writes=[bk_])
                    else:
                        P.cp('act', stb[s][:], stg[s][:], reads=[sk], writes=[bk_])
                else:
                    wsel = w_ck1 if blk < 25 else w_cv1
                    hf = (blk - 23) % 2
                    P.dma(stg[s][:].rearrange("p a b -> p (a b)")[:, 0:2048].rearrange("p (j n) -> p j n", j=8),
                          wsel[hf * 1024:(hf + 1) * 1024, :].rearrange("(j p) n -> p j n", p=128), f'wld{s}', writes=[sk])
                    P.cp('act', stb[s][:].rearrange("p a b -> p (a b)")[:, 0:2048],
                         stg[s][:].rearrange("p a b -> p (a b)")[:, 0:2048], reads=[sk], writes=[bk_])
                P.dma(wsc[blk], stb[s][:].rearrange("p a b -> p (a b)"), f'wst{s}', reads=[bk_], writes=['wsc'])
            for which in range(2):
                for hf in range(2):
                    s = hf
                    P.dma(stb[s][:].rearrange("p a b -> p (a b)"), wsc[23 + which * 2 + hf], f'w2l{s}',
                          reads=['wsc'], writes=[f'stb{s}'])
                    w1v_ = stb[s][:].rearrange("p a b -> p (a b)")[:, 0:2048].rearrange("p (j n) -> p j n", j=8)
                    for hc in range(2):
                        for jj in range(8):
                            j = hf * 8 + jj
                            P.mm(X[:, (which * 2 + hc) * 2 + hf:(which * 2 + hc) * 2 + hf + 1],
                                 w1v_[:, jj, hc * 128:(hc + 1) * 128],
                                 post[:, which * 16 + j:which * 16 + j + 1], jj == 0, jj == 7,
                                 reads=[f'stb{s}', 'post'], xw=['X'])
            P.cp('dve', small[:, 0:8], X[:, 0:8], xr=['X'], writes=['small'])
            P.tt('dve', posb[:], small[:, 0:8:2], small[:, 1:8:2], ALU.add, reads=['small'], writes=['posb'])
            P.ts('dve', posbh[:], posb[:], 0.5, None, ALU.mult, reads=['posb'], writes=['posbh'])
            P.ts('dve', prmh[:], prm[:, :, 2:4], 0.5, None, ALU.mult, reads=['prm'], writes=['prmh'])
            P.op('pool', lambda e: e.memset(kcmpT[:], 0.0), writes=['kcmpT'])
            for i_ in range(4):
                P.op('pool', lambda e, i_=i_: e.memset(kr2[i_][:], 0.0), writes=[f'kr2_{i_}'])
            for g in range(2):
                P.op('pool', lambda e, g=g: e.memset(hidv[g][:], 0.0), writes=[f'hidv{g}'])
            P.op('pool', lambda e: e.memset(vslc[:, :, :, 64:65], 1.0), writes=['vslc'])
            P.op('pool', lambda e: e.memset(vwin[:, :, :, 64:65], 1.0), writes=['vwin'])
            P.op('pool', lambda e: e.memset(vcf[:, :, 0:64], 0.0), writes=['vcf_v'])
            for t in range(4):
                P.op('pool', lambda e, t=t: e.memset(vcn[t][:, :, 0:64], 0.0), writes=[f'vcn{t}_v'])
            P.fence()

        xt2 = [P.sb(f"xt{i}", [128, D], F32) for i in range(1)]
        xs2 = [P.sb(f"xs{i}", [128, D], BF16) for i in range(2)]
        hT2 = [P.sb(f"hT{i}", [128, 8, CH], BF16) for i in range(2)]
        chalf = P.sb("chalf", [128, 1], F32)
        P.op('pool', lambda e: e.memset(chalf[:], -0.5), writes=['chalf'])
        P.op('pool', lambda e: e.memset(qaug[:], 0.0), writes=[f'q{h}' for h in range(16)], cost=8000)
        P.alias['T16'] = [f'T16c{i}' for i in range(8)]
        P.alias['u'] = [f'u{i}' for i in range(8)]
        P.alias['vcf'] = ['vcf_v', 'vcf_o']
        for t_ in range(4):
            P.alias[f'vcn{t_}'] = [f'vcn{t_}_v', f'vcn{t_}_o']
        gchunk = [0]
        wb = [P.sb(f"wb{i}", [128, 8, 512], BF16) for i in range(2)]
        qaug = P.sb("qaug", [128, 16, CH], BF16)
        B0 = P.sb("B0", [128, 8, CH], BF16)
        B1 = P.sb("B1", [128, 8, CH], BF16)
        B2 = P.sb("B2", [128, 8, CH], BF16)
        u = P.sb("u", [128, 8, 30 + CH], BF16)
        T16 = P.sb("T16", [128, 4, D], F32)
        dgA = P.sb("dgA", [128, 16, 128], BF16)
        dgB = P.sb("dgB", [128, 15, 128], BF16)
        Pb = [P.sb(f"Pb{i}", [128, CH], BF16) for i in range(3)]
        Pf2 = [P.sb(f"Pf{i}", [96, CH], BF16) for i in range(2)]
        Pn2 = [P.sb(f"Pn{i}", [40, CH], BF16) for i in range(2)]
        ftmp = [P.sb(f"ftmp{i}", [128, 4, 64], F32) for i in range(2)]
        itmp = [P.sb(f"itmp{i}", [128, 4, 32], F32) for i in range(2)]
        ctmp = P.sb("ctmp", [128, 2, 64], F32)
        gates = P.sb("gates", [128, 4, 48], F32)
        ofb = P.sb("ofb", [128, D], BF16)
        tmpf = P.sb("tmpf", [128, 4, CH], F32)
        impacc = P.sb("impacc", [128, 4, 32], F32)
        prio = P.sb("prio", [128, 4, 32], F32)
        top8 = P.sb("top8", [128, 4, 8], F32)
        selM = P.sb("selM", [128, 4, 32], BF16)

        th = [tmpf[:, 0, :], tmpf[:, 1, :]]
        wcount = [0]

        def load_block(blk):
            s = wcount[0] % 2
            wcount[0] += 1
            P.dma(wb[s][:].rearrange("p a b -> p (a b)"), wsc[blk], f'wb{s}', reads=['wsc'], writes=[f'wb{s}'])
            return wb[s], f'wb{s}'

        def fm_job(wt, wk, ui, rhs, rkeys, evac, bank=None):
            ps, pk = (bank or nextA)()
            for kc in range(8):
                P.mm(ps[:, :], wt[:, kc, ui * 128:(ui + 1) * 128], rhs[:, kc, :], kc == 0, kc == 7,
                     reads=[wk] + rkeys, xw=[pk])
            evac(ps, pk)

        HT = {}

        def emit_pro_A1(seq, tc):
            if True:
                T0 = tc * CH
                hp = gchunk[0] % 2
                gchunk[0] += 1
                hT, hk = hT2[hp], f'hT{hp}'
                for qt in range(4):
                    xi = qt % 2
                    xt, xs, xtk, xsk = xt2[0], xs2[xi], 'xt0', f'xs{xi}'
                    sc = 16 + 4 * xi
                    P.dma(xt[:], x[seq, T0 + qt * 128:T0 + (qt + 1) * 128, :], 'xld0', writes=[xtk])
                    P.act(xs[:], xt[:], AF.Square, accum=small[:, sc:sc + 1], reads=[xtk], writes=[xsk, f'ss{xi}'])
                    P.ts('pool', small[:, sc + 1:sc + 2], small[:, sc:sc + 1], 1.0 / D, EPS, ALU.mult, ALU.add,
                         reads=[f'ss{xi}'], writes=[f'ms{xi}'])
                    P.tt('pool', small[:, sc + 2:sc + 3], small[:, sc + 1:sc + 2], chalf[:], ALU.pow,
                         reads=[f'ms{xi}', 'chalf'], writes=[f'rstd{xi}'])
                    P.ts('dve', xs[:], xt[:], small[:, sc + 2:sc + 3], None, ALU.mult, reads=[xtk, f'rstd{xi}'], writes=[xsk])
                    for c in range(8):
                        P.tr(T[:, c * 128:(c + 1) * 128], xs[:, c * 128:(c + 1) * 128], ident[:],
                             reads=[xsk, 'ident'], xw=['T'])
                    P.cp('act', hT[:, :, qt * 128:(qt + 1) * 128], T[:, :].rearrange("p (c n) -> p c n", c=8),
                         xr=['T'], writes=[hk])
                if tc > 0:
                    for i in range(4):
                        P.cp('pool', kr2[i][:, 0:16], kr2[i][:, 512:528], reads=[f'kr2_{i}'], writes=[f'kr2_{i}'])
                wt, wk = load_block(0)
                for ui in range(4):
                    def ev(ps, pk, ui=ui):
                        P.cp('dve', kr2[ui][0:64, 16:528], ps[0:64, :], xr=[pk], writes=[f'kr2_{ui}'])
                        P.cp('act', kr2[ui][64:128, 15:527], ps[64:128, :], xr=[pk], writes=[f'kr2_{ui}'])
                    fm_job(wt, wk, ui, hT, [hk], ev)
                wt, wk = load_block(1)
                for ui, (dst, dk) in enumerate([(kslc, 'kslc'), (kwin, 'kwin')]):
                    def ev(ps, pk, dst=dst, dk=dk):
                        P.cp('dve', dst[0:64, 0, T0:T0 + CH], ps[0:64, :], xr=[pk], writes=[dk])
                        P.cp('act', dst[0:64, 1, T0:T0 + CH], ps[64:128, :], xr=[pk], writes=[dk])
                    fm_job(wt, wk, ui, hT, [hk], ev)
                for qt in range(4):
                    ps, pk = nextA()
                    for kc in range(8):
                        P.mm(ps[:, 0:256], hT[:, kc, qt * 128:(qt + 1) * 128], wt[:, kc, 256:512], kc == 0, kc == 7,
                             reads=[wk, hk], xw=[pk])
                    kb = 4 * tc + qt
                    P.cp('dve', vslc[:, kb, :, 0:64], ps[:, 0:128].rearrange("p (g d) -> p g d", g=2),
                         xr=[pk], writes=['vslc'])
                    P.cp('act', vwin[:, kb, :, 0:64], ps[:, 128:256].rearrange("p (g d) -> p g d", g=2),
                         xr=[pk], writes=['vwin'])
                wt, wk = load_block(2)
                for qt in range(4):
                    ps, pk = nextA()
                    for kc in range(8):
                        P.mm(ps[:, 0:48], hT[:, kc, qt * 128:(qt + 1) * 128], wt[:, kc, 0:48], kc == 0, kc == 7,
                             reads=[wk, hk], xw=[pk])
                    P.act(gates[:, qt, :], ps[:, 0:48], AF.Tanh, scale=0.5, xr=[pk], writes=['gates'])
                    P.ts('dve', gates[:, qt, :], gates[:, qt, :], 0.5, 0.5, ALU.mult, ALU.add, reads=['gates'], writes=['gates'])

                def ev_q(i):
                    def ev(ps, pk):
                        P.act(qaug[0:64, 2 * i, :], ps[0:64, :], AF.Copy, scale=0.125, xr=[pk], writes=[f'q{2 * i}'])
                        P.ts('dve', qaug[0:64, 2 * i + 1, :], ps[64:128, :], 0.125, None, ALU.mult,
                             xr=[pk], writes=[f'q{2 * i + 1}'])
                    return ev

                for blk in range(2, 5):
                    if blk > 2:
                        wt, wk = load_block(blk)
                    for ui in range(4):
                        gu = blk * 4 + ui
                        if gu < 9 or gu > 16:
                            continue
                        fm_job(wt, wk, ui, hT, [hk], ev_q(gu - 9))
                HT[(seq, tc)] = (hT, hk)

        chunks = [(sq_, tc_) for sq_ in range(NSEQ) for tc_ in range(NCH)]
        for ci, (seq, tc) in enumerate(chunks):
            if True:
                T0 = tc * CH
                if (seq, tc) not in HT:
                    emit_pro_A1(seq, tc)
                hT, hk = HT[(seq, tc)]
                for blk in (5, 6):
                    wt, wk = load_block(blk)
                    for ui in range(4):
                        i = (blk - 5) * 4 + ui

                        def ev(ps, pk, i=i):
                            P.act(th[i % 2], ps[:, :], AF.Tanh, scale=0.5, xr=[pk], writes=[f'sq{i % 2}'])
                            P.stt('dve', B0[:, i, :], th[i % 2], 1.0, ps[:, :], ALU.add, ALU.mult,
                                  xr=[pk], reads=[f'sq{i % 2}'], writes=['B0'])
                        fm_job(wt, wk, ui, hT, [hk], ev)
                c0 = max(0, 32 * tc - 1)
                nn = 32 * tc + 31 - c0
                nfar = max(0, 32 * tc - 8)
                for which in range(2):
                    w1t = []
                    for hf in range(2):
                        wt, wk = load_block(23 + which * 2 + hf)
                        w1t.append((wt[:].rearrange("p a b -> p (a b)")[:, 0:2048].rearrange("p (j n) -> p j n", j=8), wk))
                    for g in range(2):
                        ki = which * 2 + g
                        ps, pk = nextA()
                        for hc in range(2):
                            for j in range(16):
                                st = 16 * (c0 - 32 * tc) + 16 + 2 * j
                                w1v_, wk = w1t[j // 8]
                                P.mm(ps[:, hc * 32:hc * 32 + nn], w1v_[:, j % 8, hc * 128:(hc + 1) * 128],
                                     kr2[ki][:, st:st + 16 * (nn - 1) + 1:16], j == 0, j == 15,
                                     reads=[wk, f'kr2_{ki}'], xw=[pk])
                        for hc in range(2):
                            pc = which * 2 + hc
                            P.act(ctmp[:, hc, 0:nn], ps[:, hc * 32:hc * 32 + nn], AF.Tanh, scale=0.5,
                                  bias=posbh[:, pc:pc + 1], xr=[pk], writes=[f'cth{hc}'])
                            P.ts('dve', ctmp[:, hc, 32:32 + nn], ps[:, hc * 32:hc * 32 + nn], posb[:, pc:pc + 1], None, ALU.add,
                                 xr=[pk], reads=['posb'], writes=[f'ctt{hc}'])
                            hdst, hkey = (hidk[:, hc, 0:nn], 'hidk') if which == 0 else \
                                (hidv[g][:, hc, 8 + c0:8 + c0 + nn], f'hidv{g}')
                            P.stt('dve', hdst, ctmp[:, hc, 0:nn], 1.0, ctmp[:, hc, 32:32 + nn], ALU.add, ALU.mult,
                                  reads=[f'cth{hc}', f'ctt{hc}'], writes=[hkey])
                        if which == 0:
                            for hc in range(2):
                                P.mm(X[0:64, 0:nn], w2k[:, hc, :], hidk[:, hc, 0:nn], hc == 0, hc == 1,
                                     reads=['w2k', 'hidk'], xw=['X'])
                            P.ts('dve', kcmpT[0:64, g, 8 + c0:8 + c0 + nn], X[0:64, 0:nn], 0.5, None, ALU.mult, xr=['X'], writes=['kcmpT'])
                        else:
                            if nfar > 0:
                                for hc in range(2):
                                    P.mm(X[0:nfar, 0:64], hidv[g][:, hc, 8:8 + nfar], w2v[:, hc, :], hc == 0, hc == 1,
                                         reads=['w2v', f'hidv{g}'], xw=['X'])
                                P.ts('dve', vcf[0:nfar, g, 0:64], X[0:nfar, 0:64], 0.5, None, ALU.mult, xr=['X'], writes=['vcf_v'])
                            for hc in range(2):
                                P.mm(X[0:40, 64:128], hidv[g][:, hc, 32 * tc:32 * tc + 40], w2v[:, hc, :], hc == 0, hc == 1,
                                     reads=['w2v', f'hidv{g}'], xw=['X'])
                            P.ts('dve', vcn[tc][:, g, 0:64], X[0:40, 64:128], 0.5, None, ALU.mult, xr=['X'], writes=[f'vcn{tc}_v'])
                orot = [0]

                def nextO():
                    i = orot[0] % 2
                    orot[0] += 1
                    return O[i], f"O{i}"

                fbrot = [0]

                def finish_branch(h, br, Ot, Ok, first, zcol):
                    k = fbrot[0] % 2
                    fbrot[0] += 1
                    base = 64 + 12 * k
                    rzm, rz, fac = small[:, base:base + 4], small[:, base + 4:base + 8], small[:, base + 8:base + 12]
                    kz, kr, kf = f'rz{k}', f'rzr{k}', f'fac{k}'
                    Ov = Ot[:, :].rearrange("p (q n) -> p q n", q=4)
                    P.ts('dve', rzm, Ov[:, :, zcol], 1e-30, None, ALU.max, xr=[Ok], writes=[kz])
                    P.op('dve', lambda e: e.reciprocal(rz, rzm), reads=[kz], writes=[kr], cost=80)
                    P.tt('dve', fac, rz, gates[:, :, br * 16 + h], ALU.mult, reads=[kr, 'gates'], writes=[kf])
                    hi = 1 if h >= 8 else 0
                    tks = [f'T16c{2 * qt + hi}' for qt in range(4)]
                    dst = T16[:, :, h * 64:(h + 1) * 64]
                    facb = fac.unsqueeze(2).broadcast_to([128, 4, 64])
                    if first:
                        P.tt('dve', dst, Ov[:, :, 0:64], facb, ALU.mult, xr=[Ok], reads=[kf], writes=tks)
                    else:
                        P.tt('dve', ftmp[k][:], Ov[:, :, 0:64], facb, ALU.mult, xr=[Ok], reads=[kf], writes=[f'ftmp{k}'])
                        P.tt('pool', dst, dst, ftmp[k][:], ALU.add, reads=[f'ftmp{k}'] + tks, writes=tks)
                    return k, rz, kr

                for g in range(2):
                    for hh in range(8):
                        h = g * 8 + hh
                        Pf, Pn, pfk, pnk = Pf2[h % 2], Pn2[h % 2], f'Pf{h % 2}', f'Pn{h % 2}'
                        if nfar > 0:
                            ps, pk = nextS()
                            P.mm(ps[0:nfar, :], kcmpT[:, g, 8:8 + nfar], qaug[:, h, :], True, True,
                                 reads=['kcmpT', f'q{h}'], xw=[pk])
                            P.act(Pf[0:nfar, :], ps[0:nfar, :], AF.Exp, bias=b31[0:nfar, h:h + 1], xr=[pk], writes=[pfk])
                        ps, pk = nextS()
                        P.mm(ps[0:40, :], kcmpT[:, g, 32 * tc:32 * tc + 40], qaug[:, h, :], True, True,
                             reads=['kcmpT', f'q{h}'], xw=[pk])
                        P.act(Pn[:, :], ps[0:40, :], AF.Exp, bias=b31[0:40, h:h + 1], xr=[pk], writes=[pnk])
                        P.tt('pool', Pn[:, :], Pn[:, :], Er[:, h, :], ALU.mult, reads=[pnk, f'Er{h}'], writes=[pnk])
                        Ot, Ok = A[3], 'A3'
                        Ov = Ot[:, :].rearrange("p (q n) -> p q n", q=4)
                        for qt in range(4):
                            if nfar > 0:
                                P.mm(Ov[:, qt, 0:97], Pf[0:nfar, qt * 128:(qt + 1) * 128], vcf[0:nfar, g, :], True, False,
                                     reads=[pfk, 'vcf'], xw=[Ok])
                            P.mm(Ov[:, qt, 0:97], Pn[:, qt * 128:(qt + 1) * 128], vcn[tc][:, g, :], nfar == 0, True,
                                 reads=[pnk, f'vcn{tc}'], xw=[Ok])
                        k, rz, kr = finish_branch(h, 0, Ot, Ok, True, 96)
                        rzb = rz.unsqueeze(2).broadcast_to([128, 4, 32])
                        if hh == 0:
                            P.tt('dve', impacc[:], Ov[:, :, 64:96], rzb, ALU.mult, xr=[Ok], reads=[kr], writes=['impacc'])
                        else:
                            P.tt('dve', itmp[k][:], Ov[:, :, 64:96], rzb, ALU.mult, xr=[Ok], reads=[kr], writes=[f'itmp{k}'])
                            P.tt('pool', impacc[:], impacc[:], itmp[k][:], ALU.add, reads=[f'itmp{k}', 'impacc'],
                                 writes=['impacc'])
                    P.tt('dve', prio[:], impacc[:], selA[:, 4 * tc:4 * tc + 4, :], ALU.mult,
                         reads=['impacc', 'selA'], writes=['prio'])
                    P.tt('dve', prio[:], prio[:], selB[:, 4 * tc:4 * tc + 4, :], ALU.add,
                         reads=['prio', 'selB'], writes=['prio'])
                    for qt in range(4):
                        P.op('dve', lambda e, qt=qt: e.max(top8[:, qt, :], prio[:, qt, :]), reads=['prio'], writes=['top8'])
                    P.ts('dve', small[:, 40:44], top8[:, :, 7], -5e5, None, ALU.max, reads=['top8'], writes=['thr'])
                    for qt in range(4):
                        P.ts('dve', selM[:, qt, :], prio[:, qt, :], small[:, 40 + qt:41 + qt], NEG, ALU.is_lt, ALU.mult,
                             reads=['prio', 'thr'], writes=['selM'])
                        P.tr(T[0:32, qt * 128:(qt + 1) * 128], selM[:, qt, :], ident[:], reads=['selM', 'ident'], xw=['T'])
                    for hh in range(8):
                        h = g * 8 + hh
                        P.cp('act' if hh % 2 == 0 else 'dve', qaug[64:96, h, :], T[0:32, 0:512], xr=['T'], writes=[f'qm{h}'])
                tiles = []
                for br, h in [(2, hh_) for hh_ in range(16)] + [(1, hh_) for hh_ in range(16)]:
                    g = h // 8
                    if True:
                        kb_lo = 0 if br == 1 else max(0, 4 * tc - 4)
                        kbs = list(range(kb_lo, 4 * tc + 4))
                        for i, kb in enumerate(kbs):
                            qlo = max(0, kb - 4 * tc)
                            qhi = 3 if br == 1 else min(3, kb + 4 - 4 * tc)
                            tiles.append(dict(h=h, g=g, br=br, kb=kb, qlo=qlo, qhi=qhi, first=(i == 0),
                                              last=(i == len(kbs) - 1)))
                prot = [0]
                cur = {}

                def stage_qk(t):
                    ps, pk = nextS()
                    pi = prot[0] % 3
                    prot[0] += 1
                    t['ps'], t['pk'], t['pb'], t['pbk'] = ps, pk, Pb[pi], f'Pb{pi}'
                    h, g, kb = t['h'], t['g'], t['kb']
                    c0_, c1_ = t['qlo'] * 128, (t['qhi'] + 1) * 128
                    if t['br'] == 1:
                        P.mm(ps[:, c0_:c1_], kslc[:, g, kb * 128:(kb + 1) * 128], qaug[:, h, c0_:c1_], True, True,
                             reads=['kslc', f'q{h}', f'qm{h}'], xw=[pk])
                    else:
                        P.mm(ps[:, c0_:c1_], kwin[:, g, kb * 128:(kb + 1) * 128], qaug[:, h, c0_:c1_], True, True,
                             reads=['kwin', f'q{h}'], xw=[pk])
                    P.act(t['pb'][:, c0_:c1_], ps[:, c0_:c1_], AF.Exp, bias=b31[:, h:h + 1], xr=[pk], writes=[t['pbk']])
                    d0 = kb - 4 * tc
                    dl = [d for d in (0, 1) if 0 <= d0 + d <= 3 and t['qlo'] <= d0 + d <= t['qhi']]
                    if dl:
                        a = (d0 + dl[0]) * 128
                        b = (d0 + dl[-1] + 1) * 128
                        P.tt('dve', t['pb'][:, a:b], t['pb'][:, a:b], Ed[:, h, dl[0] * 128:(dl[-1] + 1) * 128], ALU.mult,
                             rate=1.92, reads=[t['pbk'], f'Ed{h}'], writes=[t['pbk']])
                    if t['br'] == 2 and 0 <= d0 + 4 <= 3:
                        a = (d0 + 4) * 128
                        P.tt('pool', t['pb'][:, a:a + 128], t['pb'][:, a:a + 128], m4[:], ALU.mult,
                             reads=[t['pbk'], 'm4'], writes=[t['pbk']])

                def stage_pv(t):
                    key = (t['h'], t['br'])
                    if t['first']:
                        cur[key] = nextO()
                    Ot, Ok = cur[key]
                    Ov = Ot[:, :].rearrange("p (q n) -> p q n", q=4)
                    vt, vk = (vslc, 'vslc') if t['br'] == 1 else (vwin, 'vwin')
                    for qt in range(t['qlo'], t['qhi'] + 1):
                        st = t['first'] and qt == t['qlo']
                        P.mm(Ov[:, qt, 0:65], t['pb'][:, qt * 128:(qt + 1) * 128], vt[:, t['kb'], t['g'], :], st, False,
                             reads=[t['pbk'], vk], xw=[Ok], skip=True)
                    if t['last']:
                        finish_branch(t['h'], t['br'], Ot, Ok, False, 64)

                for i in range(len(tiles) + 1):
                    if i < len(tiles):
                        stage_qk(tiles[i])
                    if i >= 1:
                        stage_pv(tiles[i - 1])
                if tc == 0:
                    P.op('pool', lambda e: e.memset(u[:, :, 0:30], 0.0), writes=['u'])
                else:
                    P.cp('pool', u[:, :, 0:30], u[:, :, CH:CH + 30], reads=['u'], writes=['u'])
                for blk in range(9, 13):
                    wt, wk = load_block(blk)
                    for ui in range(4):
                        gu = blk * 4 + ui - 36
                        i = gu // 2
                        if gu % 2 == 0:
                            def ev(ps, pk, i=i):
                                P.act(th[i % 2], ps[:, :], AF.Tanh, scale=0.5, xr=[pk], writes=[f'sq{i % 2}'])
                        else:
                            def ev(ps, pk, i=i):
                                P.stt('dve', u[:, i, 30:30 + CH], th[i % 2], 1.0, ps[:, :], ALU.add, ALU.mult,
                                      xr=[pk], reads=[f'sq{i % 2}'], writes=[f'u{i}'])
                        fm_job(wt, wk, ui, hT, [hk], ev, bank=nextBG)
                for blk in (13, 14):
                    wt, wk = load_block(blk)
                    for ui in range(4):
                        i = (blk - 13) * 4 + ui

                        def ev(ps, pk, i=i):
                            P.act(th[i % 2], ps[:, :], AF.Tanh, scale=0.5, xr=[pk], writes=[f'sq{i % 2}'])
                            P.stt('dve', B2[:, i, :], th[i % 2], 1.0, ps[:, :], ALU.add, ALU.mult,
                                  xr=[pk], reads=[f'sq{i % 2}'], writes=['B2'])
                        fm_job(wt, wk, ui, hT, [hk], ev, bank=nextBG)
                for qt in range(4):
                    P.cp('act', ofb[:], T16[:, qt, :], reads=[f'T16c{2 * qt}', f'T16c{2 * qt + 1}'], writes=['ofb'])
                    for c in range(8):
                        P.tr(T[:, c * 128:(c + 1) * 128], ofb[:, c * 128:(c + 1) * 128], ident[:],
                             reads=['ofb', 'ident'], xw=['T'])
                    P.stt('dve', B1[:, :, qt * 128:(qt + 1) * 128], T[:, :].rearrange("p (c n) -> p c n", c=8), 0.5,
                          B0[:, :, qt * 128:(qt + 1) * 128], ALU.mult, ALU.mult, xr=['T'], reads=['B0'], writes=['B1'])
                cfp = T16[:, :, :].rearrange("p a b -> p (a b)").rearrange("p (c n) -> p c n", c=8)
                for i in range(8):
                    P.tt('pool', dgA[:], ident[:].unsqueeze(1).broadcast_to([128, 16, 128]),
                         prm[:, i, 4:20].unsqueeze(2).broadcast_to([128, 16, 128]), ALU.mult,
                         reads=['ident', 'prm'], writes=['dgA'])
                    P.tt('pool', dgB[:], ident[:].unsqueeze(1).broadcast_to([128, 15, 128]),
                         prm[:, i, 20:35].unsqueeze(2).broadcast_to([128, 15, 128]), ALU.mult,
                         reads=['ident', 'prm'], writes=['dgB'])
                    ps, pk = nextA()
                    for j in range(31):
                        dgt, dk_ = (dgA[:, j, :], 'dgA') if j < 16 else (dgB[:, j - 16, :], 'dgB')
                        P.mm(ps[:, :], dgt, u[:, i, j:j + CH], j == 0, j == 30, reads=[dk_, f'u{i}'], xw=[pk])
                    P.act(cfp[:, i, :], ps[:, :], AF.Identity, bias=prm[:, i, 1:2], scale=0.5, xr=[pk], writes=[f'T16c{i}'])
                    P.act(tmpf[:, i % 2, :], cfp[:, i, :], AF.Square, reads=[f'T16c{i}'], writes=[f'sq{i % 2}'])
                    P.mm(O[0][:, :], onesf[:], cfp[:, i, :], i == 0, i == 7, reads=['onesf', f'T16c{i}'], xw=['O0'])
                    P.mm(O[1][:, :], onesf[:], tmpf[:, i % 2, :], i == 0, i == 7, reads=['onesf', f'sq{i % 2}'], xw=['O1'])
                for blk in (7, 8):
                    wt, wk = load_block(blk)
                    for ui in range(4):
                        f = (blk - 7) * 4 + ui

                        def ev(ps, pk, f=f):
                            P.act(B0[:, f, :], ps[:, :], AF.Copy, scale=4.0, xr=[pk], writes=['B0'])
                        fm_job(wt, wk, ui, B1, ['B1'], ev)
                mean, msq, rstd = tmpf[:, 2, :], tmpf[:, 0, :], tmpf[:, 3, :]
                P.ts('dve', mean, O[0][:, :], 1.0 / D, None, ALU.mult, xr=['O0'], writes=['mean'])
                P.tt('dve', msq, mean, mean, ALU.mult, reads=['mean'], writes=['sq0'])
                P.stt('dve', rstd, O[1][:, :], 1.0 / D, msq, ALU.mult, ALU.subtract, xr=['O1'], reads=['sq0'], writes=['rstdc'])
                P.ts('dve', rstd, rstd, EPS, None, ALU.add, reads=['rstdc'], writes=['rstdc'])
                P.act(rstd, rstd, AF.Sqrt, reads=['rstdc'], writes=['rstdc'])
                P.op('dve', lambda e: e.reciprocal(rstd, rstd), reads=['rstdc'], writes=['rstdc'], cost=75 + CH / 0.96)
                for i in range(8):
                    P.tt('dve', cfp[:, i, :], cfp[:, i, :], mean, ALU.subtract, reads=[f'T16c{i}', 'mean'], writes=[f'T16c{i}'])
                    P.tt('pool', cfp[:, i, :], cfp[:, i, :], rstd, ALU.mult, reads=[f'T16c{i}', 'rstdc'], writes=[f'T16c{i}'])
                    P.act(th[i % 2], cfp[:, i, :], AF.Tanh, bias=prmh[:, i, 1:2], scale=prmh[:, i, 0:1],
                          reads=[f'T16c{i}', 'prmh'], writes=[f'sq{i % 2}'])
                    P.ts('dve', cfp[:, i, :], cfp[:, i, :], prm[:, i, 2:3], prm[:, i, 3:4], ALU.mult, ALU.add,
                         reads=[f'T16c{i}', 'prm'], writes=[f'T16c{i}'])
                    P.stt('dve', th[i % 2], th[i % 2], 1.0, cfp[:, i, :], ALU.add, ALU.mult,
                          reads=[f'sq{i % 2}', f'T16c{i}'], writes=[f'sq{i % 2}'])
                    P.tt('dve', B1[:, i, :], th[i % 2], B2[:, i, :], ALU.mult,
                         reads=[f'sq{i % 2}', 'B2'], writes=['B1'])
                for qt in range(4):
                    P.dma(T16[:, qt, :], x[seq, T0 + qt * 128:T0 + (qt + 1) * 128, :], f'xres{qt}', writes=[f'T16c{2 * qt}', f'T16c{2 * qt + 1}'])
                if ci + 1 < len(chunks):
                    emit_pro_A1(*chunks[ci + 1])
                for blk in (15, 16):
                    wt, wk = load_block(blk)
                    for ui in range(4):
                        f = (blk - 15) * 4 + ui

                        def ev(ps, pk, f=f):
                            P.cp('act', B2[:, f, :], ps[:, :], xr=[pk], writes=['B2'])
                        fm_job(wt, wk, ui, B1, ['B1'], ev)
                for blk in range(17, 21):
                    wt, wk = load_block(blk)
                    for ui in range(4):
                        gu = blk * 4 + ui - 68
                        i = gu % 8
                        if gu < 8:
                            def ev(ps, pk, i=i):
                                P.act(th[i % 2], ps[:, :], AF.Tanh, scale=0.5, xr=[pk], writes=[f'sq{i % 2}'])
                                P.stt('dve', B1[:, i, :], th[i % 2], 1.0, B2[:, i, :], ALU.add, ALU.mult,
                                      reads=[f'sq{i % 2}', 'B2'], writes=['B1'])
                        else:
                            def ev(ps, pk, i=i):
                                P.act(th[i % 2], ps[:, :], AF.Tanh, scale=0.5, xr=[pk], writes=[f'sq{i % 2}'])
                                P.stt('dve', th[i % 2], th[i % 2], 1.0, B0[:, i, :], ALU.add, ALU.mult,
                                      reads=[f'sq{i % 2}', 'B0'], writes=[f'sq{i % 2}'])
                                P.tt('pool', B1[:, i, :], B1[:, i, :], th[i % 2], ALU.add,
                                     reads=[f'sq{i % 2}', 'B1'], writes=['B1'])
                        fm_job(wt, wk, ui, hT, [hk], ev)
                for blk in (21, 22):
                    wt, wk = load_block(blk)
                    half = blk - 21
                    for qt in range(4):
                        ps, pk = nextA()
                        for kc in range(8):
                            P.mm(ps[:, :], B1[:, kc, qt * 128:(qt + 1) * 128], wt[:, kc, :], kc == 0, kc == 7,
                                 reads=[wk, 'B1'], xw=[pk])
                        dst = T16[:, qt, half * 512:(half + 1) * 512]
                        P.stt('dve', dst, ps[:, :], 0.125, dst, ALU.mult, ALU.add, xr=[pk], reads=[f'T16c{2 * qt + half}'], writes=[f'T16c{2 * qt + half}'])
                for qt in range(4):
                    tk = [f'T16c{2 * qt}', f'T16c{2 * qt + 1}']
                    P.act(ofb[:], T16[:, qt, :], AF.Square, accum=small[:, 48 + qt:49 + qt], reads=tk,
                          writes=['ofb', f'fss{qt}'])
                    P.ts('pool', small[:, 52 + qt:53 + qt], small[:, 48 + qt:49 + qt], 1.0 / D, EPS, ALU.mult, ALU.add,
                         reads=[f'fss{qt}'], writes=[f'fms{qt}'])
                    P.tt('pool', small[:, 60 + qt:61 + qt], small[:, 52 + qt:53 + qt], chalf[:], ALU.pow,
                         reads=[f'fms{qt}', 'chalf'], writes=[f'frs{qt}'])
                    P.stt('dve', T16[:, qt, :], T16[:, qt, :], small[:, 60 + qt:61 + qt], normf[:], ALU.mult, ALU.mult,
                          reads=tk + [f'frs{qt}', 'normf'], writes=tk)
                    P.dma(y[seq, T0 + qt * 128:T0 + (qt + 1) * 128, :], T16[:, qt, :], f'yst{qt}', reads=tk)
        P.emit(final_streams=[f'yst{q_}' for q_ in range(4)])
    return nc


_CACHE = {}


def kernel(x, norm_in_g, w_in, pos_ck, w_ck1, w_ck2, pos_cv, w_cv1, w_cv2, rel_bias,
           conv_w, conv_b, conv_ln_g, conv_ln_b, w_conv_proj, w_nsa_proj, w_out, norm_f_g):
    f = lambda a: np.ascontiguousarray(np.asarray(a, dtype=np.float32))
    x = f(x)
    if 'nc' not in _CACHE:
        _CACHE['nc'] = build_nc()
        _CACHE['consts'] = _host_consts()
    nc = _CACHE['nc']
    prm = np.concatenate([f(norm_in_g).reshape(1, D), f(conv_b).reshape(1, D), f(conv_ln_g).reshape(1, D),
                          f(conv_ln_b).reshape(1, D), f(conv_w).reshape(31, D)], axis=0)
    shared = {
        "w_in": f(w_in).reshape(D, 7984), "w_nsa": f(w_nsa_proj).reshape(D, D), "w_conv": f(w_conv_proj).reshape(D, D),
        "w_out": f(w_out).reshape(D, D), "w_ck1": f(w_ck1).reshape(2048, 256), "w_cv1": f(w_cv1).reshape(2048, 256),
        "w_ck2": f(w_ck2).reshape(256, 64), "w_cv2": f(w_cv2).reshape(256, 64),
        "pos_ck": f(pos_ck).reshape(16, 128), "pos_cv": f(pos_cv).reshape(16, 128),
        "rel_bias": f(rel_bias).reshape(1, 512), "rel_bias2": f(rel_bias).reshape(32, 16), "prm": np.ascontiguousarray(prm),
        "norm_f": f(norm_f_g).reshape(1, D),
    }
    shared.update(_CACHE['consts'])
    in_maps = []
    for c in range(NCORES):
        m = dict(shared)
        m["x"] = np.ascontiguousarray(x[c * NSEQ:(c + 1) * NSEQ])
        in_maps.append(m)
    res = run_bass_kernel_spmd(nc, in_maps, core_ids=list(range(NCORES)))
    return np.concatenate([r["y"] for r in res.results], axis=0).astype(np.float32)
```

```python
import numpy as np
from contextlib import ExitStack
import ml_dtypes
import concourse.bass as bass
import concourse.mybir as mybir
from concourse.bass_utils import run_bass_kernel_spmd

F32 = mybir.dt.float32
BF16 = mybir.dt.bfloat16
F32R = mybir.dt.float32r
AF = mybir.ActivationFunctionType
ALU = mybir.AluOpType

NCORES = 8
NSEQ = 4
S = 2048
D = 1024
CH = 512
NCH = S // CH
EPS = 1e-6
NEG = -30000.0


class Prog:
    ENGS = ('pe', 'act', 'dve', 'pool', 'sp')
    XLAT = 450.0
    SLAT = 120.0

    def __init__(self, nc, es):
        self.nc = nc
        self.es = es
        self.nodes = []
        self.regions = [0]
        self.sems = {}
        self.res = {}
        self.last_stream = {}
        self.alias = {}
        for e in self.ENGS:
            self._sem('E_' + e)

    def _sem(self, name):
        if name not in self.sems:
            self.sems[name] = self.es.enter_context(self.nc.semaphore(name))
        return self.sems[name]

    def sb(self, name, shape, dt, es=None):
        return (es or self.es).enter_context(self.nc.sbuf_tensor(name, list(shape), dt))

    def ps(self, name, shape, dt):
        return self.es.enter_context(self.nc.psum_tensor(name, list(shape), dt))

    def op(self, eng, fn, reads=(), writes=(), xr=(), xw=(), dma=None, cost=150.0, nbytes=0):
        nid = len(self.nodes)
        al = self.alias
        reads = [x for k in reads for x in al.get(k, (k,))]
        writes = [x for k in writes for x in al.get(k, (k,))]
        me = eng if dma is None else 'dma:' + dma
        if dma is not None:
            self._sem(dma)
        wait, order = set(), set()

        def R(k):
            return self.res.setdefault(k, {'w': {}, 'r': []})

        for k in reads:
            for a, p in R(k)['w'].items():
                wait.add(p)
        for k in xr:
            r = R(k)
            for a, p in r['w'].items():
                wait.add(p)
            for a, p in r['r']:
                if a != me:
                    wait.add(p)
        for k in list(writes) + list(xw):
            r = R(k)
            for a, p in r['w'].items():
                (order if (a == me and (eng == 'pe' or dma is not None)) else wait).add(p)
            for a, p in r['r']:
                (order if (a == me and (eng == 'pe' or dma is not None)) else wait).add(p)
        if dma is not None and dma in self.last_stream:
            order.add(self.last_stream[dma])
        if dma is not None:
            self.last_stream[dma] = nid
        wait.discard(nid)
        order.discard(nid)
        order -= wait
        for k in list(reads) + list(xr):
            self.res[k]['r'].append((me, nid))
        for k in list(writes) + list(xw):
            self.res[k]['w'] = {me: nid}
            self.res[k]['r'] = []
        self.nodes.append(dict(eng=eng, fn=fn, dma=dma, wait=wait, order=order, cost=float(cost), nbytes=nbytes))
        return nid

    def fence(self):
        self.regions.append(len(self.nodes))

    def _schedule_region(self, lo, hi):
        import heapq
        nodes = self.nodes
        succ = {}
        indeg = {}
        for i in range(lo, hi):
            n = nodes[i]
            cnt = 0
            for p in n['wait'] | n['order']:
                if p >= lo:
                    succ.setdefault(p, []).append(i)
                    cnt += 1
            indeg[i] = cnt
        ready = {e: [] for e in self.ENGS}
        est = {}
        for i in range(lo, hi):
            if indeg[i] == 0:
                est[i] = 0.0
                heapq.heappush(ready[nodes[i]['eng']], i)
        free = {e: 0.0 for e in self.ENGS}
        start, finish = {}, {}
        dma_free = [0.0]
        order = {e: [] for e in self.ENGS}
        K = 24
        remaining = hi - lo
        while remaining:
            best = None
            for e in self.ENGS:
                h = ready[e]
                if not h:
                    continue
                cands = heapq.nsmallest(K, h)
                for i in cands:
                    st = max(free[e], est[i])
                    key = (st, i)
                    if best is None or key < best[0]:
                        best = (key, e, i)
            (st, i), e, _ = best
            ready[e].remove(i)
            heapq.heapify(ready[e])
            n = nodes[i]
            start[i] = st
            if n['dma'] is None:
                finish[i] = st + n['cost']
                free[e] = finish[i]
            else:
                free[e] = st + 60.0
                d0 = max(st + 1800.0, dma_free[0])
                finish[i] = d0 + n['nbytes'] / 150.0
                dma_free[0] = finish[i]
            order[e].append(i)
            remaining -= 1
            for sidx in succ.get(i, ()):
                indeg[sidx] -= 1
                if indeg[sidx] == 0:
                    sn = nodes[sidx]
                    t = 0.0
                    for p in sn['wait']:
                        if p >= lo:
                            t = max(t, finish[p] + (self.XLAT if nodes[p]['eng'] != sn['eng'] or nodes[p]['dma'] else self.SLAT))
                    for p in sn['order']:
                        if p >= lo:
                            t = max(t, start[p])
                    est[sidx] = t
                    heapq.heappush(ready[sn['eng']], sidx)
        self.sim_end = max(finish.values()) if finish else 0.0
        return order

    def emit(self, final_streams=()):
        nodes = self.nodes
        bounds = self.regions + [len(nodes)]
        eng_order = {e: [] for e in self.ENGS}
        pos = {}
        cnt = {s: 0 for s in self.sems}
        for r in range(len(bounds) - 1):
            lo, hi = bounds[r], bounds[r + 1]
            order = self._schedule_region(lo, hi)
            for e in self.ENGS:
                for i in order[e]:
                    n = nodes[i]
                    if n['dma'] is None:
                        cnt['E_' + e] += 1
                        pos[i] = ('E_' + e, cnt['E_' + e])
                    else:
                        cnt[n['dma']] += 16
                        pos[i] = (n['dma'], cnt[n['dma']])
                    eng_order[e].append(('op', i))
            if r < len(bounds) - 2:
                snap = dict(cnt)
                for e in self.ENGS:
                    eng_order[e].append(('fence', snap))
        block = self.es.enter_context(self.nc.Block())
        P = self
        self.n_waits = 0

        def run(name, eng):
            seen = {}
            for kind, x in eng_order[name]:
                if kind == 'fence':
                    for s, v in x.items():
                        if v > 0 and s != 'E_' + name and seen.get(s, 0) < v:
                            seen[s] = v
                            eng.wait_ge(P.sems[s], v)
                            P.n_waits += 1
                    continue
                n = nodes[x]
                need = {}
                for p in n['wait']:
                    s, v = pos[p]
                    if need.get(s, 0) < v:
                        need[s] = v
                for s, v in need.items():
                    if seen.get(s, 0) < v:
                        seen[s] = v
                        eng.wait_ge(P.sems[s], v)
                        P.n_waits += 1
                s, v = pos[x]
                n['fn'](eng).then_inc(P.sems[s], 16 if n['dma'] is not None else 1)
            if name == 'sp':
                for s in final_streams:
                    eng.wait_ge(P.sems[s], cnt[s])

        @block.tensor
        def _(e):
            run('pe', e)

        @block.scalar
        def _(e):
            run('act', e)

        @block.vector
        def _(e):
            run('dve', e)

        @block.gpsimd
        def _(e):
            run('pool', e)

        @block.sync
        def _(e):
            run('sp', e)

    @staticmethod
    def _n(ap):
        n = 1
        for d in ap.shape[1:]:
            n *= d
        return n

    def mm(self, out, lhsT, rhs, start, stop, reads, xw, skip=False):
        n = self._n(out)
        c = max(78.0, n / 1.95 + 12.0)
        if rhs.dtype == F32:
            c *= 4
        elif rhs.dtype == F32R:
            c *= 2
        self.op('pe', lambda e: e.matmul(out, lhsT, rhs, start=start, stop=stop,
                                         skip_group_check=skip), reads=reads, xw=xw, cost=c)

    def tr(self, out, in_, ident, reads, xw):
        self.op('pe', lambda e: e.transpose(out, in_, ident), reads=reads, xw=xw, cost=110.0)

    def act(self, out, in_, func, bias=None, scale=None, accum=None, **kw):
        def f(e):
            a = dict(out=out, in_=in_, func=func)
            if bias is not None:
                a['bias'] = bias
            if scale is not None:
                a['scale'] = scale
            if accum is not None:
                a['accum_out'] = accum
            return e.activation(**a)
        c = 120.0 + self._n(out) / 1.2
        self.op('act', f, cost=c, **kw)

    def _vc(self, eng, out, rate=0.96):
        if eng == 'pool':
            return 120.0 + self._n(out) / 0.45
        return 75.0 + self._n(out) / rate

    def tt(self, eng, out, in0, in1, op, rate=0.96, **kw):
        self.op(eng, lambda e: e.tensor_tensor(out, in0, in1, op), cost=self._vc(eng, out, rate), **kw)

    def ts(self, eng, out, in0, s1, s2, op0, op1=None, rate=0.96, **kw):
        c = self._vc(eng, out, rate)
        if op1 is None:
            self.op(eng, lambda e: e.tensor_scalar(out, in0, s1, s2, op0=op0), cost=c, **kw)
        else:
            self.op(eng, lambda e: e.tensor_scalar(out, in0, s1, s2, op0=op0, op1=op1), cost=c, **kw)

    def stt(self, eng, out, in0, scalar, in1, op0, op1, rate=0.96, **kw):
        self.op(eng, lambda e: e.scalar_tensor_tensor(out, in0, scalar, in1, op0=op0, op1=op1),
                cost=self._vc(eng, out, rate), **kw)

    def cp(self, eng, out, in_, rate=0.96, **kw):
        if eng == 'act':
            self.act(out, in_, AF.Copy, **kw)
        else:
            self.op(eng, lambda e: e.tensor_copy(out, in_), cost=self._vc(eng, out, rate), **kw)

    def dma(self, out, in_, sem, eng='sp', **kw):
        nb = 1
        for d in out.shape:
            nb *= d
        nb *= 2 if out.dtype == BF16 else 4
        self.op(eng, lambda e: e.dma_start(out=out, in_=in_), dma=sem, nbytes=nb, **kw)


def _t5_bucket(rel):
    rel = np.maximum(rel, 0)
    relf = np.maximum(rel, 1).astype(np.float32)
    large = 16 + (np.log(relf / np.float32(16)) / np.float32(np.log(128 / 16)) * np.float32(16)).astype(np.int32)
    large = np.minimum(large, 31)
    return np.where(rel < 16, rel, large)


def _host_consts():
    c = {}
    bf = ml_dtypes.bfloat16
    c['c_ident'] = np.eye(128, dtype=np.float32).astype(bf)
    c['c_identf'] = np.eye(128, dtype=np.float32)
    c['c_onesf'] = np.ones((128, 128), np.float32)
    relv = np.arange(1136) - 527
    bkv = _t5_bucket(relv)
    ohv = np.zeros((32, 1136), np.float32)
    for b in range(32):
        ohv[b, :] = ((bkv == b) & (relv >= 0))
    c['c_ohv'] = ohv
    jr = np.zeros((128, 168), np.float32)
    for kk in range(128):
        jr[kk, 127 - kk] = 1.0
    for kk in range(40):
        jr[kk, 128 + 39 - kk] = 1.0
    c['c_jrev'] = jr.astype(bf)
    c['c_m4'] = (np.arange(128)[None, :] < np.arange(128)[:, None]).astype(np.float32).astype(bf)
    selA = np.zeros((128, 16, 32), np.float32)
    selB = np.zeros((128, 16, 32), np.float32)
    j = np.arange(32)[None, :]
    for qt in range(16):
        t = qt * 128 + np.arange(128)[:, None]
        blk = t // 64
        valid = j <= blk
        forced = (j == 0) | (j == blk) | (j == blk - 1)
        selA[:, qt, :] = (valid & ~forced)
        selB[:, qt, :] = np.where(valid, np.where(forced, 1e6, 0.0), -1e6)
    c['c_selA'] = selA.astype(bf)
    c['c_selB'] = selB.astype(bf)
    cs = np.arange(127) * 16
    ce = cs + 31
    ss = np.arange(32) * 64
    ov = ((cs[:, None] <= ss[None, :] + 63) & (ce[:, None] >= ss[None, :])).astype(np.float32)
    ovf = np.zeros((128, 33), np.float32)
    ovf[:127, :32] = ov
    ovf[:127, 32] = 1.0
    c['c_ovf'] = ovf.astype(bf)
    ovn = np.zeros((40, 4, 33), np.float32)
    for tc in range(4):
        for rr in range(40):
            cidx = 32 * tc - 8 + rr
            if 0 <= cidx < 127:
                ovn[rr, tc, :32] = ov[cidx]
                ovn[rr, tc, 32] = 1.0
    c['c_ovn'] = ovn.astype(bf)
    bi = np.zeros((32, S), np.float32)
    for jj in range(32):
        bi[jj, jj * 64:(jj + 1) * 64] = 1.0
    c['c_blkind'] = bi.astype(bf)
    return c


def _units():
    U = []
    KC, VC, KS, VS, KW, VW, GT, ZN, GA, GB, ZC, MC, MN = (1024, 1152, 1280, 1408, 1536, 1664, 1792, 1840,
                                                          2864, 3888, 4912, 5936, 6960)
    U += [('w_in', [(KC, 64), (KC, 64)]), ('w_in', [(KC + 64, 64), (KC + 64, 64)]),
          ('w_in', [(VC, 64), (VC, 64)]), ('w_in', [(VC + 64, 64), (VC + 64, 64)])]
    U += [('w_in', [(KS, 128)]), ('w_in', [(KW, 128)]), ('w_in', [(VS, 128)]), ('w_in', [(VW, 128)])]
    U += [('w_in', [(GT, 48)])]
    U += [('w_in', [(i * 128, 128)]) for i in range(8)]
    U += [None, None, None]
    U += [('w_in', [(ZN + i * 128, 128)]) for i in range(8)]
    U += [('w_nsa', [(i * 128, 128)]) for i in range(8)]
    for i in range(8):
        U += [('w_in', [(GB + i * 128, 128)]), ('w_in', [(GA + i * 128, 128)])]
    U += [('w_in', [(ZC + i * 128, 128)]) for i in range(8)]
    U += [('w_conv', [(i * 128, 128)]) for i in range(8)]
    U += [('w_in', [(MC + i * 128, 128)]) for i in range(8)]
    U += [('w_in', [(MN + i * 128, 128)]) for i in range(8)]
    U += [('w_out', [(i * 128, 128)]) for i in range(8)]
    assert len(U) == 92
    return U


NBLK = 23 + 4


def build_nc():
    nc = bass.Bass("TRN2", target_bir_lowering=False)
    dt = nc.dram_tensor
    x = dt("x", [NSEQ, S, D], F32, kind="ExternalInput").ap()
    y = dt("y", [NSEQ, S, D], F32, kind="ExternalOutput").ap()
    w_in = dt("w_in", [D, 7984], F32, kind="ExternalInput").ap()
    w_nsa = dt("w_nsa", [D, D], F32, kind="ExternalInput").ap()
    w_conv = dt("w_conv", [D, D], F32, kind="ExternalInput").ap()
    w_out = dt("w_out", [D, D], F32, kind="ExternalInput").ap()
    wsrc = {'w_in': w_in, 'w_nsa': w_nsa, 'w_conv': w_conv, 'w_out': w_out}
    w_ck1 = dt("w_ck1", [2048, 256], F32, kind="ExternalInput").ap()
    w_cv1 = dt("w_cv1", [2048, 256], F32, kind="ExternalInput").ap()
    w_ck2 = dt("w_ck2", [256, 64], F32, kind="ExternalInput").ap()
    w_cv2 = dt("w_cv2", [256, 64], F32, kind="ExternalInput").ap()
    pos_ck = dt("pos_ck", [16, 128], F32, kind="ExternalInput").ap()
    pos_cv = dt("pos_cv", [16, 128], F32, kind="ExternalInput").ap()
    rel_bias = dt("rel_bias", [1, 512], F32, kind="ExternalInput").ap()
    prm_in = dt("prm", [35, D], F32, kind="ExternalInput").ap()
    norm_f = dt("norm_f", [1, D], F32, kind="ExternalInput").ap()
    c_ident = dt("c_ident", [128, 128], BF16, kind="ExternalInput").ap()
    c_identf = dt("c_identf", [128, 128], F32, kind="ExternalInput").ap()
    c_onesf = dt("c_onesf", [128, 128], F32, kind="ExternalInput").ap()
    c_ohv = dt("c_ohv", [32, 1136], F32, kind="ExternalInput").ap()
    c_jrev = dt("c_jrev", [128, 168], BF16, kind="ExternalInput").ap()
    gvd = dt("gvd", [16, 1136], BF16, kind="Internal")
    rel_bias2 = dt("rel_bias2", [32, 16], F32, kind="ExternalInput").ap()
    c_m4 = dt("c_m4", [128, 128], BF16, kind="ExternalInput").ap()
    c_selA = dt("c_selA", [128, 16, 32], BF16, kind="ExternalInput").ap()
    c_selB = dt("c_selB", [128, 16, 32], BF16, kind="ExternalInput").ap()
    c_ovf = dt("c_ovf", [128, 33], BF16, kind="ExternalInput").ap()
    c_ovn = dt("c_ovn", [40, 4, 33], BF16, kind="ExternalInput").ap()
    c_blkind = dt("c_blkind", [32, S], BF16, kind="ExternalInput").ap()
    wsc = dt("wsc", [NBLK, 128, 4096], BF16, kind="Internal").ap()

    units = _units()

    with ExitStack() as es:
        P = Prog(nc, es)
        A = [P.ps(f"psA{i}", [128, 512], F32) for i in range(4)]
        O = [P.ps(f"psO{i}", [128, 512], F32) for i in range(2)]
        X = P.ps("psX", [128, 512], F32)
        T = P.ps("psT", [128, 1024], BF16)
        arot = [0]

        def nextA():
            i = arot[0] % 4
            arot[0] += 1
            return A[i], f"A{i}"

        srot = [0]

        def nextS():
            i = srot[0] % 3
            srot[0] += 1
            return A[i], f"A{i}"

        brot = [0]

        def nextBG():
            i = brot[0] % 2
            brot[0] += 1
            return (X, "X")

        ident = P.sb("ident", [128, 128], BF16)
        identf = P.sb("identf", [128, 128], F32)
        onesf = P.sb("onesf", [128, 128], F32)
        Ed = P.sb("Ed", [128, 16, 256], BF16)
        Er = P.sb("Er", [40, 16, 512], BF16)
        m4 = P.sb("m4", [128, 128], BF16)
        selA = P.sb("selA", [128, 16, 32], BF16)
        selB = P.sb("selB", [128, 16, 32], BF16)
        prm = P.sb("prm_sb", [128, 8, 35], F32)
        normf = P.sb("normf", [128, D], F32)
        b31 = P.sb("b31", [128, 16], F32)
        w2k = P.sb("w2k", [128, 2, 64], BF16)
        w2v = P.sb("w2v", [128, 2, 64], BF16)
        posb = P.sb("posb", [128, 4], F32)
        posbh = P.sb("posbh", [128, 4], F32)
        prmh = P.sb("prmh", [128, 8, 2], F32)
        kslc = P.sb("kslc", [128, 2, S], BF16)
        kwin = P.sb("kwin", [128, 2, S], BF16)
        vslc = P.sb("vslc", [128, 16, 2, 65], BF16)
        vwin = P.sb("vwin", [128, 16, 2, 65], BF16)
        kr2 = [P.sb(f"kr2_{i}", [128, 528], BF16) for i in range(4)]
        hidv = [P.sb(f"hidv{g}", [128, 2, 136], BF16) for g in range(2)]
        hidk = P.sb("hidk", [128, 2, 32], BF16)
        kcmpT = P.sb("kcmpT", [128, 2, 136], BF16)
        vcf = P.sb("vcf", [128, 2, 97], BF16)
        vcn = [P.sb(f"vcn{t}", [40, 2, 97], BF16) for t in range(4)]
        small = P.sb("small", [128, 96], F32)

        with ExitStack() as ses:
            for (t, src, k) in [(ident, c_ident, 'ident'), (identf, c_identf, 'identf'), (onesf, c_onesf, 'onesf'),
                                (m4, c_m4, 'm4'), (selA, c_selA, 'selA'), (selB, c_selB, 'selB')]:
                P.dma(t[:], src, 'c_' + k, writes=[k])
            P.dma(normf[:], norm_f.partition_broadcast(128), 'cst', writes=['normf'])
            P.op('pool', lambda e: e.memset(kslc[:], 0.0), writes=['kslc'], cost=4000)
            P.op('pool', lambda e: e.memset(kwin[:], 0.0), writes=['kwin'], cost=4000)
            for g in range(2):
                P.dma(kslc[64:96, g, :], c_blkind, 'cst', writes=['kslc'])
                P.dma(vcf[:, g, 64:97], c_ovf, 'cst', writes=['vcf_o'])
                for t in range(4):
                    P.dma(vcn[t][:, g, 64:97], c_ovn[:, t, :], 'cst', writes=[f'vcn{t}_o'])
            praw = P.sb("praw", [35, D], F32, ses)
            P.dma(praw[:], prm_in, 'c1', writes=['praw'])
            for c in range(8):
                P.tr(X[:, c * 35:(c + 1) * 35], praw[:, c * 128:(c + 1) * 128], identf[0:35, 0:35],
                     reads=['praw', 'identf'], xw=['X'])
            P.cp('dve', prm[:].rearrange("p a b -> p (a b)"), X[:, 0:280], xr=['X'], writes=['prm'])
            posraw = P.sb("posraw", [32, 128], F32, ses)
            P.dma(posraw[0:16, :], pos_ck, 'c2', writes=['posraw'])
            P.dma(posraw[16:32, :], pos_cv, 'c2', writes=['posraw'])
            P.tr(X[:, 0:32], posraw[:], identf[0:32, 0:32], reads=['posraw', 'identf'], xw=['X'])
            post = P.sb("post", [128, 32], BF16, ses)
            P.cp('dve', post[:], X[:, 0:32], xr=['X'], writes=['post'])
            w2raw = P.sb("w2raw", [128, 2, 2, 64], F32, ses)
            P.dma(w2raw[:, 0, :, :], w_ck2.rearrange("(c p) n -> p c n", p=128), 'c3', writes=['w2raw'])
            P.dma(w2raw[:, 1, :, :], w_cv2.rearrange("(c p) n -> p c n", p=128), 'c3', writes=['w2raw'])
            P.cp('dve', w2k[:], w2raw[:, 0, :, :], reads=['w2raw'], writes=['w2k'])
            P.cp('dve', w2v[:], w2raw[:, 1, :, :], reads=['w2raw'], writes=['w2v'])
            rb = P.sb("rb", [128, 32, 16], F32, ses)
            P.dma(rb[:].rearrange("p a b -> p (a b)"), rel_bias.partition_broadcast(128), 'c4', writes=['rb'])
            P.cp('dve', b31[:], rb[:, 31, :], reads=['rb'], writes=['b31'])
            rbT = P.sb("rbT", [32, 16], F32, ses)
            ohv = P.sb("ohv", [32, 1136], F32, ses)
            jrev = P.sb("jrev", [128, 168], BF16, ses)
            gv = P.sb("gv", [16, 1136], BF16, ses)
            Hd = P.sb("Hd", [128, 16, 256], BF16, ses)
            Hc = P.sb("Hc", [40, 16, 512], BF16, ses)
            P.dma(rbT[:], rel_bias2, 'c6', writes=['rbT'])
            P.dma(ohv[:], c_ohv, 'c7', writes=['ohv'])
            P.dma(jrev[:], c_jrev, 'c8', writes=['jrev'])
            P.act(rbT[:], rbT[:], AF.Exp, reads=['rbT'], writes=['rbT'])
            for ci, (c0_, c1_) in enumerate([(0, 512), (512, 1024), (1024, 1136)]):
                P.mm(A[ci][0:16, 0:c1_ - c0_], rbT[:, :], ohv[:, c0_:c1_], True, True, reads=['rbT', 'ohv'], xw=[f'A{ci}'])
            P.op('dve', lambda e: e.reciprocal(small[0:16, 0:1], A[1][0:16, 215:216]), xr=['A1'], writes=['small'])
            for ci, (c0_, c1_) in enumerate([(0, 512), (512, 1024), (1024, 1136)]):
                P.ts('dve', gv[:, c0_:c1_], A[ci][0:16, 0:c1_ - c0_], small[0:16, 0:1], None, ALU.mult,
                     xr=[f'A{ci}'], reads=['small'], writes=['gv'])
            P.dma(gvd.ap(), gv[:], 'c9', reads=['gv'], writes=['gvd'])
            P.dma(Hd[:], bass.AP(tensor=gvd, offset=400, ap=[[1, 128], [1136, 16], [1, 256]]), 'c10',
                  reads=['gvd'], writes=['Hd'])
            P.dma(Hc[:], bass.AP(tensor=gvd, offset=0, ap=[[16, 40], [1136, 16], [1, 512]]), 'c11',
                  reads=['gvd'], writes=['Hc'])
            for h2 in range(8):
                ps, pk = nextA()
                P.mm(ps[:, :], jrev[:, 0:128], Hd[:, 2 * h2:2 * h2 + 2, :].rearrange("p a b -> p (a b)"), True, True,
                     reads=['jrev', 'Hd'], xw=[pk])
                P.cp('dve' if h2 % 2 else 'act', Ed[:, 2 * h2:2 * h2 + 2, :].rearrange("p a b -> p (a b)"), ps[:, :],
                     xr=[pk], writes=[f'Ed{2 * h2}', f'Ed{2 * h2 + 1}'])
            for h in range(16):
                ps, pk = nextA()
                P.mm(ps[0:40, :], jrev[0:40, 128:168], Hc[:, h, :], True, True, reads=['jrev', 'Hc'], xw=[pk])
                P.cp('dve' if h % 2 else 'act', Er[:, h, :], ps[0:40, :], xr=[pk], writes=[f'Er{h}'])
            stg = [P.sb(f"stg{i}", [128, 8, 512], F32, ses) for i in range(2)]
            stb = [P.sb(f"stb{i}", [128, 8, 512], BF16, ses) for i in range(2)]
            gin_b = prm[:, :, 0:1].broadcast_to([128, 8, 512])
            for i_ in range(2):
                P.op('pool', lambda e, i_=i_: e.memset(stb[i_][:], 0.0), writes=[f'stb{i_}'], cost=4000)
            for blk in range(NBLK):
                s = blk % 2
                sk, bk_ = f'stg{s}', f'stb{s}'
                if blk < 23:
                    scale = False
                    for ui in range(4):
                        un = units[blk * 4 + ui]
                        if un is None:
                            continue
                        src, cols = un
                        scale = scale or (src == 'w_in')
                        off = ui * 128
                        for (c0, n) in cols:
                            P.dma(stg[s][:, :, off:off + n],
                                  wsrc[src][:, c0:c0 + n].rearrange("(k p) n -> p k n", p=128), f'wld{s}', writes=[sk])
                            off += n
                    if blk == 4:
                        P.op('pool', lambda e, t=stg[s]: e.memset(t[:, :, 128:512], 0.0), writes=[sk])
                    if blk == 2:
                        P.op('pool', lambda e, t=stg[s]: e.memset(t[:, :, 48:128], 0.0), writes=[sk])
                    if scale:
                        P.tt('dve' if blk % 2 == 0 else 'pool', stb[s][:], stg[s][:], gin_b, ALU.mult,
                             reads=[sk, 'prm'], writes=[bk_])
                    else:
                        P.cp('act', stb[s][:], stg[s][:], reads=[sk], writes=[bk_])
                else:
                    wsel = w_ck1 if blk < 25 else w_cv1
                    hf = (blk - 23) % 2
                    P.dma(stg[s][:].rearrange("p a b -> p (a b)")[:, 0:2048].rearrange("p (j n) -> p j n", j=8),
                          wsel[hf * 1024:(hf + 1) * 1024, :].rearrange("(j p) n -> p j n", p=128), f'wld{s}', writes=[sk])
                    P.cp('act', stb[s][:].rearrange("p a b -> p (a b)")[:, 0:2048],
                         stg[s][:].rearrange("p a b -> p (a b)")[:, 0:2048], reads=[sk], writes=[bk_])
                P.dma(wsc[blk], stb[s][:].rearrange("p a b -> p (a b)"), f'wst{s}', reads=[bk_], writes=['wsc'])
            for which in range(2):
                for hf in range(2):
                    s = hf
                    P.dma(stb[s][:].rearrange("p a b -> p (a b)"), wsc[23 + which * 2 + hf], f'w2l{s}',
                          reads=['wsc'], writes=[f'stb{s}'])
                    w1v_ = stb[s][:].rearrange("p a b -> p (a b)")[:, 0:2048].rearrange("p (j n) -> p j n", j=8)
                    for hc in range(2):
                        for jj in range(8):
                            j = hf * 8 + jj
                            P.mm(X[:, (which * 2 + hc) * 2 + hf:(which * 2 + hc) * 2 + hf + 1],
                                 w1v_[:, jj, hc * 128:(hc + 1) * 128],
                                 post[:, which * 16 + j:which * 16 + j + 1], jj == 0, jj == 7,
                                 reads=[f'stb{s}', 'post'], xw=['X'])
            P.cp('dve', small[:, 0:8], X[:, 0:8], xr=['X'], writes=['small'])
            P.tt('dve', posb[:], small[:, 0:8:2], small[:, 1:8:2], ALU.add, reads=['small'], writes=['posb'])
            P.ts('dve', posbh[:], posb[:], 0.5, None, ALU.mult, reads=['posb'], writes=['posbh'])
            P.ts('dve', prmh[:], prm[:, :, 2:4], 0.5, None, ALU.mult, reads=['prm'], writes=['prmh'])
            P.op('pool', lambda e: e.memset(kcmpT[:], 0.0), writes=['kcmpT'])
            for i_ in range(4):
                P.op('pool', lambda e, i_=i_: e.memset(kr2[i_][:], 0.0), writes=[f'kr2_{i_}'])
            for g in range(2):
                P.op('pool', lambda e, g=g: e.memset(hidv[g][:], 0.0), writes=[f'hidv{g}'])
            P.op('pool', lambda e: e.memset(vslc[:, :, :, 64:65], 1.0), writes=['vslc'])
            P.op('pool', lambda e: e.memset(vwin[:, :, :, 64:65], 1.0), writes=['vwin'])
            P.op('pool', lambda e: e.memset(vcf[:, :, 0:64], 0.0), writes=['vcf_v'])
            for t in range(4):
                P.op('pool', lambda e, t=t: e.memset(vcn[t][:, :, 0:64], 0.0), writes=[f'vcn{t}_v'])
            P.fence()

        xt2 = [P.sb(f"xt{i}", [128, D], F32) for i in range(1)]
        xs2 = [P.sb(f"xs{i}", [128, D], BF16) for i in range(2)]
        hT2 = [P.sb(f"hT{i}", [128, 8, CH], BF16) for i in range(2)]
        chalf = P.sb("chalf", [128, 1], F32)
        P.op('pool', lambda e: e.memset(chalf[:], -0.5), writes=['chalf'])
        P.op('pool', lambda e: e.memset(qaug[:], 0.0), writes=[f'q{h}' for h in range(16)], cost=8000)
        P.alias['T16'] = [f'T16c{i}' for i in range(8)]
        P.alias['u'] = [f'u{i}' for i in range(8)]
        P.alias['vcf'] = ['vcf_v', 'vcf_o']
        for t_ in range(4):
            P.alias[f'vcn{t_}'] = [f'vcn{t_}_v', f'vcn{t_}_o']
        gchunk = [0]
        wb = [P.sb(f"wb{i}", [128, 8, 512], BF16) for i in range(2)]
        qaug = P.sb("qaug", [128, 16, CH], BF16)
        B0 = P.sb("B0", [128, 8, CH], BF16)
        B1 = P.sb("B1", [128, 8, CH], BF16)
        B2 = P.sb("B2", [128, 8, CH], BF16)
        u = P.sb("u", [128, 8, 30 + CH], BF16)
        T16 = P.sb("T16", [128, 4, D], F32)
        dgA = P.sb("dgA", [128, 16, 128], BF16)
        dgB = P.sb("dgB", [128, 15, 128], BF16)
        Pb = [P.sb(f"Pb{i}", [128, CH], BF16) for i in range(3)]
        Pf2 = [P.sb(f"Pf{i}", [96, CH], BF16) for i in range(2)]
        Pn2 = [P.sb(f"Pn{i}", [40, CH], BF16) for i in range(2)]
        ftmp = [P.sb(f"ftmp{i}", [128, 4, 64], F32) for i in range(2)]
        itmp = [P.sb(f"itmp{i}", [128, 4, 32], F32) for i in range(2)]
        ctmp = P.sb("ctmp", [128, 2, 64], F32)
        gates = P.sb("gates", [128, 4, 48], F32)
        ofb = P.sb("ofb", [128, D], BF16)
        tmpf = P.sb("tmpf", [128, 4, CH], F32)
        impacc = P.sb("impacc", [128, 4, 32], F32)
        prio = P.sb("prio", [128, 4, 32], F32)
        top8 = P.sb("top8", [128, 4, 8], F32)
        selM = P.sb("selM", [128, 4, 32], BF16)

        th = [tmpf[:, 0, :], tmpf[:, 1, :]]
        wcount = [0]

        def load_block(blk):
            s = wcount[0] % 2
            wcount[0] += 1
            P.dma(wb[s][:].rearrange("p a b -> p (a b)"), wsc[blk], f'wb{s}', reads=['wsc'], writes=[f'wb{s}'])
            return wb[s], f'wb{s}'

        def fm_job(wt, wk, ui, rhs, rkeys, evac, bank=None):
            ps, pk = (bank or nextA)()
            for kc in range(8):
                P.mm(ps[:, :], wt[:, kc, ui * 128:(ui + 1) * 128], rhs[:, kc, :], kc == 0, kc == 7,
                     reads=[wk] + rkeys, xw=[pk])
            evac(ps, pk)

        HT = {}

        def emit_pro_A1(seq, tc):
            if True:
                T0 = tc * CH
                hp = gchunk[0] % 2
                gchunk[0] += 1
                hT, hk = hT2[hp], f'hT{hp}'
                for qt in range(4):
                    xi = qt % 2
                    xt, xs, xtk, xsk = xt2[0], xs2[xi], 'xt0', f'xs{xi}'
                    sc = 16 + 4 * xi
                    P.dma(xt[:], x[seq, T0 + qt * 128:T0 + (qt + 1) * 128, :], 'xld0', writes=[xtk])
                    P.act(xs[:], xt[:], AF.Square, accum=small[:, sc:sc + 1], reads=[xtk], writes=[xsk, f'ss{xi}'])
                    P.ts('pool', small[:, sc + 1:sc + 2], small[:, sc:sc + 1], 1.0 / D, EPS, ALU.mult, ALU.add,
                         reads=[f'ss{xi}'], writes=[f'ms{xi}'])
                    P.tt('pool', small[:, sc + 2:sc + 3], small[:, sc + 1:sc + 2], chalf[:], ALU.pow,
                         reads=[f'ms{xi}', 'chalf'], writes=[f'rstd{xi}'])
                    P.ts('dve', xs[:], xt[:], small[:, sc + 2:sc + 3], None, ALU.mult, reads=[xtk, f'rstd{xi}'], writes=[xsk])
                    for c in range(8):
                        P.tr(T[:, c * 128:(c + 1) * 128], xs[:, c * 128:(c + 1) * 128], ident[:],
                             reads=[xsk, 'ident'], xw=['T'])
                    P.cp('act', hT[:, :, qt * 128:(qt + 1) * 128], T[:, :].rearrange("p (c n) -> p c n", c=8),
                         xr=['T'], writes=[hk])
                if tc > 0:
                    for i in range(4):
                        P.cp('pool', kr2[i][:, 0:16], kr2[i][:, 512:528], reads=[f'kr2_{i}'], writes=[f'kr2_{i}'])
                wt, wk = load_block(0)
                for ui in range(4):
                    def ev(ps, pk, ui=ui):
                        P.cp('dve', kr2[ui][0:64, 16:528], ps[0:64, :], xr=[pk], writes=[f'kr2_{ui}'])
                        P.cp('act', kr2[ui][64:128, 15:527], ps[64:128, :], xr=[pk], writes=[f'kr2_{ui}'])
                    fm_job(wt, wk, ui, hT, [hk], ev)
                wt, wk = load_block(1)
                for ui, (dst, dk) in enumerate([(kslc, 'kslc'), (kwin, 'kwin')]):
                    def ev(ps, pk, dst=dst, dk=dk):
                        P.cp('dve', dst[0:64, 0, T0:T0 + CH], ps[0:64, :], xr=[pk], writes=[dk])
                        P.cp('act', dst[0:64, 1, T0:T0 + CH], ps[64:128, :], xr=[pk], writes=[dk])
                    fm_job(wt, wk, ui, hT, [hk], ev)
                for qt in range(4):
                    ps, pk = nextA()
                    for kc in range(8):
                        P.mm(ps[:, 0:256], hT[:, kc, qt * 128:(qt + 1) * 128], wt[:, kc, 256:512], kc == 0, kc == 7,
                             reads=[wk, hk], xw=[pk])
                    kb = 4 * tc + qt
                    P.cp('dve', vslc[:, kb, :, 0:64], ps[:, 0:128].rearrange("p (g d) -> p g d", g=2),
                         xr=[pk], writes=['vslc'])
                    P.cp('act', vwin[:, kb, :, 0:64], ps[:, 128:256].rearrange("p (g d) -> p g d", g=2),
                         xr=[pk], writes=['vwin'])
                wt, wk = load_block(2)
                for qt in range(4):
                    ps, pk = nextA()
                    for kc in range(8):
                        P.mm(ps[:, 0:48], hT[:, kc, qt * 128:(qt + 1) * 128], wt[:, kc, 0:48], kc == 0, kc == 7,
                             reads=[wk, hk], xw=[pk])
                    P.act(gates[:, qt, :], ps[:, 0:48], AF.Tanh, scale=0.5, xr=[pk], writes=['gates'])
                    P.ts('dve', gates[:, qt, :], gates[:, qt, :], 0.5, 0.5, ALU.mult, ALU.add, reads=['gates'], writes=['gates'])

                def ev_q(i):
                    def ev(ps, pk):
                        P.act(qaug[0:64, 2 * i, :], ps[0:64, :], AF.Copy, scale=0.125, xr=[pk], writes=[f'q{2 * i}'])
                        P.ts('dve', qaug[0:64, 2 * i + 1, :], ps[64:128, :], 0.125, None, ALU.mult,
                             xr=[pk], writes=[f'q{2 * i + 1}'])
                    return ev

                for blk in range(2, 5):
                    if blk > 2:
                        wt, wk = load_block(blk)
                    for ui in range(4):
                        gu = blk * 4 + ui
                        if gu < 9 or gu > 16:
                            continue
                        fm_job(wt, wk, ui, hT, [hk], ev_q(gu - 9))
                HT[(seq, tc)] = (hT, hk)

        chunks = [(sq_, tc_) for sq_ in range(NSEQ) for tc_ in range(NCH)]
        for ci, (seq, tc) in enumerate(chunks):
            if True:
                T0 = tc * CH
                if (seq, tc) not in HT:
                    emit_pro_A1(seq, tc)
                hT, hk = HT[(seq, tc)]
                for blk in (5, 6):
                    wt, wk = load_block(blk)
                    for ui in range(4):
                        i = (blk - 5) * 4 + ui

                        def ev(ps, pk, i=i):
                            P.act(th[i % 2], ps[:, :], AF.Tanh, scale=0.5, xr=[pk], writes=[f'sq{i % 2}'])
                            P.stt('dve', B0[:, i, :], th[i % 2], 1.0, ps[:, :], ALU.add, ALU.mult,
                                  xr=[pk], reads=[f'sq{i % 2}'], writes=['B0'])
                        fm_job(wt, wk, ui, hT, [hk], ev)
                c0 = max(0, 32 * tc - 1)
                nn = 32 * tc + 31 - c0
                nfar = max(0, 32 * tc - 8)
                for which in range(2):
                    w1t = []
                    for hf in range(2):
                        wt, wk = load_block(23 + which * 2 + hf)
                        w1t.append((wt[:].rearrange("p a b -> p (a b)")[:, 0:2048].rearrange("p (j n) -> p j n", j=8), wk))
                    for g in range(2):
                        ki = which * 2 + g
                        ps, pk = nextA()
                        for hc in range(2):
                            for j in range(16):
                                st = 16 * (c0 - 32 * tc) + 16 + 2 * j
                                w1v_, wk = w1t[j // 8]
                                P.mm(ps[:, hc * 32:hc * 32 + nn], w1v_[:, j % 8, hc * 128:(hc + 1) * 128],
                                     kr2[ki][:, st:st + 16 * (nn - 1) + 1:16], j == 0, j == 15,
                                     reads=[wk, f'kr2_{ki}'], xw=[pk])
                        for hc in range(2):
                            pc = which * 2 + hc
                            P.act(ctmp[:, hc, 0:nn], ps[:, hc * 32:hc * 32 + nn], AF.Tanh, scale=0.5,
                                  bias=posbh[:, pc:pc + 1], xr=[pk], writes=[f'cth{hc}'])
                            P.ts('dve', ctmp[:, hc, 32:32 + nn], ps[:, hc * 32:hc * 32 + nn], posb[:, pc:pc + 1], None, ALU.add,
                                 xr=[pk], reads=['posb'], writes=[f'ctt{hc}'])
                            hdst, hkey = (hidk[:, hc, 0:nn], 'hidk') if which == 0 else \
                                (hidv[g][:, hc, 8 + c0:8 + c0 + nn], f'hidv{g}')
                            P.stt('dve', hdst, ctmp[:, hc, 0:nn], 1.0, ctmp[:, hc, 32:32 + nn], ALU.add, ALU.mult,
                                  reads=[f'cth{hc}', f'ctt{hc}'], writes=[hkey])
                        if which == 0:
                            for hc in range(2):
                                P.mm(X[0:64, 0:nn], w2k[:, hc, :], hidk[:, hc, 0:nn], hc == 0, hc == 1,
                                     reads=['w2k', 'hidk'], xw=['X'])
                            P.ts('dve', kcmpT[0:64, g, 8 + c0:8 + c0 + nn], X[0:64, 0:nn], 0.5, None, ALU.mult, xr=['X'], writes=['kcmpT'])
                        else:
                            if nfar > 0:
                                for hc in range(2):
                                    P.mm(X[0:nfar, 0:64], hidv[g][:, hc, 8:8 + nfar], w2v[:, hc, :], hc == 0, hc == 1,
                                         reads=['w2v', f'hidv{g}'], xw=['X'])
                                P.ts('dve', vcf[0:nfar, g, 0:64], X[0:nfar, 0:64], 0.5, None, ALU.mult, xr=['X'], writes=['vcf_v'])
                            for hc in range(2):
                                P.mm(X[0:40, 64:128], hidv[g][:, hc, 32 * tc:32 * tc + 40], w2v[:, hc, :], hc == 0, hc == 1,
                                     reads=['w2v', f'hidv{g}'], xw=['X'])
                            P.ts('dve', vcn[tc][:, g, 0:64], X[0:40, 64:128], 0.5, None, ALU.mult, xr=['X'], writes=[f'vcn{tc}_v'])
                orot = [0]

                def nextO():
                    i = orot[0] % 2
                    orot[0] += 1
                    return O[i], f"O{i}"

                fbrot = [0]

                def finish_branch(h, br, Ot, Ok, first, zcol):
                    k = fbrot[0] % 2
                    fbrot[0] += 1
                    base = 64 + 12 * k
                    rzm, rz, fac = small[:, base:base + 4], small[:, base + 4:base + 8], small[:, base + 8:base + 12]
                    kz, kr, kf = f'rz{k}', f'rzr{k}', f'fac{k}'
                    Ov = Ot[:, :].rearrange("p (q n) -> p q n", q=4)
                    P.ts('dve', rzm, Ov[:, :, zcol], 1e-30, None, ALU.max, xr=[Ok], writes=[kz])
                    P.op('dve', lambda e: e.reciprocal(rz, rzm), reads=[kz], writes=[kr], cost=80)
                    P.tt('dve', fac, rz, gates[:, :, br * 16 + h], ALU.mult, reads=[kr, 'gates'], writes=[kf])
                    hi = 1 if h >= 8 else 0
                    tks = [f'T16c{2 * qt + hi}' for qt in range(4)]
                    dst = T16[:, :, h * 64:(h + 1) * 64]
                    facb = fac.unsqueeze(2).broadcast_to([128, 4, 64])
                    if first:
                        P.tt('dve', dst, Ov[:, :, 0:64], facb, ALU.mult, xr=[Ok], reads=[kf], writes=tks)
                    else:
                        P.tt('dve', ftmp[k][:], Ov[:, :, 0:64], facb, ALU.mult, xr=[Ok], reads=[kf], writes=[f'ftmp{k}'])
                        P.tt('pool', dst, dst, ftmp[k][:], ALU.add, reads=[f'ftmp{k}'] + tks, writes=tks)
                    return k, rz, kr

                for g in range(2):
                    for hh in range(8):
                        h = g * 8 + hh
                        Pf, Pn, pfk, pnk = Pf2[h % 2], Pn2[h % 2], f'Pf{h % 2}', f'Pn{h % 2}'
                        if nfar > 0:
                            ps, pk = nextS()
                            P.mm(ps[0:nfar, :], kcmpT[:, g, 8:8 + nfar], qaug[:, h, :], True, True,
                                 reads=['kcmpT', f'q{h}'], xw=[pk])
                            P.act(Pf[0:nfar, :], ps[0:nfar, :], AF.Exp, bias=b31[0:nfar, h:h + 1], xr=[pk], writes=[pfk])
                        ps, pk = nextS()
                        P.mm(ps[0:40, :], kcmpT[:, g, 32 * tc:32 * tc + 40], qaug[:, h, :], True, True,
                             reads=['kcmpT', f'q{h}'], xw=[pk])
                        P.act(Pn[:, :], ps[0:40, :], AF.Exp, bias=b31[0:40, h:h + 1], xr=[pk], writes=[pnk])
                        P.tt('pool', Pn[:, :], Pn[:, :], Er[:, h, :], ALU.mult, reads=[pnk, f'Er{h}'], writes=[pnk])
                        Ot, Ok = A[3], 'A3'
                        Ov = Ot[:, :].rearrange("p (q n) -> p q n", q=4)
                        for qt in range(4):
                            if nfar > 0:
                                P.mm(Ov[:, qt, 0:97], Pf[0:nfar, qt * 128:(qt + 1) * 128], vcf[0:nfar, g, :], True, False,
                                     reads=[pfk, 'vcf'], xw=[Ok])
                            P.mm(Ov[:, qt, 0:97], Pn[:, qt * 128:(qt + 1) * 128], vcn[tc][:, g, :], nfar == 0, True,
                                 reads=[pnk, f'vcn{tc}'], xw=[Ok])
                        k, rz, kr = finish_branch(h, 0, Ot, Ok, True, 96)
                        rzb = rz.unsqueeze(2).broadcast_to([128, 4, 32])
                        if hh == 0:
                            P.tt('dve', impacc[:], Ov[:, :, 64:96], rzb, ALU.mult, xr=[Ok], reads=[kr], writes=['impacc'])
                        else:
                            P.tt('dve', itmp[k][:], Ov[:, :, 64:96], rzb, ALU.mult, xr=[Ok], reads=[kr], writes=[f'itmp{k}'])
                            P.tt('pool', impacc[:], impacc[:], itmp[k][:], ALU.add, reads=[f'itmp{k}', 'impacc'],
                                 writes=['impacc'])
                    P.tt('dve', prio[:], impacc[:], selA[:, 4 * tc:4 * tc + 4, :], ALU.mult,
                         reads=['impacc', 'selA'], writes=['prio'])
                    P.tt('dve', prio[:], prio[:], selB[:, 4 * tc:4 * tc + 4, :], ALU.add,
                         reads=['prio', 'selB'], writes=['prio'])
                    for qt in range(4):
                        P.op('dve', lambda e, qt=qt: e.max(top8[:, qt, :], prio[:, qt, :]), reads=['prio'], writes=['top8'])
                    P.ts('dve', small[:, 40:44], top8[:, :, 7], -5e5, None, ALU.max, reads=['top8'], writes=['thr'])
                    for qt in range(4):
                        P.ts('dve', selM[:, qt, :], prio[:, qt, :], small[:, 40 + qt:41 + qt], NEG, ALU.is_lt, ALU.mult,
                             reads=['prio', 'thr'], writes=['selM'])
                        P.tr(T[0:32, qt * 128:(qt + 1) * 128], selM[:, qt, :], ident[:], reads=['selM', 'ident'], xw=['T'])
                    for hh in range(8):
                        h = g * 8 + hh
                        P.cp('act' if hh % 2 == 0 else 'dve', qaug[64:96, h, :], T[0:32, 0:512], xr=['T'], writes=[f'qm{h}'])
                tiles = []
                for br, h in [(2, hh_) for hh_ in range(16)] + [(1, hh_) for hh_ in range(16)]:
                    g = h // 8
                    if True:
                        kb_lo = 0 if br == 1 else max(0, 4 * tc - 4)
                        kbs = list(range(kb_lo, 4 * tc + 4))
                        for i, kb in enumerate(kbs):
                            qlo = max(0, kb - 4 * tc)
                            qhi = 3 if br == 1 else min(3, kb + 4 - 4 * tc)
                            tiles.append(dict(h=h, g=g, br=br, kb=kb, qlo=qlo, qhi=qhi, first=(i == 0),
                                              last=(i == len(kbs) - 1)))
                prot = [0]
                cur = {}

                def stage_qk(t):
                    ps, pk = nextS()
                    pi = prot[0] % 3
                    prot[0] += 1
                    t['ps'], t['pk'], t['pb'], t['pbk'] = ps, pk, Pb[pi], f'Pb{pi}'
                    h, g, kb = t['h'], t['g'], t['kb']
                    c0_, c1_ = t['qlo'] * 128, (t['qhi'] + 1) * 128
                    if t['br'] == 1:
                        P.mm(ps[:, c0_:c1_], kslc[:, g, kb * 128:(kb + 1) * 128], qaug[:, h, c0_:c1_], True, True,
                             reads=['kslc', f'q{h}', f'qm{h}'], xw=[pk])
                    else:
                        P.mm(ps[:, c0_:c1_], kwin[:, g, kb * 128:(kb + 1) * 128], qaug[:, h, c0_:c1_], True, True,
                             reads=['kwin', f'q{h}'], xw=[pk])
                    P.act(t['pb'][:, c0_:c1_], ps[:, c0_:c1_], AF.Exp, bias=b31[:, h:h + 1], xr=[pk], writes=[t['pbk']])
                    d0 = kb - 4 * tc
                    dl = [d for d in (0, 1) if 0 <= d0 + d <= 3 and t['qlo'] <= d0 + d <= t['qhi']]
                    if dl:
                        a = (d0 + dl[0]) * 128
                        b = (d0 + dl[-1] + 1) * 128
                        P.tt('dve', t['pb'][:, a:b], t['pb'][:, a:b], Ed[:, h, dl[0] * 128:(dl[-1] + 1) * 128], ALU.mult,
                             rate=1.92, reads=[t['pbk'], f'Ed{h}'], writes=[t['pbk']])
                    if t['br'] == 2 and 0 <= d0 + 4 <= 3:
                        a = (d0 + 4) * 128
                        P.tt('pool', t['pb'][:, a:a + 128], t['pb'][:, a:a + 128], m4[:], ALU.mult,
                             reads=[t['pbk'], 'm4'], writes=[t['pbk']])

                def stage_pv(t):
                    key = (t['h'], t['br'])
                    if t['first']:
                        cur[key] = nextO()
                    Ot, Ok = cur[key]
                    Ov = Ot[:, :].rearrange("p (q n) -> p q n", q=4)
                    vt, vk = (vslc, 'vslc') if t['br'] == 1 else (vwin, 'vwin')
                    for qt in range(t['qlo'], t['qhi'] + 1):
                        st = t['first'] and qt == t['qlo']
                        P.mm(Ov[:, qt, 0:65], t['pb'][:, qt * 128:(qt + 1) * 128], vt[:, t['kb'], t['g'], :], st, False,
                             reads=[t['pbk'], vk], xw=[Ok], skip=True)
                    if t['last']:
                        finish_branch(t['h'], t['br'], Ot, Ok, False, 64)

                for i in range(len(tiles) + 1):
                    if i < len(tiles):
                        stage_qk(tiles[i])
                    if i >= 1:
                        stage_pv(tiles[i - 1])
                if tc == 0:
                    P.op('pool', lambda e: e.memset(u[:, :, 0:30], 0.0), writes=['u'])
                else:
                    P.cp('pool', u[:, :, 0:30], u[:, :, CH:CH + 30], reads=['u'], writes=['u'])
                for blk in range(9, 13):
                    wt, wk = load_block(blk)
                    for ui in range(4):
                        gu = blk * 4 + ui - 36
                        i = gu // 2
                        if gu % 2 == 0:
                            def ev(ps, pk, i=i):
                                P.act(th[i % 2], ps[:, :], AF.Tanh, scale=0.5, xr=[pk], writes=[f'sq{i % 2}'])
                        else:
                            def ev(ps, pk, i=i):
                                P.stt('dve', u[:, i, 30:30 + CH], th[i % 2], 1.0, ps[:, :], ALU.add, ALU.mult,
                                      xr=[pk], reads=[f'sq{i % 2}'], writes=[f'u{i}'])
                        fm_job(wt, wk, ui, hT, [hk], ev, bank=nextBG)
                for blk in (13, 14):
                    wt, wk = load_block(blk)
                    for ui in range(4):
                        i = (blk - 13) * 4 + ui

                        def ev(ps, pk, i=i):
                            P.act(th[i % 2], ps[:, :], AF.Tanh, scale=0.5, xr=[pk], writes=[f'sq{i % 2}'])
                            P.stt('dve', B2[:, i, :], th[i % 2], 1.0, ps[:, :], ALU.add, ALU.mult,
                                  xr=[pk], reads=[f'sq{i % 2}'], writes=['B2'])
                        fm_job(wt, wk, ui, hT, [hk], ev, bank=nextBG)
                for qt in range(4):
                    P.cp('act', ofb[:], T16[:, qt, :], reads=[f'T16c{2 * qt}', f'T16c{2 * qt + 1}'], writes=['ofb'])
                    for c in range(8):
                        P.tr(T[:, c * 128:(c + 1) * 128], ofb[:, c * 128:(c + 1) * 128], ident[:],
                             reads=['ofb', 'ident'], xw=['T'])
                    P.stt('dve', B1[:, :, qt * 128:(qt + 1) * 128], T[:, :].rearrange("p (c n) -> p c n", c=8), 0.5,
                          B0[:, :, qt * 128:(qt + 1) * 128], ALU.mult, ALU.mult, xr=['T'], reads=['B0'], writes=['B1'])
                cfp = T16[:, :, :].rearrange("p a b -> p (a b)").rearrange("p (c n) -> p c n", c=8)
                for i in range(8):
                    P.tt('pool', dgA[:], ident[:].unsqueeze(1).broadcast_to([128, 16, 128]),
                         prm[:, i, 4:20].unsqueeze(2).broadcast_to([128, 16, 128]), ALU.mult,
                         reads=['ident', 'prm'], writes=['dgA'])
                    P.tt('pool', dgB[:], ident[:].unsqueeze(1).broadcast_to([128, 15, 128]),
                         prm[:, i, 20:35].unsqueeze(2).broadcast_to([128, 15, 128]), ALU.mult,
                         reads=['ident', 'prm'], writes=['dgB'])
                    ps, pk = nextA()
                    for j in range(31):
                        dgt, dk_ = (dgA[:, j, :], 'dgA') if j < 16 else (dgB[:, j - 16, :], 'dgB')
                        P.mm(ps[:, :], dgt, u[:, i, j:j + CH], j == 0, j == 30, reads=[dk_, f'u{i}'], xw=[pk])
                    P.act(cfp[:, i, :], ps[:, :], AF.Identity, bias=prm[:, i, 1:2], scale=0.5, xr=[pk], writes=[f'T16c{i}'])
                    P.act(tmpf[:, i % 2, :], cfp[:, i, :], AF.Square, reads=[f'T16c{i}'], writes=[f'sq{i % 2}'])
                    P.mm(O[0][:, :], onesf[:], cfp[:, i, :], i == 0, i == 7, reads=['onesf', f'T16c{i}'], xw=['O0'])
                    P.mm(O[1][:, :], onesf[:], tmpf[:, i % 2, :], i == 0, i == 7, reads=['onesf', f'sq{i % 2}'], xw=['O1'])
                for blk in (7, 8):
                    wt, wk = load_block(blk)
                    for ui in range(4):
                        f = (blk - 7) * 4 + ui

                        def ev(ps, pk, f=f):
                            P.act(B0[:, f, :], ps[:, :], AF.Copy, scale=4.0, xr=[pk], writes=['B0'])
                        fm_job(wt, wk, ui, B1, ['B1'], ev)
                mean, msq, rstd = tmpf[:, 2, :], tmpf[:, 0, :], tmpf[:, 3, :]
                P.ts('dve', mean, O[0][:, :], 1.0 / D, None, ALU.mult, xr=['O0'], writes=['mean'])
                P.tt('dve', msq, mean, mean, ALU.mult, reads=['mean'], writes=['sq0'])
                P.stt('dve', rstd, O[1][:, :], 1.0 / D, msq, ALU.mult, ALU.subtract, xr=['O1'], reads=['sq0'], writes=['rstdc'])
                P.ts('dve', rstd, rstd, EPS, None, ALU.add, reads=['rstdc'], writes=['rstdc'])
                P.act(rstd, rstd, AF.Sqrt, reads=['rstdc'], writes=['rstdc'])
                P.op('dve', lambda e: e.reciprocal(rstd, rstd), reads=['rstdc'], writes=['rstdc'], cost=75 + CH / 0.96)
                for i in range(8):
                    P.tt('dve', cfp[:, i, :], cfp[:, i, :], mean, ALU.subtract, reads=[f'T16c{i}', 'mean'], writes=[f'T16c{i}'])
                    P.tt('pool', cfp[:, i, :], cfp[:, i, :], rstd, ALU.mult, reads=[f'T16c{i}', 'rstdc'], writes=[f'T16c{i}'])
                    P.act(th[i % 2], cfp[:, i, :], AF.Tanh, bias=prmh[:, i, 1:2], scale=prmh[:, i, 0:1],
                          reads=[f'T16c{i}', 'prmh'], writes=[f'sq{i % 2}'])
                    P.ts('dve', cfp[:, i, :], cfp[:, i, :], prm[:, i, 2:3], prm[:, i, 3:4], ALU.mult, ALU.add,
                         reads=[f'T16c{i}', 'prm'], writes=[f'T16c{i}'])
                    P.stt('dve', th[i % 2], th[i % 2], 1.0, cfp[:, i, :], ALU.add, ALU.mult,
                          reads=[f'sq{i % 2}', f'T16c{i}'], writes=[f'sq{i % 2}'])
                    P.tt('dve', B1[:, i, :], th[i % 2], B2[:, i, :], ALU.mult,
                         reads=[f'sq{i % 2}', 'B2'], writes=['B1'])
                for qt in range(4):
                    P.dma(T16[:, qt, :], x[seq, T0 + qt * 128:T0 + (qt + 1) * 128, :], f'xres{qt}', writes=[f'T16c{2 * qt}', f'T16c{2 * qt + 1}'])
                if ci + 1 < len(chunks):
                    emit_pro_A1(*chunks[ci + 1])
                for blk in (15, 16):
                    wt, wk = load_block(blk)
                    for ui in range(4):
                        f = (blk - 15) * 4 + ui

                        def ev(ps, pk, f=f):
                            P.cp('act', B2[:, f, :], ps[:, :], xr=[pk], writes=['B2'])
                        fm_job(wt, wk, ui, B1, ['B1'], ev)
                for blk in range(17, 21):
                    wt, wk = load_block(blk)
                    for ui in range(4):
                        gu = blk * 4 + ui - 68
                        i = gu % 8
                        if gu < 8:
                            def ev(ps, pk, i=i):
                                P.act(th[i % 2], ps[:, :], AF.Tanh, scale=0.5, xr=[pk], writes=[f'sq{i % 2}'])
                                P.stt('dve', B1[:, i, :], th[i % 2], 1.0, B2[:, i, :], ALU.add, ALU.mult,
                                      reads=[f'sq{i % 2}', 'B2'], writes=['B1'])
                        else:
                            def ev(ps, pk, i=i):
                                P.act(th[i % 2], ps[:, :], AF.Tanh, scale=0.5, xr=[pk], writes=[f'sq{i % 2}'])
                                P.stt('dve', th[i % 2], th[i % 2], 1.0, B0[:, i, :], ALU.add, ALU.mult,
                                      reads=[f'sq{i % 2}', 'B0'], writes=[f'sq{i % 2}'])
                                P.tt('pool', B1[:, i, :], B1[:, i, :], th[i % 2], ALU.add,
                                     reads=[f'sq{i % 2}', 'B1'], writes=['B1'])
                        fm_job(wt, wk, ui, hT, [hk], ev)
                for blk in (21, 22):
                    wt, wk = load_block(blk)
                    half = blk - 21
                    for qt in range(4):
                        ps, pk = nextA()
                        for kc in range(8):
                            P.mm(ps[:, :], B1[:, kc, qt * 128:(qt + 1) * 128], wt[:, kc, :], kc == 0, kc == 7,
                                 reads=[wk, 'B1'], xw=[pk])
                        dst = T16[:, qt, half * 512:(half + 1) * 512]
                        P.stt('dve', dst, ps[:, :], 0.125, dst, ALU.mult, ALU.add, xr=[pk], reads=[f'T16c{2 * qt + half}'], writes=[f'T16c{2 * qt + half}'])
                for qt in range(4):
                    tk = [f'T16c{2 * qt}', f'T16c{2 * qt + 1}']
                    P.act(ofb[:], T16[:, qt, :], AF.Square, accum=small[:, 48 + qt:49 + qt], reads=tk,
                          writes=['ofb', f'fss{qt}'])
                    P.ts('pool', small[:, 52 + qt:53 + qt], small[:, 48 + qt:49 + qt], 1.0 / D, EPS, ALU.mult, ALU.add,
                         reads=[f'fss{qt}'], writes=[f'fms{qt}'])
                    P.tt('pool', small[:, 60 + qt:61 + qt], small[:, 52 + qt:53 + qt], chalf[:], ALU.pow,
                         reads=[f'fms{qt}', 'chalf'], writes=[f'frs{qt}'])
                    P.stt('dve', T16[:, qt, :], T16[:, qt, :], small[:, 60 + qt:61 + qt], normf[:], ALU.mult, ALU.mult,
                          reads=tk + [f'frs{qt}', 'normf'], writes=tk)
                    P.dma(y[seq, T0 + qt * 128:T0 + (qt + 1) * 128, :], T16[:, qt, :], f'yst{qt}', reads=tk)
        P.emit(final_streams=[f'yst{q_}' for q_ in range(4)])
    return nc


_CACHE = {}


def kernel(x, norm_in_g, w_in, pos_ck, w_ck1, w_ck2, pos_cv, w_cv1, w_cv2, rel_bias,
           conv_w, conv_b, conv_ln_g, conv_ln_b, w_conv_proj, w_nsa_proj, w_out, norm_f_g):
    f = lambda a: np.ascontiguousarray(np.asarray(a, dtype=np.float32))
    x = f(x)
    if 'nc' not in _CACHE:
        _CACHE['nc'] = build_nc()
        _CACHE['consts'] = _host_consts()
    nc = _CACHE['nc']
    prm = np.concatenate([f(norm_in_g).reshape(1, D), f(conv_b).reshape(1, D), f(conv_ln_g).reshape(1, D),
                          f(conv_ln_b).reshape(1, D), f(conv_w).reshape(31, D)], axis=0)
    shared = {
        "w_in": f(w_in).reshape(D, 7984), "w_nsa": f(w_nsa_proj).reshape(D, D), "w_conv": f(w_conv_proj).reshape(D, D),
        "w_out": f(w_out).reshape(D, D), "w_ck1": f(w_ck1).reshape(2048, 256), "w_cv1": f(w_cv1).reshape(2048, 256),
        "w_ck2": f(w_ck2).reshape(256, 64), "w_cv2": f(w_cv2).reshape(256, 64),
        "pos_ck": f(pos_ck).reshape(16, 128), "pos_cv": f(pos_cv).reshape(16, 128),
        "rel_bias": f(rel_bias).reshape(1, 512), "rel_bias2": f(rel_bias).reshape(32, 16), "prm": np.ascontiguousarray(prm),
        "norm_f": f(norm_f_g).reshape(1, D),
    }
    shared.update(_CACHE['consts'])
    in_maps = []
    for c in range(NCORES):
        m = dict(shared)
        m["x"] = np.ascontiguousarray(x[c * NSEQ:(c + 1) * NSEQ])
        in_maps.append(m)
    res = run_bass_kernel_spmd(nc, in_maps, core_ids=list(range(NCORES)))
    return np.concatenate([r["y"] for r in res.results], axis=0).astype(np.float32)
```
